# Optimizing a Trainium2 kernel written in Bass

```python
import math
import jax, jax.numpy as jnp
from jax import lax
import numpy as np

D_MODEL = 1024
BATCH = 2
SEQ = 8192
DEPTH = 4

N_MIXERS = 3
D_FF = 2816
LN_EPS = 1e-5
DEEPNORM_ALPHA = (2 * DEPTH) ** 0.25
DEEPNORM_BETA = (8 * DEPTH) ** -0.25
MACARON_WEIGHT = 0.5

SSM_EXPAND = 2
SSM_D_INNER = SSM_EXPAND * D_MODEL
SSM_HEAD_DIM = 64
SSM_HEADS = SSM_D_INNER // SSM_HEAD_DIM
SSM_GROUPS = 4
SSM_HEADS_PER_GROUP = SSM_HEADS // SSM_GROUPS
SSM_D_STATE = 128
SSM_CONV = 4
SSM_CHUNK = 256
SSM_CONV_DIM = SSM_D_INNER + 2 * SSM_GROUPS * SSM_D_STATE
SSM_IN_DIM = 2 * SSM_D_INNER + 2 * SSM_GROUPS * SSM_D_STATE + SSM_HEADS

ATT_HEAD_DIM = 64
FOX_HEADS = D_MODEL // ATT_HEAD_DIM
FOX_Q_BLOCK = 128
MOBA_HEADS = D_MODEL // ATT_HEAD_DIM
MOBA_BLOCK = 256
MOBA_TOPK = 3
MOBA_Q_BLOCK = 32

N_SSM_LAYERS = (DEPTH + 2) // 3
N_FOX_LAYERS = (DEPTH + 1) // 3
N_MOBA_LAYERS = DEPTH // 3

kernel_name = 'hybrid_ssd_fox_moba_macaron_deepnorm'


def layer_norm(x, g, b):
    xf = x.astype(jnp.float32)
    mu = jnp.mean(xf, -1, keepdims=True)
    var = jnp.mean(jnp.square(xf - mu), -1, keepdims=True)
    return ((xf - mu) * lax.rsqrt(var + LN_EPS) * g + b).astype(x.dtype)


def swiglu(x, w_gate, w_up, w_down):
    return (jax.nn.silu(x @ w_gate) * (x @ w_up)) @ w_down


def segsum(a):
    T = a.shape[-1]
    cs = jnp.cumsum(a, axis=-1)
    diff = cs[..., :, None] - cs[..., None, :]
    return jnp.where(jnp.tril(jnp.ones((T, T), dtype=bool)), diff, -jnp.inf)


def ssd_chunked(xdt, adt, b, c):
    Bsz, S, G, R, P = xdt.shape
    N = b.shape[-1]
    Q = SSM_CHUNK
    nc = -(-S // Q)
    pad = nc * Q - S
    padseq = lambda t: jnp.pad(t, [(0, 0), (0, pad)] + [(0, 0)] * (t.ndim - 2))
    xdt, adt, b, c = padseq(xdt), padseq(adt), padseq(b), padseq(c)
    xdt = xdt.reshape(Bsz, nc, Q, G, R, P)
    b = b.reshape(Bsz, nc, Q, G, N)
    c = c.reshape(Bsz, nc, Q, G, N)
    a_t = adt.reshape(Bsz, nc, Q, G, R).transpose(0, 1, 3, 4, 2)
    a_cs = jnp.cumsum(a_t, axis=-1)
    decay_in = jnp.exp(segsum(a_t))
    cb = jnp.einsum('bclgn,bcsgn->bcgls', c, b)
    m = cb[:, :, :, None] * decay_in
    y_diag = jnp.einsum('bcgrls,bcsgrp->bclgrp', m, xdt)
    decay_to_end = jnp.exp(a_cs[..., -1:] - a_cs).transpose(0, 1, 4, 2, 3)
    states = jnp.einsum('bcsgn,bcsgrp->bcgrpn', b, xdt * decay_to_end[..., None])
    states = jnp.concatenate([jnp.zeros_like(states[:, :1]), states], axis=1)
    a_last = jnp.pad(a_cs[..., -1].transpose(0, 2, 3, 1), [(0, 0), (0, 0), (0, 0), (1, 0)])
    chunk_decay = jnp.exp(segsum(a_last))
    states = jnp.einsum('bgrzc,bcgrpn->bzgrpn', chunk_decay, states)[:, :-1]
    decay_out = jnp.exp(a_cs).transpose(0, 1, 4, 2, 3)
    y_off = jnp.einsum('bclgn,bcgrpn->bclgrp', c, states) * decay_out[..., None]
    return (y_diag + y_off).reshape(Bsz, nc * Q, G, R, P)[:, :S]


def mamba2_mixer(x, w_in, conv_w, conv_b, dt_bias, a_log, d_skip, norm_w, w_out):
    Bsz, S, _ = x.shape
    G, R, P, N = SSM_GROUPS, SSM_HEADS_PER_GROUP, SSM_HEAD_DIM, SSM_D_STATE
    f32 = jnp.float32
    zxbcdt = x @ w_in
    z, xbc, dt = jnp.split(zxbcdt, [SSM_D_INNER, SSM_D_INNER + SSM_CONV_DIM], axis=-1)
    xbc = lax.conv_general_dilated(
        xbc, conv_w[:, None, :], window_strides=(1,), padding=[(SSM_CONV - 1, 0)],
        dimension_numbers=('NWC', 'WIO', 'NWC'), feature_group_count=SSM_CONV_DIM) + conv_b
    xbc = jax.nn.silu(xbc)
    xs, b_ssm, c_ssm = jnp.split(xbc, [SSM_D_INNER, SSM_D_INNER + G * N], axis=-1)
    dt = jax.nn.softplus(dt.astype(f32) + dt_bias.astype(f32))
    a = -jnp.exp(a_log.astype(f32))
    xh = xs.astype(f32).reshape(Bsz, S, G, R, P)
    bg = b_ssm.astype(f32).reshape(Bsz, S, G, N)
    cg = c_ssm.astype(f32).reshape(Bsz, S, G, N)
    y = ssd_chunked(xh * dt.reshape(Bsz, S, G, R)[..., None], (dt * a).reshape(Bsz, S, G, R), bg, cg)
    y = y + xh * d_skip.astype(f32).reshape(G, R)[:, :, None]
    y = y.reshape(Bsz, S, SSM_D_INNER) * jax.nn.silu(z.astype(f32))
    yg = y.reshape(Bsz, S, G, SSM_D_INNER // G)
    yg = yg * lax.rsqrt(jnp.mean(jnp.square(yg), -1, keepdims=True) + LN_EPS)
    y = yg.reshape(Bsz, S, SSM_D_INNER) * norm_w
    return y.astype(x.dtype) @ w_out


def fox_attention(x, w_in, b_f, w_out):
    Bsz, S, _ = x.shape
    H, Dh, QB = FOX_HEADS, ATT_HEAD_DIM, FOX_Q_BLOCK
    f32 = jnp.float32
    proj = x @ w_in
    q, k, v, f_logit = jnp.split(proj, [H * Dh, 2 * H * Dh, 3 * H * Dh], axis=-1)
    to_heads = lambda t: t.reshape(Bsz, S, H, Dh).transpose(0, 2, 1, 3).astype(f32)
    q, k, v = to_heads(q), to_heads(k), to_heads(v)
    log_f = jax.nn.log_sigmoid(f_logit.astype(f32) + b_f.astype(f32))
    cum = jnp.cumsum(log_f, axis=1).transpose(0, 2, 1)
    scale = Dh ** -0.5
    kpos = jnp.arange(S)

    def block(i):
        start = i * QB
        qb = lax.dynamic_slice_in_dim(q, start, QB, axis=2)
        cq = lax.dynamic_slice_in_dim(cum, start, QB, axis=2)
        logits = jnp.einsum('bhqd,bhkd->bhqk', qb, k) * scale + (cq[..., :, None] - cum[..., None, :])
        qpos = start + jnp.arange(QB)
        logits = jnp.where(kpos[None, :] <= qpos[:, None], logits, -jnp.inf)
        p = jax.nn.softmax(logits, axis=-1)
        return jnp.einsum('bhqk,bhkd->bhqd', p, v)

    o = lax.map(block, jnp.arange(S // QB))
    o = o.transpose(1, 0, 3, 2, 4).reshape(Bsz, S, H * Dh)
    return o.astype(x.dtype) @ w_out


def moba_attention(x, w_in, w_out):
    Bsz, S, _ = x.shape
    H, Dh, L, QB = MOBA_HEADS, ATT_HEAD_DIM, MOBA_BLOCK, MOBA_Q_BLOCK
    f32 = jnp.float32
    proj = x @ w_in
    q, k, v = jnp.split(proj, [H * Dh, 2 * H * Dh], axis=-1)
    to_heads = lambda t: t.reshape(Bsz, S, H, Dh).transpose(0, 2, 1, 3).astype(f32)
    q, k, v = to_heads(q), to_heads(k), to_heads(v)
    nb = -(-S // L)
    pad = nb * L - S
    k_blk = jnp.pad(k, [(0, 0), (0, 0), (0, pad), (0, 0)]).reshape(Bsz, H, nb, L, Dh)
    v_blk = jnp.pad(v, [(0, 0), (0, 0), (0, pad), (0, 0)]).reshape(Bsz, H, nb, L, Dh)
    k_mean = jnp.mean(k_blk, axis=3)
    k_sel_n = min(MOBA_TOPK, nb)
    scale = Dh ** -0.5
    bi = jnp.arange(Bsz)[:, None, None, None]
    hi = jnp.arange(H)[None, :, None, None]
    blk_ids = jnp.arange(nb)

    def chunk(i):
        start = i * QB
        own = start // L
        qc = lax.dynamic_slice_in_dim(q, start, QB, axis=2)
        gate = jnp.einsum('bhqd,bhnd->bhqn', qc, k_mean)
        gate = jnp.where(blk_ids < own, gate, -jnp.inf)
        _, idx = lax.top_k(gate, k_sel_n)
        valid = jnp.arange(k_sel_n) < own
        k_sel = k_blk[bi, hi, idx]
        v_sel = v_blk[bi, hi, idx]
        s_past = jnp.einsum('bhqd,bhqnld->bhqnl', qc, k_sel) * scale
        s_past = jnp.where(valid[:, None], s_past, -jnp.inf).reshape(Bsz, H, QB, k_sel_n * L)
        k_own = lax.dynamic_index_in_dim(k_blk, own, axis=2, keepdims=False)
        v_own = lax.dynamic_index_in_dim(v_blk, own, axis=2, keepdims=False)
        s_own = jnp.einsum('bhqd,bhld->bhql', qc, k_own) * scale
        qpos = start + jnp.arange(QB)
        kpos = own * L + jnp.arange(L)
        s_own = jnp.where(kpos[None, :] <= qpos[:, None], s_own, -jnp.inf)
        p = jax.nn.softmax(jnp.concatenate([s_past, s_own], axis=-1), axis=-1)
        o = jnp.einsum('bhqm,bhqmd->bhqd', p[..., :k_sel_n * L], v_sel.reshape(Bsz, H, QB, k_sel_n * L, Dh))
        return o + jnp.einsum('bhql,bhld->bhqd', p[..., k_sel_n * L:], v_own)

    o = lax.map(chunk, jnp.arange(S // QB))
    o = o.transpose(1, 0, 3, 2, 4).reshape(Bsz, S, H * Dh)
    return o.astype(x.dtype) @ w_out


def setup_inputs(seed: int = 0) -> dict:
    key = jax.random.key(seed)
    ks = jax.random.split(key, 20)
    f32 = jnp.float32
    nrm = lambda k, shape, s: jax.random.normal(k, shape, f32) * s
    x = nrm(ks[0], (BATCH, SEQ, D_MODEL), 1.0)
    ffn_w_gate = nrm(ks[1], (DEPTH, 2, D_MODEL, D_FF), D_MODEL ** -0.5)
    ffn_w_up = nrm(ks[2], (DEPTH, 2, D_MODEL, D_FF), D_MODEL ** -0.5)
    ffn_w_down = nrm(ks[3], (DEPTH, 2, D_FF, D_MODEL), D_FF ** -0.5 * DEEPNORM_BETA)
    ln_g = 1.0 + nrm(ks[4], (DEPTH, 3, D_MODEL), 0.02)
    ln_b = nrm(ks[5], (DEPTH, 3, D_MODEL), 0.02)
    ssm_w_in = nrm(ks[6], (N_SSM_LAYERS, D_MODEL, SSM_IN_DIM), D_MODEL ** -0.5)
    ssm_conv_w = nrm(ks[7], (N_SSM_LAYERS, SSM_CONV, SSM_CONV_DIM), SSM_CONV ** -0.5)
    ssm_conv_b = nrm(ks[8], (N_SSM_LAYERS, SSM_CONV_DIM), 0.02)
    dt0 = jnp.exp(jax.random.uniform(ks[9], (N_SSM_LAYERS, SSM_HEADS), f32, math.log(1e-3), math.log(1e-1)))
    ssm_dt_bias = dt0 + jnp.log(-jnp.expm1(-dt0))
    ssm_a_log = jnp.log(jax.random.uniform(ks[10], (N_SSM_LAYERS, SSM_HEADS), f32, 1.0, 16.0))
    ssm_d = 1.0 + nrm(ks[11], (N_SSM_LAYERS, SSM_HEADS), 0.1)
    ssm_norm_w = 1.0 + nrm(ks[12], (N_SSM_LAYERS, SSM_D_INNER), 0.02)
    ssm_w_out = nrm(ks[13], (N_SSM_LAYERS, SSM_D_INNER, D_MODEL), SSM_D_INNER ** -0.5 * DEEPNORM_BETA)
    fox_w_in = nrm(ks[14], (N_FOX_LAYERS, D_MODEL, 3 * FOX_HEADS * ATT_HEAD_DIM + FOX_HEADS), D_MODEL ** -0.5)
    fox_b_f = jax.random.uniform(ks[15], (N_FOX_LAYERS, FOX_HEADS), f32, 1.0, 6.0)
    fox_w_out = nrm(ks[16], (N_FOX_LAYERS, FOX_HEADS * ATT_HEAD_DIM, D_MODEL), (FOX_HEADS * ATT_HEAD_DIM) ** -0.5 * DEEPNORM_BETA)
    moba_w_in = nrm(ks[17], (N_MOBA_LAYERS, D_MODEL, 3 * MOBA_HEADS * ATT_HEAD_DIM), D_MODEL ** -0.5)
    moba_w_out = nrm(ks[18], (N_MOBA_LAYERS, MOBA_HEADS * ATT_HEAD_DIM, D_MODEL), (MOBA_HEADS * ATT_HEAD_DIM) ** -0.5 * DEEPNORM_BETA)
    return {'x': x, 'ffn_w_gate': ffn_w_gate, 'ffn_w_up': ffn_w_up, 'ffn_w_down': ffn_w_down,
            'ln_g': ln_g, 'ln_b': ln_b,
            'ssm_w_in': ssm_w_in, 'ssm_conv_w': ssm_conv_w, 'ssm_conv_b': ssm_conv_b,
            'ssm_dt_bias': ssm_dt_bias, 'ssm_a_log': ssm_a_log, 'ssm_d': ssm_d,
            'ssm_norm_w': ssm_norm_w, 'ssm_w_out': ssm_w_out,
            'fox_w_in': fox_w_in, 'fox_b_f': fox_b_f, 'fox_w_out': fox_w_out,
            'moba_w_in': moba_w_in, 'moba_w_out': moba_w_out}


def reference(x, ffn_w_gate, ffn_w_up, ffn_w_down, ln_g, ln_b,
              ssm_w_in, ssm_conv_w, ssm_conv_b, ssm_dt_bias, ssm_a_log, ssm_d, ssm_norm_w, ssm_w_out,
              fox_w_in, fox_b_f, fox_w_out, moba_w_in, moba_w_out):
    h = x
    for layer in range(DEPTH):
        kind, j = layer % N_MIXERS, layer // N_MIXERS
        ff = swiglu(h, ffn_w_gate[layer, 0], ffn_w_up[layer, 0], ffn_w_down[layer, 0])
        h = layer_norm(DEEPNORM_ALPHA * h + MACARON_WEIGHT * ff, ln_g[layer, 0], ln_b[layer, 0])
        if kind == 0:
            mix = mamba2_mixer(h, ssm_w_in[j], ssm_conv_w[j], ssm_conv_b[j], ssm_dt_bias[j],
                               ssm_a_log[j], ssm_d[j], ssm_norm_w[j], ssm_w_out[j])
        elif kind == 1:
            mix = fox_attention(h, fox_w_in[j], fox_b_f[j], fox_w_out[j])
        else:
            mix = moba_attention(h, moba_w_in[j], moba_w_out[j])
        h = layer_norm(DEEPNORM_ALPHA * h + mix, ln_g[layer, 1], ln_b[layer, 1])
        ff = swiglu(h, ffn_w_gate[layer, 1], ffn_w_up[layer, 1], ffn_w_down[layer, 1])
        h = layer_norm(DEEPNORM_ALPHA * h + MACARON_WEIGHT * ff, ln_g[layer, 2], ln_b[layer, 2])
    return h
```

```python
import contextlib
import numpy as np
import concourse.bass as bass
import concourse.mybir as mybir
from concourse.bass_utils import run_bass_kernel_spmd

F32 = mybir.dt.float32
BF16 = mybir.dt.bfloat16
AF = mybir.ActivationFunctionType
ALU = mybir.AluOpType
AX = mybir.AxisListType

COMPUTE = ("pe", "act", "dve", "pool")
N_DMA_SEMS = 12


class _Op:
    __slots__ = ("eng", "fn", "deps", "is_dma", "ndma", "signal", "sem", "val", "prev")

    def __init__(self, eng, fn, is_dma, ndma):
        self.eng = eng
        self.fn = fn
        self.deps = set()
        self.is_dma = is_dma
        self.ndma = ndma
        self.signal = False
        self.sem = None
        self.val = None


class Sched:
    def __init__(self, nc):
        self.nc = nc
        self.ops = []
        self.last_w = {}
        self.readers = {}
        self.nstage = 0
        self.sfx = ""

    def new_stage(self):
        self.nstage += 1
        self.sfx = "_s%d" % self.nstage

    def op(self, eng, fn, reads=(), writes=(), dma=0):
        def _isps(k):
            return isinstance(k[0], str) and k[0].startswith("ps_")

        def _norm(k):
            return tuple(x for i, x in enumerate(k) if i == 0 or not isinstance(x, str)) if _isps(k) else k
        writes = [_norm(k) for k in writes] + [_norm(k) for k in reads if _isps(k)]
        reads = [k for k in reads if not _isps(k)]
        o = _Op(eng, fn, dma > 0, dma)
        idx = len(self.ops)
        for k in reads:
            w = self.last_w.get(k)
            if w is not None:
                o.deps.add(w)
        for k in writes:
            w = self.last_w.get(k)
            if w is not None:
                o.deps.add(w)
            for r in self.readers.get(k, ()):
                o.deps.add(r)
        for k in reads:
            self.readers.setdefault(k, []).append(idx)
        for k in writes:
            self.last_w[k] = idx
            self.readers[k] = []
        o.deps.discard(idx)
        self.ops.append(o)
        return idx

    def pe(self, fn, reads=(), writes=()):
        return self.op("pe", fn, reads, writes)

    def act(self, fn, reads=(), writes=()):
        return self.op("act", fn, reads, writes)

    def dve(self, fn, reads=(), writes=()):
        return self.op("dve", fn, reads, writes)

    def pool(self, fn, reads=(), writes=()):
        return self.op("pool", fn, reads, writes)

    def dma(self, fn, reads=(), writes=(), n=1, q="sp"):
        return self.op(q, fn, reads, writes, dma=n)

    def setup(self, es):
        nc = self.nc
        self.sems = {e: es.enter_context(nc.semaphore("s_" + e)) for e in COMPUTE}
        self.dsems = [es.enter_context(nc.semaphore("d%d" % i)) for i in range(N_DMA_SEMS)]
        self.cnt = {e: 0 for e in COMPUTE}
        self.dtot = [0] * N_DMA_SEMS
        self.rr = 0

    def emit(self):
        nc = self.nc
        ops = self.ops
        for o in ops:
            if o.eng == "pe":
                o.deps = {d for d in o.deps if not (ops[d].eng == "pe" and not ops[d].is_dma)}
        for o in ops:
            for d in o.deps:
                ops[d].signal = True
        engs = {"pe": nc.tensor, "act": nc.scalar, "dve": nc.vector, "pool": nc.gpsimd, "sp": nc.sync}
        by_eng = {e: [] for e in engs}
        for i, o in enumerate(ops):
            by_eng[o.eng].append(i)
        for e in COMPUTE:
            comp = [i for i in by_eng[e] if not ops[i].is_dma]
            if comp:
                ops[comp[-1]].signal = True
        sems, dsems, cnt, dtot = self.sems, self.dsems, self.cnt, self.dtot
        for o in ops:
            if o.is_dma:
                si = self.rr % N_DMA_SEMS
                self.rr += 1
                o.sem = ("d", si)
                o.prev = dtot[si]
                dtot[si] += 16 * o.ndma
                o.val = dtot[si]
            elif o.signal:
                cnt[o.eng] += 1
                o.sem = ("c", o.eng)
                o.val = cnt[o.eng]
        final = [(("c", e), cnt[e]) for e in COMPUTE] + [(("d", i), dtot[i]) for i in range(N_DMA_SEMS)]

        def semobj(s):
            return dsems[s[1]] if s[0] == "d" else sems[s[1]]

        def run_engine(ename, eng):
            waited = {}

            def wait(s, v):
                if v <= 0:
                    return
                if waited.get(s, 0) >= v:
                    return
                eng.wait_ge(semobj(s), v)
                waited[s] = v
            for i in by_eng[ename]:
                o = ops[i]
                for d in sorted(o.deps):
                    po = ops[d]
                    wait(po.sem, po.val)
                if o.is_dma:
                    wait(o.sem, o.prev)
                    insts = o.fn(eng)
                    if not isinstance(insts, (list, tuple)):
                        insts = [insts]
                    assert len(insts) == o.ndma, (len(insts), o.ndma)
                    for ins in insts:
                        ins.then_inc(semobj(o.sem), 16)
                else:
                    ins = o.fn(eng)
                    if o.signal:
                        ins.then_inc(semobj(o.sem), 1)
            for (s_, v_) in final:
                wait(s_, v_)

        with nc.Block() as block:
            @block.tensor
            def _(e):
                run_engine("pe", e)

            @block.scalar
            def _(e):
                run_engine("act", e)

            @block.vector
            def _(e):
                run_engine("dve", e)

            @block.gpsimd
            def _(e):
                run_engine("pool", e)

            @block.sync
            def _(e):
                run_engine("sp", e)
        self.ops = []
        self.last_w = {}
        self.readers = {}


D = 1024
DFF = 2816
NF = DFF // 128
LN_EPS = 1e-5
ALPHA = 8.0 ** 0.25


def ln_feature_major(S, x, keyx, ntile, gam, bet, ones_bf, scr, ps_s, ps_q, tag, out_bf=None, key_bf=None):
    ybf, sqb, mean, rstd, nmr, tmp = scr["ybf"], scr["sqb"], scr["mean"], scr["rstd"], scr["nmr"], scr["tmp"]
    for t in range(ntile):
        sl = slice(t * 512, (t + 1) * 512)
        for d in range(8):
            S.act(lambda e, d=d, sl=sl: e.activation(out=ybf[:, d, :], in_=x[:, d, sl], func=AF.Copy),
                  reads=[(keyx, d, t)], writes=[("ybf", d)])
            S.act(lambda e, d=d, sl=sl: e.activation(out=sqb[:, d, :], in_=x[:, d, sl], func=AF.Square),
                  reads=[(keyx, d, t)], writes=[("sqb", d)])
        for d in range(8):
            S.pe(lambda e, d=d: e.matmul(ps_s[:, :], lhsT=ones_bf[:, :], rhs=ybf[:, d, :], start=(d == 0), stop=(d == 7)),
                 reads=[("ybf", d), ("ones",)], writes=[("ps_s",)])
        for d in range(8):
            S.pe(lambda e, d=d: e.matmul(ps_q[:, :], lhsT=ones_bf[:, :], rhs=sqb[:, d, :], start=(d == 0), stop=(d == 7)),
                 reads=[("sqb", d), ("ones",)], writes=[("ps_q",)])
        S.dve(lambda e: e.tensor_scalar(out=mean[:, :], in0=ps_s[:, :], scalar1=1.0 / D, scalar2=None, op0=ALU.mult),
              reads=[("ps_s",)], writes=[("mean",)])
        S.dve(lambda e: e.tensor_tensor(out=nmr[:, :], in0=mean[:, :], in1=mean[:, :], op=ALU.mult),
              reads=[("mean",)], writes=[("nmr",)])
        S.dve(lambda e: e.scalar_tensor_tensor(out=rstd[:, :], in0=ps_q[:, :], scalar=1.0 / D, in1=nmr[:, :],
                                               op0=ALU.mult, op1=ALU.subtract),
              reads=[("ps_q",), ("nmr",)], writes=[("rstd",)])
        S.dve(lambda e: e.tensor_scalar(out=rstd[:, :], in0=rstd[:, :], scalar1=LN_EPS, scalar2=None, op0=ALU.add),
              reads=[("rstd",)], writes=[("rstd",)])
        S.act(lambda e: e.activation(out=rstd[:, :], in_=rstd[:, :], func=AF.Ln),
              reads=[("rstd",)], writes=[("rstd",)])
        S.act(lambda e: e.activation(out=rstd[:, :], in_=rstd[:, :], func=AF.Exp, scale=-0.5),
              reads=[("rstd",)], writes=[("rstd",)])
        S.dve(lambda e: e.scalar_tensor_tensor(out=nmr[:, :], in0=mean[:, :], scalar=-1.0, in1=rstd[:, :],
                                               op0=ALU.mult, op1=ALU.mult),
              reads=[("mean",), ("rstd",)], writes=[("nmr",)])
        for d in range(8):
            S.dve(lambda e, d=d, sl=sl: e.tensor_tensor(out=tmp[:, d % 2, :], in0=x[:, d, sl], in1=rstd[:, :], op=ALU.mult),
                  reads=[(keyx, d, t), ("rstd",)], writes=[("lntmp", d % 2)])
            S.pool(lambda e, d=d: e.tensor_tensor(out=tmp[:, d % 2, :], in0=tmp[:, d % 2, :], in1=nmr[:, :], op=ALU.add),
                   reads=[("lntmp", d % 2), ("nmr",)], writes=[("lntmp", d % 2)])
            S.act(lambda e, d=d, sl=sl: e.activation(out=x[:, d, sl], in_=tmp[:, d % 2, :], func=AF.Identity,
                                                     scale=gam[:, d:d + 1], bias=bet[:, d:d + 1]),
                  reads=[("lntmp", d % 2), ("lnp",)], writes=[(keyx, d, t)])
            if out_bf is not None:
                S.act(lambda e, d=d, sl=sl: e.activation(out=out_bf[:, d, sl], in_=tmp[:, d % 2, :], func=AF.Identity,
                                                         scale=gam[:, d:d + 1], bias=bet[:, d:d + 1]),
                      reads=[("lntmp", d % 2), ("lnp",)], writes=[(key_bf, d, t)])


def stage_ffn(S, Hin, Hout, wg, wu, wd, lng, lnb, T, TP=1024):
    nc = S.nc
    S.new_stage()
    npass = T // TP
    ntile = TP // 512
    with contextlib.ExitStack() as es:
        sb = lambda name, shape, dt: es.enter_context(nc.sbuf_tensor(name + S.sfx, shape, dt))
        x = sb("x", [128, 8, TP], F32)
        xbf = sb("xbf", [128, 8, TP], BF16)
        hmid = sb("hmid", [128, NF, TP], BF16)
        stg = [sb("stg%d" % i, [128, 8, 256], F32) for i in range(2)]
        wgb = [sb("wgb%d" % i, [128, 8, 256], BF16) for i in range(2)]
        wub = [sb("wub%d" % i, [128, 8, 256], BF16) for i in range(2)]
        wdb = [sb("wdb%d" % i, [128, NF, 256], BF16) for i in range(2)]
        sg = [sb("sg%d" % i, [128, 512], F32) for i in range(2)]
        otmp = [sb("otmp%d" % i, [128, 512], F32) for i in range(2)]
        scr = dict(ybf=sb("ybf", [128, 8, 512], BF16), sqb=sb("sqb", [128, 8, 512], BF16),
                   mean=sb("mean", [128, 512], F32), rstd=sb("rstd", [128, 512], F32),
                   nmr=sb("nmr", [128, 512], F32), tmp=sb("lntmp", [128, 2, 512], F32))
        gam = sb("gam", [128, 8], F32)
        bet = sb("bet", [128, 8], F32)
        ones_bf = sb("ones_bf", [128, 128], BF16)
        ps_g = [es.enter_context(nc.psum_tensor("ps_g%d" % i + S.sfx, [128, 512], F32)) for i in range(2)]
        ps_u = [es.enter_context(nc.psum_tensor("ps_u%d" % i + S.sfx, [128, 512], F32)) for i in range(2)]
        ps_o = [es.enter_context(nc.psum_tensor("ps_o%d" % i + S.sfx, [128, 512], F32)) for i in range(2)]
        ps_s = es.enter_context(nc.psum_tensor("ps_s" + S.sfx, [128, 512], F32))
        ps_q = es.enter_context(nc.psum_tensor("ps_q" + S.sfx, [128, 512], F32))

        S.dma(lambda e: [e.dma_start(out=gam[:, :], in_=lng),
                         e.dma_start(out=bet[:, :], in_=lnb)],
              writes=[("lnp",)], n=2)
        S.pool(lambda e: e.memset(ones_bf[:, :], 1.0), writes=[("ones",)])
        Hin_v = Hin.rearrange("(c p) t -> p c t", p=128)
        Hout_v = Hout.rearrange("(c p) t -> p c t", p=128)
        wg_v = wg.rearrange("(c p) f -> p c f", p=128)
        wu_v = wu.rearrange("(c p) f -> p c f", p=128)
        wd_v = wd.rearrange("(c p) d -> p c d", p=128)
        stg_i = 0
        wi = 0
        gq = 0
        for p in range(npass):
            t0 = p * TP
            for c in range(8):
                S.dma(lambda e, c=c, t0=t0: e.dma_start(out=x[:, c, :], in_=Hin_v[:, c, t0:t0 + TP]),
                      writes=[("x", c, t) for t in range(ntile)])
                for t in range(ntile):
                    S.act(lambda e, c=c, t=t: e.activation(out=xbf[:, c, t * 512:(t + 1) * 512], in_=x[:, c, t * 512:(t + 1) * 512], func=AF.Copy),
                          reads=[("x", c, t)], writes=[("xbf", c, t)])
            for fg in range(NF // 2):
                f0 = fg * 256
                b = wi % 2
                wi += 1
                for (wv, wb, nm) in ((wg_v, wgb, "wgb"), (wu_v, wub, "wub")):
                    s = stg_i % 2
                    stg_i += 1
                    S.dma(lambda e, wv=wv, s=s, f0=f0: e.dma_start(out=stg[s][:, :, :], in_=wv[:, :, f0:f0 + 256]),
                          writes=[("stg", s)])
                    S.pool(lambda e, wb=wb, b=b, s=s: e.tensor_copy(out=wb[b][:, :, :], in_=stg[s][:, :, :]),
                           reads=[("stg", s)], writes=[(nm, b)])
                for fc in range(2):
                    f = fg * 2 + fc
                    for t in range(ntile):
                        q = gq % 2
                        gq += 1
                        sl = slice(t * 512, (t + 1) * 512)
                        for k in range(8):
                            S.pe(lambda e, q=q, b=b, k=k, fc=fc, sl=sl: e.matmul(
                                ps_g[q][:, :], lhsT=wgb[b][:, k, fc * 128:(fc + 1) * 128], rhs=xbf[:, k, sl],
                                start=(k == 0), stop=(k == 7)),
                                reads=[("wgb", b), ("xbf", k, t)], writes=[("ps_g", q)])
                        for k in range(8):
                            S.pe(lambda e, q=q, b=b, k=k, fc=fc, sl=sl: e.matmul(
                                ps_u[q][:, :], lhsT=wub[b][:, k, fc * 128:(fc + 1) * 128], rhs=xbf[:, k, sl],
                                start=(k == 0), stop=(k == 7)),
                                reads=[("wub", b), ("xbf", k, t)], writes=[("ps_u", q)])
                        S.act(lambda e, q=q: e.activation(out=sg[q][:, :], in_=ps_g[q][:, :], func=AF.Silu),
                              reads=[("ps_g", q)], writes=[("sg", q)])
                        S.dve(lambda e, q=q, f=f, sl=sl: e.tensor_tensor(out=hmid[:, f, sl], in0=sg[q][:, :], in1=ps_u[q][:, :], op=ALU.mult),
                              reads=[("sg", q), ("ps_u", q)], writes=[("hmid", f, t)])
            oq = 0
            for dg in range(4):
                d0 = dg * 256
                b = dg % 2
                for (c0, nch) in ((0, 8), (8, 8), (16, 6)):
                    s = stg_i % 2
                    stg_i += 1
                    S.dma(lambda e, s=s, c0=c0, nch=nch, d0=d0: e.dma_start(out=stg[s][:, 0:nch, :], in_=wd_v[:, c0:c0 + nch, d0:d0 + 256]),
                          writes=[("stg", s)])
                    S.pool(lambda e, b=b, s=s, c0=c0, nch=nch: e.tensor_copy(out=wdb[b][:, c0:c0 + nch, :], in_=stg[s][:, 0:nch, :]),
                           reads=[("stg", s)], writes=[("wdb", b, c0)])
                for dc in range(2):
                    d = dg * 2 + dc
                    for t in range(ntile):
                        q = oq % 2
                        oq += 1
                        sl = slice(t * 512, (t + 1) * 512)
                        for f in range(NF):
                            S.pe(lambda e, q=q, b=b, f=f, dc=dc, sl=sl: e.matmul(
                                ps_o[q][:, :], lhsT=wdb[b][:, f, dc * 128:(dc + 1) * 128], rhs=hmid[:, f, sl],
                                start=(f == 0), stop=(f == NF - 1)),
                                reads=[("wdb", b, (f // 8) * 8), ("hmid", f, t)], writes=[("ps_o", q)])
                        S.act(lambda e, q=q: e.activation(out=otmp[q][:, :], in_=ps_o[q][:, :], func=AF.Copy, scale=0.5),
                              reads=[("ps_o", q)], writes=[("otmp", q)])
                        S.dve(lambda e, q=q, d=d, sl=sl: e.scalar_tensor_tensor(out=x[:, d, sl], in0=x[:, d, sl], scalar=ALPHA, in1=otmp[q][:, :],
                                                                                op0=ALU.mult, op1=ALU.add),
                              reads=[("otmp", q), ("x", d, t)], writes=[("x", d, t)])
            ln_feature_major(S, x, "x", ntile, gam, bet, ones_bf, scr, ps_s, ps_q, "ffn")
            for c in range(8):
                S.dma(lambda e, c=c, t0=t0: e.dma_start(out=Hout_v[:, c, t0:t0 + TP], in_=x[:, c, :]),
                      reads=[("x", c, t) for t in range(ntile)], writes=[("Hout", p, c)])
        S.emit()


SEQ = 8192
NBLK = SEQ // 128
NEG = -30000.0


def load_cast_weight(S, w_dram_view, dst_bf, stg, key, ncol, nrowchunks=8):
    for c0 in range(0, ncol, 256):
        w = min(256, ncol - c0)
        S.dma(lambda e, c0=c0, w=w: e.dma_start(out=stg[:, 0:nrowchunks, 0:w], in_=w_dram_view[:, :, c0:c0 + w]),
              writes=[("stg_shared",)])
        S.pool(lambda e, c0=c0, w=w: e.tensor_copy(out=dst_bf[:, :, c0:c0 + w], in_=stg[:, 0:nrowchunks, 0:w]),
               reads=[("stg_shared",)], writes=[(key,)])


def stage_attn(S, kind, Hfull, wq, wk, wv, wf, bfr, cst, oh, OT, dbg=None):
    nc = S.nc
    S.new_stage()
    fox = kind == "fox"
    with contextlib.ExitStack() as es:
        sb = lambda name, shape, dt: es.enter_context(nc.sbuf_tensor(name + S.sfx, shape, dt))
        pst = lambda name, shape, dt=F32: es.enter_context(nc.psum_tensor(name + S.sfx, shape, dt))
        QT = sb("QT", [128, 2, SEQ], BF16)
        KT = sb("KT", [128, 2, SEQ], BF16)
        V = sb("V", [128, NBLK, 4, 66], BF16)
        xt = [sb("xt%d" % i, [128, 8, 512], F32) for i in range(2)]
        xb = [sb("xb%d" % i, [128, 8, 512], BF16) for i in range(2)]
        stg = sb("stg", [128, 8, 256], F32)
        wqb = sb("wqb", [128, 8, 256], BF16)
        wkb = sb("wkb", [128, 8, 256], BF16)
        wvb = sb("wvb", [128, 8, 256], BF16)
        cf = sb("cf", [128, 384], F32)
        identb = sb("identb", [128, 128], BF16)
        trimb = sb("trimb", [128, 128], BF16)
        onesf = sb("onesf", [128, 128], F32)
        onecol = sb("onecol", [128, 1], F32)
        Pt = [sb("Pt%d" % i, [128, 512], BF16) for i in range(2)]
        rc = sb("rc", [128, 512], F32)
        rb = sb("rb", [128, 512], F32)
        ot = [sb("ot%d" % i, [128, 512], BF16) for i in range(2)]
        ps_p = [pst("ps_p%d" % i, [128, 512]) for i in range(2)]
        ps_S = [pst("ps_S%d" % i, [128, 512]) for i in range(2)]
        ps_O = [pst("ps_O%d" % i, [128, 512]) for i in range(2)]
        ps_x = pst("ps_x", [128, 512])
        ps_y = pst("ps_y", [128, 512])
        if fox:
            wfb = sb("wfb", [128, 8, 4], BF16)
            wfs = sb("wfs", [128, 8, 4], F32)
            bft = sb("bft", [128, 4], F32)
            lf = sb("lf", [128, NBLK, 4], F32)
            hsA = sb("hsA", [128, NBLK, 4], F32)
            hsB = sb("hsB", [128, NBLK, 4], F32)
            tot = sb("tot", [128, NBLK, 4], F32)
            cum = sb("cum", [128, NBLK, 4], F32)
            ncum = sb("ncum", [128, NBLK, 4], F32)
            dg = [sb("dg%d" % i, [128, 128], F32) for i in range(2)]
            cqb = [sb("cqb%d" % i, [128, 512], F32) for i in range(2)]
            Sb = [sb("Sb%d" % i, [128, 512], F32) for i in range(2)]
        else:
            ohs = sb("ohs", [32, 32 * 128], F32)
            ohb = sb("ohb", [32, 32, 128], BF16)
            id30 = sb("id30", [128, 128], BF16)
            kmf = sb("kmf", [128, 2, 32], F32)
            kmb = sb("kmb", [128, 2, 32], BF16)
            G = sb("G", [128, 32], F32)
            mx = sb("mx", [128, 8], F32)
            selm = sb("selm", [128, 4, 32], BF16)
            selbT = [sb("selbT%d" % i, [32, 512], BF16) for i in range(2)]

        ident = cf[:, 0:128]
        triU = cf[:, 128:256]
        S.dma(lambda e: e.dma_start(out=cf[:, :], in_=cst), writes=[("cf",)])
        S.pool(lambda e: e.tensor_copy(out=identb[:, :], in_=cf[:, 0:128]), reads=[("cf",)], writes=[("identb",)])
        S.pool(lambda e: e.tensor_copy(out=trimb[:, :], in_=cf[:, 256:384]), reads=[("cf",)], writes=[("trimb",)])
        S.pool(lambda e: e.memset(onesf[:, :], 1.0), writes=[("onesf",)])
        S.pool(lambda e: e.memset(onecol[:, :], 1.0), writes=[("onecol",)])
        S.pool(lambda e: e.memset(V[:, :, :, 64:66], 1.0), writes=[("Vones",)])
        Hv = Hfull.rearrange("(c p) t -> p c t", p=128)
        load_cast_weight(S, wq.rearrange("(c p) f -> p c f", p=128), wqb, stg, "wqb", 256)
        load_cast_weight(S, wk.rearrange("(c p) f -> p c f", p=128), wkb, stg, "wkb", 256)
        load_cast_weight(S, wv.rearrange("(c p) f -> p c f", p=128), wvb, stg, "wvb", 256)
        if fox:
            S.dma(lambda e: [e.dma_start(out=wfs[:, :, :], in_=wf.rearrange("(c p) f -> p c f", p=128)),
                             e.dma_start(out=bft[:, :], in_=bfr)], writes=[("wfs",), ("bft",)], n=2)
            S.pool(lambda e: e.tensor_copy(out=wfb[:, :, :], in_=wfs[:, :, :]), reads=[("wfs",)], writes=[("wfb",)])
        else:
            S.dma(lambda e: e.dma_start(out=ohs[:, :], in_=oh), writes=[("ohs",)])
            S.pool(lambda e: e.tensor_copy(out=ohb[:, :, :], in_=ohs[:, :].rearrange("p (n m) -> p n m", m=128)),
                   reads=[("ohs",)], writes=[("ohb",)])
            S.pool(lambda e: e.tensor_scalar(out=id30[:, :], in0=cf[:, 0:128], scalar1=-NEG, scalar2=None, op0=ALU.mult),
                   reads=[("cf",)], writes=[("id30",)])

        pq = 0
        for tt in range(SEQ // 512):
            t0 = tt * 512
            b = tt % 2
            S.dma(lambda e, b=b, t0=t0: e.dma_start(out=xt[b][:, :, :], in_=Hv[:, :, t0:t0 + 512]), writes=[("xt", b)])
            S.pool(lambda e, b=b: e.tensor_copy(out=xb[b][:, :, :], in_=xt[b][:, :, :]), reads=[("xt", b)], writes=[("xb", b)])
            for (wb, wkey, dst, dkey, scale) in ((wqb, "wqb", QT, "QT", 0.125), (wkb, "wkb", KT, "KT", 1.0)):
                for hp in range(2):
                    q = pq % 2
                    pq += 1
                    for k in range(8):
                        S.pe(lambda e, q=q, wb=wb, k=k, hp=hp, b=b: e.matmul(
                            ps_p[q][:, :], lhsT=wb[:, k, hp * 128:(hp + 1) * 128], rhs=xb[b][:, k, :], start=(k == 0), stop=(k == 7)),
                            reads=[(wkey,), ("xb", b)], writes=[("ps_p", q)])
                    S.act(lambda e, q=q, dst=dst, hp=hp, t0=t0, scale=scale: e.activation(
                        out=dst[:, hp, t0:t0 + 512], in_=ps_p[q][:, :], func=AF.Copy, scale=scale),
                        reads=[("ps_p", q)], writes=[(dkey, hp, tt)])
            for s in range(4):
                blk = tt * 4 + s
                q = pq % 2
                pq += 1
                for k in range(8):
                    S.pe(lambda e, q=q, k=k, s=s, b=b: e.matmul(
                        ps_p[q][:, 0:256], lhsT=xb[b][:, k, s * 128:(s + 1) * 128], rhs=wvb[:, k, :], start=(k == 0), stop=(k == 7)),
                        reads=[("wvb",), ("xb", b)], writes=[("ps_p", q)])
                S.dve(lambda e, q=q, blk=blk: e.tensor_copy(out=V[:, blk, :, 0:64], in_=ps_p[q][:, 0:256].rearrange("p (h d) -> p h d", h=4)),
                      reads=[("ps_p", q)], writes=[("V", blk)])
                if fox:
                    q = pq % 2
                    pq += 1
                    for k in range(8):
                        S.pe(lambda e, q=q, k=k, s=s, b=b: e.matmul(
                            ps_p[q][:, 0:4], lhsT=xb[b][:, k, s * 128:(s + 1) * 128], rhs=wfb[:, k, :], start=(k == 0), stop=(k == 7)),
                            reads=[("wfb",), ("xb", b)], writes=[("ps_p", q)])
                    S.dve(lambda e, q=q, blk=blk: e.tensor_tensor(out=lf[:, blk, :], in0=ps_p[q][:, 0:4], in1=bft[:, :], op=ALU.add),
                          reads=[("ps_p", q), ("bft",)], writes=[("lf",)])
        if fox:
            lf2 = lf[:, :, :].rearrange("p n h -> p (n h)")
            S.act(lambda e: e.activation(out=lf2, in_=lf2, func=AF.Exp, scale=-1.0), reads=[("lf",)], writes=[("lf",)])
            S.act(lambda e: e.activation(out=lf2, in_=lf2, func=AF.Ln, bias=onecol[:, 0:1]), reads=[("lf",), ("onecol",)], writes=[("lf",)])
            S.dve(lambda e: e.tensor_scalar(out=lf2, in0=lf2, scalar1=-1.0, scalar2=None, op0=ALU.mult), reads=[("lf",)], writes=[("lf",)])
            S.pe(lambda e: e.matmul(ps_x[:, 0:256], lhsT=triU, rhs=lf2, start=True, stop=True), reads=[("lf",), ("cf",)], writes=[("ps_x",)])
            S.pe(lambda e: e.matmul(ps_y[:, 0:256], lhsT=onesf[:, :], rhs=lf2, start=True, stop=True), reads=[("lf",), ("onesf",)], writes=[("ps_y",)])
            S.dve(lambda e: e.tensor_copy(out=tot[:, :, :].rearrange("p n h -> p (n h)"), in_=ps_y[:, 0:256]), reads=[("ps_y",)], writes=[("tot",)])
            S.dve(lambda e: e.tensor_copy(out=hsA[:, :, :], in_=tot[:, :, :]), reads=[("tot",)], writes=[("hsA",)])
            A, B, ka, kb_ = hsA, hsB, "hsA", "hsB"
            for sft in (1, 2, 4, 8, 16, 32):
                S.dve(lambda e, A=A, B=B, sft=sft: e.tensor_tensor(out=B[:, sft:, :], in0=A[:, sft:, :], in1=A[:, 0:NBLK - sft, :], op=ALU.add),
                      reads=[(ka,)], writes=[(kb_,)])
                S.dve(lambda e, A=A, B=B, sft=sft: e.tensor_copy(out=B[:, 0:sft, :], in_=A[:, 0:sft, :]),
                      reads=[(ka,)], writes=[(kb_,)])
                A, B, ka, kb_ = B, A, kb_, ka
            S.dve(lambda e, A=A: e.tensor_tensor(out=cum[:, :, :], in0=A[:, :, :], in1=tot[:, :, :], op=ALU.subtract),
                  reads=[(ka,), ("tot",)], writes=[("cum",)])
            S.dve(lambda e: e.tensor_tensor(out=cum[:, :, :].rearrange("p n h -> p (n h)"), in0=cum[:, :, :].rearrange("p n h -> p (n h)"),
                                            in1=ps_x[:, 0:256], op=ALU.add),
                  reads=[("cum",), ("ps_x",)], writes=[("cum",)])
            S.dve(lambda e: e.tensor_scalar(out=ncum[:, :, :], in0=cum[:, :, :], scalar1=-1.0, scalar2=None, op0=ALU.mult),
                  reads=[("cum",)], writes=[("ncum",)])
        else:
            for hp in range(2):
                S.dve(lambda e, hp=hp: e.tensor_reduce(out=kmf[:, hp, :], in_=KT[:, hp, :].rearrange("p (n l) -> p n l", l=256),
                                                       axis=AX.X, op=ALU.add),
                      reads=[("KT", hp, tt) for tt in range(SEQ // 512)], writes=[("kmf",)])
            S.dve(lambda e: e.tensor_copy(out=kmb[:, :, :], in_=kmf[:, :, :]), reads=[("kmf",)], writes=[("kmb",)])

        if dbg is not None and fox:
            S.pool(lambda e: e.tensor_copy(out=Sb[0][:, :], in_=QT[:, 0, 512:1024]), reads=[("QT", 0, 1)], writes=[("Sb", 0)])
            S.pool(lambda e: e.tensor_copy(out=Sb[1][:, 0:128], in_=KT[:, 0, 256:384]), reads=[("KT", 0, 0)], writes=[("Sb", 1)])
            S.dma(lambda e: [e.dma_start(out=dbg[:, 1536:2048], in_=Sb[0][:, :]), e.dma_start(out=dbg[:, 2048:2176], in_=Sb[1][:, 0:128])],
                  reads=[("Sb", 0), ("Sb", 1)], writes=[("dbg3",)], n=2)
            S.dma(lambda e: [e.dma_start(out=dbg[:, 0:256], in_=cum[:, :, :].rearrange("p n h -> p (n h)")),
                             e.dma_start(out=dbg[:, 256:512], in_=lf[:, :, :].rearrange("p n h -> p (n h)"))],
                  reads=[("cum",), ("lf",)], writes=[("dbg",)], n=2)
        sqi = 0
        oqi = 0
        for h in range(4):
            hp = h // 2
            r0 = (h % 2) * 64
            rows = slice(r0, r0 + 64)
            if not fox:
                S.dve(lambda e: e.memset(G[:, :], -1e30), writes=[("G",)])
            for qt in range(SEQ // 512):
                q0 = qt * 512
                cb = qt % 2
                if fox:
                    for s in range(4):
                        blk = 4 * qt + s
                        S.dve(lambda e, s=s, blk=blk, h=h: e.tensor_scalar(out=dg[s % 2][:, :], in0=ident, scalar1=cum[:, blk, h:h + 1],
                                                                          scalar2=None, op0=ALU.mult),
                              reads=[("cum",), ("cf",)], writes=[("dg", s % 2)])
                        S.pe(lambda e, s=s: e.matmul(ps_x[:, s * 128:(s + 1) * 128], lhsT=onesf[:, :], rhs=dg[s % 2][:, :], start=True, stop=True),
                             reads=[("dg", s % 2), ("onesf",)], writes=[("ps_x",)])
                    S.act(lambda e, cb=cb: e.activation(out=cqb[cb][:, :], in_=ps_x[:, :], func=AF.Copy),
                          reads=[("ps_x",)], writes=[("cqb", cb)])
                else:
                    for s in range(4):
                        qb = 4 * qt + s
                        own = qb // 2
                        if own >= 3:
                            S.pe(lambda e, s=s, hp=hp, rows=rows, q0=q0: e.matmul(
                                ps_x[:, s * 32:(s + 1) * 32], lhsT=QT[rows, hp, q0 + s * 128:q0 + (s + 1) * 128], rhs=kmb[rows, hp, :],
                                start=True, stop=True),
                                reads=[("QT", hp, qt), ("kmb",)], writes=[("ps_x",)])
                            S.dve(lambda e, s=s, own=own: e.tensor_copy(out=G[:, 0:own], in_=ps_x[:, s * 32:s * 32 + own]),
                                  reads=[("ps_x",)], writes=[("G",)])
                            S.dve(lambda e: e.max(out=mx[:, :], in_=G[:, :]), reads=[("G",)], writes=[("mx",)])
                            S.dve(lambda e, s=s: e.tensor_scalar(out=selm[:, s, :], in0=G[:, :], scalar1=mx[:, 2:3], scalar2=-1.0,
                                                                op0=ALU.is_ge, op1=ALU.add),
                                  reads=[("G",), ("mx",)], writes=[("selm", s)])
                            S.dve(lambda e, s=s, own=own: e.memset(selm[:, s, own:own + 1], 0.0), writes=[("selm", s)])
                        else:
                            S.dve(lambda e, s=s: e.memset(selm[:, s, :], -1.0), writes=[("selm", s)])
                            S.dve(lambda e, s=s, own=own: e.memset(selm[:, s, 0:own + 1], 0.0), writes=[("selm", s)])
                        S.pe(lambda e, s=s: e.matmul(ps_y[0:32, s * 128:(s + 1) * 128], lhsT=selm[:, s, :], rhs=id30[:, :], start=True, stop=True),
                             reads=[("selm", s), ("id30",)], writes=[("ps_y",)])
                    S.act(lambda e, cb=cb: e.activation(out=selbT[cb][:, :], in_=ps_y[0:32, :], func=AF.Copy),
                          reads=[("ps_y",)], writes=[("selbT", cb)])
                nkb = 4 * qt + 4
                oq = oqi % 2
                oqi += 1
                for kb in range(nkb):
                    c0 = max(0, kb - 4 * qt) * 128
                    diag = kb >= 4 * qt
                    sq = sqi % 2
                    sqi += 1
                    last_s = not diag and fox
                    S.pe(lambda e, sq=sq, rows=rows, hp=hp, kb=kb, q0=q0, c0=c0, last_s=last_s: e.matmul(
                        ps_S[sq][:, c0:512], lhsT=KT[rows, hp, kb * 128:(kb + 1) * 128], rhs=QT[rows, hp, q0 + c0:q0 + 512],
                        start=True, stop=last_s),
                        reads=[("KT", hp, kb // 4), ("QT", hp, qt)], writes=[("ps_S", sq)])
                    if not fox:
                        S.pe(lambda e, sq=sq, kb=kb, cb=cb, c0=c0, diag=diag: e.matmul(
                            ps_S[sq][:, c0:512], lhsT=ohb[:, kb // 2, :], rhs=selbT[cb][:, c0:512], start=False, stop=(not diag)),
                            reads=[("ohb",), ("selbT", cb)], writes=[("ps_S", sq)])
                    if diag:
                        S.pe(lambda e, sq=sq, c0=c0: e.matmul(ps_S[sq][:, c0:c0 + 128], lhsT=identb[:, :], rhs=trimb[:, :], start=False, stop=True),
                             reads=[("identb",), ("trimb",)], writes=[("ps_S", sq)])
                    if fox:
                        S.dve(lambda e, sq=sq, cb=cb, c0=c0: e.tensor_tensor(out=Sb[sq][:, c0:512], in0=ps_S[sq][:, c0:512], in1=cqb[cb][:, c0:512], op=ALU.add),
                              reads=[("ps_S", sq), ("cqb", cb)], writes=[("Sb", sq)])
                        S.act(lambda e, sq=sq, c0=c0, kb=kb, h=h: e.activation(out=Pt[sq][:, c0:512], in_=Sb[sq][:, c0:512], func=AF.Exp,
                                                                             bias=ncum[:, kb, h:h + 1]),
                              reads=[("Sb", sq), ("ncum",)], writes=[("Pt", sq)])
                    else:
                        S.act(lambda e, sq=sq, c0=c0: e.activation(out=Pt[sq][:, c0:512], in_=ps_S[sq][:, c0:512], func=AF.Exp),
                              reads=[("ps_S", sq)], writes=[("Pt", sq)])
                    if dbg is not None and fox and h == 0 and qt == 1 and kb == 2:
                        S.dma(lambda e, sq=sq, cb=cb: [e.dma_start(out=dbg[:, 512:1024], in_=cqb[cb][:, :]),
                                                       e.dma_start(out=dbg[:, 1024:1536], in_=Sb[sq][:, :])],
                              reads=[("cqb", cb), ("Sb", sq)], writes=[("dbg2",)], n=2)
                    S.pe(lambda e, oq=oq, kb=kb, h=h, sq=sq, c0=c0, nkb=nkb: e.matmul(
                        ps_O[oq][0:65, c0:512], lhsT=V[:, kb, h, 0:65], rhs=Pt[sq][:, c0:512], start=(kb == 0), stop=(kb == nkb - 1)),
                        reads=[("V", kb), ("Vones",), ("Pt", sq)], writes=[("ps_O", oq)])
                S.dve(lambda e, oq=oq: e.reciprocal(out=rc[64:65, :], in_=ps_O[oq][64:65, :]), reads=[("ps_O", oq)], writes=[("rc",)])
                S.pe(lambda e: e.matmul(ps_x[0:64, :], lhsT=onesf[64:65, 0:64], rhs=rc[64:65, :], start=True, stop=True),
                     reads=[("rc",), ("onesf",)], writes=[("ps_x",)])
                S.act(lambda e: e.activation(out=rb[0:64, :], in_=ps_x[0:64, :], func=AF.Copy), reads=[("ps_x",)], writes=[("rb",)])
                S.dve(lambda e, oq=oq: e.tensor_tensor(out=ot[oq][0:64, :], in0=ps_O[oq][0:64, :], in1=rb[0:64, :], op=ALU.mult),
                      reads=[("ps_O", oq), ("rb",)], writes=[("ot", oq)])
                S.dma(lambda e, oq=oq, h=h, q0=q0: e.dma_start(out=OT[h * 64:(h + 1) * 64, q0:q0 + 512], in_=ot[oq][0:64, :]),
                      reads=[("ot", oq)], writes=[("OT", h, qt)])
        S.emit()


def stage_ssd(S, Hfull, wz, wx, wB, wC, wdt, cw, cbias, dtb, alog, dcol, nw, cst, YT, ntiles=SEQ // 512):
    nc = S.nc
    S.new_stage()
    with contextlib.ExitStack() as es:
        sb = lambda name, shape, dt: es.enter_context(nc.sbuf_tensor(name + S.sfx, shape, dt))
        pst = lambda name, shape, dt=F32: es.enter_context(nc.psum_tensor(name + S.sfx, shape, dt))
        decf = [sb("decf%d" % i, [128, 128], F32) for i in range(2)]
        xt = [sb("xt%d" % i, [128, 8, 512], F32) for i in range(2)]
        xb = [sb("xb%d" % i, [128, 8, 512], BF16) for i in range(2)]
        stg = sb("stg", [128, 8, 256], F32)
        wzb = sb("wzb", [128, 8, 512], BF16)
        wxb = sb("wxb", [128, 8, 512], BF16)
        wBb = sb("wBb", [128, 8, 128], BF16)
        wCb = sb("wCb", [128, 8, 128], BF16)
        wdts = sb("wdts", [128, 8, 8], F32)
        wdtb = sb("wdtb", [128, 8, 8], BF16)
        cf = sb("cf", [128, 384], F32)
        identb = sb("identb", [128, 128], BF16)
        trimb = sb("trimb", [128, 128], BF16)
        onesf = sb("onesf", [128, 128], F32)
        onesb = sb("onesb", [128, 128], BF16)
        onecol = sb("onecol", [128, 1], F32)
        cwt = sb("cwt", [128, 6, 4], F32)
        cbt = sb("cbt", [128, 6], F32)
        dtbt = sb("dtbt", [128, 8], F32)
        arep = sb("arep", [128, 8], F32)
        dct = sb("dct", [128, 4], F32)
        nwt = sb("nwt", [128, 4], F32)
        pre = sb("pre", [128, 6, 515], F32)
        acc = [sb("acc%d" % i, [128, 512], F32) for i in range(2)]
        xcf = sb("xcf", [128, 4, 512], F32)
        xcb = sb("xcb", [128, 4, 512], BF16)
        BTb = sb("BTb", [128, 512], BF16)
        CTb = sb("CTb", [128, 512], BF16)
        CTf = sb("CTf", [128, 512], F32)
        zs = sb("zs", [128, 4, 512], F32)
        xtokf = sb("xtokf", [128, 512], F32)
        Btokb = sb("Btokb", [128, 128], BF16)
        dtp = sb("dtp", [128, 8], F32)
        dtt = sb("dtt", [128, 8], F32)
        adt = sb("adt", [128, 8], F32)
        ncs = sb("ncs", [128, 8], F32)
        wtt = sb("wtt", [128, 8], F32)
        dtw = sb("dtw", [128, 8], F32)
        cbTf = sb("cbTf", [128, 128], F32)
        xdtp = sb("xdtp", [128, 8, 128], BF16)
        xdtw = sb("xdtw", [128, 512], BF16)
        arp = [sb("arp%d" % i, [128, 128], F32) for i in range(2)]
        Ef = [sb("Ef%d" % i, [128, 128], F32) for i in range(2)]
        MT = [sb("MT%d" % i, [128, 128], BF16) for i in range(2)]
        CTs = [sb("CTs%d" % i, [128, 128], BF16) for i in range(2)]
        Sf = sb("Sf", [128, 512], F32)
        stf = sb("stf", [128, 512], F32)
        Sbp = sb("Sbp", [128, 8, 128], BF16)
        yf = sb("yf", [128, 512], F32)
        yg = sb("yg", [128, 4, 512], F32)
        sqb = sb("sqb", [128, 4, 512], BF16)
        rstd = sb("rstd", [128, 512], F32)
        ytmp = [sb("ytmp%d" % i, [128, 512], F32) for i in range(2)]
        yo = [sb("yo%d" % i, [128, 512], BF16) for i in range(2)]
        ps_p = [pst("ps_p%d" % i, [128, 512]) for i in range(2)]
        ps_y = [pst("ps_y%d" % i, [128, 512]) for i in range(4)]
        ps_a = pst("ps_a", [128, 512])
        ps_s = pst("ps_s", [128, 512])
        triU = cf[:, 128:256]

        S.dma(lambda e: e.dma_start(out=cf[:, :], in_=cst), writes=[("cf",)])
        S.pool(lambda e: e.tensor_copy(out=identb[:, :], in_=cf[:, 0:128]), reads=[("cf",)], writes=[("identb",)])
        S.pool(lambda e: e.tensor_copy(out=trimb[:, :], in_=cf[:, 256:384]), reads=[("cf",)], writes=[("trimb",)])
        S.pool(lambda e: e.memset(onesf[:, :], 1.0), writes=[("onesf",)])
        S.pool(lambda e: e.memset(onesb[:, :], 1.0), writes=[("onesb",)])
        S.pool(lambda e: e.memset(onecol[:, :], 1.0), writes=[("onecol",)])
        S.pool(lambda e: e.memset(pre[:, :, 0:3], 0.0), writes=[("pre", cc) for cc in range(6)])
        S.pool(lambda e: e.memset(Sf[:, :], 0.0), writes=[("Sf", r) for r in range(8)])
        S.pool(lambda e: e.memset(Sbp[:, :, :], 0.0), writes=[("Sbp", r) for r in range(8)])
        S.pool(lambda e: e.memset(xdtp[:, :, :], 0.0), writes=[("xdtp", r) for r in range(8)])
        S.dma(lambda e: [e.dma_start(out=cwt[:, :, :], in_=cw), e.dma_start(out=cbt[:, :], in_=cbias),
                         e.dma_start(out=dtbt[:, :], in_=dtb), e.dma_start(out=arep[:, :], in_=alog),
                         e.dma_start(out=dct[:, :], in_=dcol), e.dma_start(out=nwt[:, :], in_=nw),
                         e.dma_start(out=wdts[:, :, :], in_=wdt.rearrange("(c p) f -> p c f", p=128))],
              writes=[("prm",)], n=7)
        S.pool(lambda e: e.tensor_copy(out=wdtb[:, :, :], in_=wdts[:, :, :]), reads=[("prm",)], writes=[("wdtb",)])
        S.act(lambda e: e.activation(out=arep[:, :], in_=arep[:, :], func=AF.Exp), reads=[("prm",)], writes=[("arep",)])
        S.dve(lambda e: e.tensor_scalar(out=arep[:, :], in0=arep[:, :], scalar1=-1.0, scalar2=None, op0=ALU.mult), reads=[("arep",)], writes=[("arep",)])
        Hv = Hfull.rearrange("(c p) t -> p c t", p=128)
        load_cast_weight(S, wz.rearrange("(c p) f -> p c f", p=128), wzb, stg, "wzb", 512)
        load_cast_weight(S, wx.rearrange("(c p) f -> p c f", p=128), wxb, stg, "wxb", 512)
        load_cast_weight(S, wB.rearrange("(c p) f -> p c f", p=128), wBb, stg, "wBb", 128)
        load_cast_weight(S, wC.rearrange("(c p) f -> p c f", p=128), wCb, stg, "wCb", 128)

        pq = 0
        hq = 0
        for tt in range(ntiles):
            t0 = tt * 512
            b = tt % 2
            S.dma(lambda e, b=b, t0=t0: e.dma_start(out=xt[b][:, :, :], in_=Hv[:, :, t0:t0 + 512]), writes=[("xt", b)])
            S.pool(lambda e, b=b: e.tensor_copy(out=xb[b][:, :, :], in_=xt[b][:, :, :]), reads=[("xt", b)], writes=[("xb", b)])
            for hc in range(4):
                q = pq % 2
                pq += 1
                for k in range(8):
                    S.pe(lambda e, q=q, k=k, hc=hc, b=b: e.matmul(ps_p[q][:, :], lhsT=wzb[:, k, hc * 128:(hc + 1) * 128], rhs=xb[b][:, k, :],
                                                                 start=(k == 0), stop=(k == 7)),
                         reads=[("wzb",), ("xb", b)], writes=[("ps_p", q)])
                S.act(lambda e, q=q, hc=hc: e.activation(out=zs[:, hc, :], in_=ps_p[q][:, :], func=AF.Silu),
                      reads=[("ps_p", q)], writes=[("zs", hc)])
            for cc in range(6):
                q = pq % 2
                pq += 1
                if cc < 4:
                    wsel, wkey, csl = wxb, "wxb", slice(cc * 128, (cc + 1) * 128)
                elif cc == 4:
                    wsel, wkey, csl = wBb, "wBb", slice(0, 128)
                else:
                    wsel, wkey, csl = wCb, "wCb", slice(0, 128)
                for k in range(8):
                    S.pe(lambda e, q=q, k=k, wsel=wsel, csl=csl, b=b: e.matmul(ps_p[q][:, :], lhsT=wsel[:, k, csl], rhs=xb[b][:, k, :],
                                                                            start=(k == 0), stop=(k == 7)),
                         reads=[(wkey,), ("xb", b)], writes=[("ps_p", q)])
                S.dve(lambda e, q=q, cc=cc: e.tensor_copy(out=pre[:, cc, 3:515], in_=ps_p[q][:, :]), reads=[("ps_p", q)], writes=[("pre", cc)])
                a = acc[cc % 2]
                ak = ("acc", cc % 2)
                S.dve(lambda e, a=a, cc=cc: e.tensor_scalar(out=a[:, :], in0=pre[:, cc, 0:512], scalar1=cwt[:, cc, 0:1], scalar2=None, op0=ALU.mult),
                      reads=[("pre", cc), ("prm",)], writes=[ak])
                for kk in range(1, 4):
                    S.dve(lambda e, a=a, cc=cc, kk=kk: e.scalar_tensor_tensor(out=a[:, :], in0=pre[:, cc, kk:kk + 512], scalar=cwt[:, cc, kk:kk + 1],
                                                                             in1=a[:, :], op0=ALU.mult, op1=ALU.add),
                          reads=[("pre", cc), ak], writes=[ak])
                S.pool(lambda e, cc=cc: e.tensor_copy(out=pre[:, cc, 0:3], in_=pre[:, cc, 512:515]), reads=[("pre", cc)], writes=[("pre", cc)])
                if cc < 4:
                    S.act(lambda e, a=a, cc=cc: e.activation(out=xcf[:, cc, :], in_=a[:, :], func=AF.Silu, bias=cbt[:, cc:cc + 1]),
                          reads=[ak, ("prm",)], writes=[("xcf", cc)])
                    S.pool(lambda e, cc=cc: e.tensor_copy(out=xcb[:, cc, :], in_=xcf[:, cc, :]), reads=[("xcf", cc)], writes=[("xcb", cc)])
                elif cc == 4:
                    S.act(lambda e, a=a, cc=cc: e.activation(out=BTb[:, :], in_=a[:, :], func=AF.Silu, bias=cbt[:, cc:cc + 1]),
                          reads=[ak, ("prm",)], writes=[("BTb",)])
                else:
                    S.act(lambda e, a=a, cc=cc: e.activation(out=CTf[:, :], in_=a[:, :], func=AF.Silu, bias=cbt[:, cc:cc + 1]),
                          reads=[ak, ("prm",)], writes=[("CTf",)])
                    S.pool(lambda e: e.tensor_copy(out=CTb[:, :], in_=CTf[:, :]), reads=[("CTf",)], writes=[("CTb",)])
            for c in range(4):
                cs = slice(c * 128, (c + 1) * 128)
                for cc in range(4):
                    S.pe(lambda e, cc=cc, cs=cs: e.matmul(ps_p[0][:, cc * 128:(cc + 1) * 128], lhsT=xcb[:, cc, cs], rhs=identb[:, :], start=True, stop=True),
                         reads=[("xcb", cc), ("identb",)], writes=[("ps_p", 0)])
                S.pe(lambda e, cs=cs: e.matmul(ps_p[1][:, 0:128], lhsT=BTb[:, cs], rhs=identb[:, :], start=True, stop=True),
                     reads=[("BTb",), ("identb",)], writes=[("ps_p", 1)])
                S.act(lambda e: e.activation(out=xtokf[:, :], in_=ps_p[0][:, :], func=AF.Copy), reads=[("ps_p", 0)], writes=[("xtokf",)])
                S.dve(lambda e: e.tensor_copy(out=Btokb[:, :], in_=ps_p[1][:, 0:128]), reads=[("ps_p", 1)], writes=[("Btokb",)])
                for k in range(8):
                    S.pe(lambda e, k=k, cs=cs, b=b: e.matmul(ps_a[:, 384:392], lhsT=xb[b][:, k, cs], rhs=wdtb[:, k, :], start=(k == 0), stop=(k == 7)),
                         reads=[("xb", b), ("wdtb",)], writes=[("ps_a", "dt")])
                S.dve(lambda e: e.tensor_tensor(out=dtp[:, :], in0=ps_a[:, 384:392], in1=dtbt[:, :], op=ALU.add),
                      reads=[("ps_a", "dt"), ("prm",)], writes=[("dtp",)])
                S.act(lambda e: e.activation(out=dtp[:, :], in_=dtp[:, :], func=AF.Exp), reads=[("dtp",)], writes=[("dtp",)])
                S.act(lambda e: e.activation(out=dtt[:, :], in_=dtp[:, :], func=AF.Ln, bias=onecol[:, 0:1]), reads=[("dtp",), ("onecol",)], writes=[("dtt",)])
                S.dve(lambda e: e.tensor_tensor(out=adt[:, :], in0=dtt[:, :], in1=arep[:, :], op=ALU.mult), reads=[("dtt",), ("arep",)], writes=[("adt",)])
                S.pe(lambda e: e.matmul(ps_a[:, 392:400], lhsT=triU, rhs=adt[:, :], start=True, stop=True), reads=[("adt",), ("cf",)], writes=[("ps_a", "cs")])
                S.pe(lambda e: e.matmul(ps_a[:, 400:408], lhsT=onesf[:, :], rhs=adt[:, :], start=True, stop=True), reads=[("adt",), ("onesf",)], writes=[("ps_a", "tot")])
                S.dve(lambda e: e.tensor_scalar(out=ncs[:, :], in0=ps_a[:, 392:400], scalar1=-1.0, scalar2=None, op0=ALU.mult),
                      reads=[("ps_a", "cs")], writes=[("ncs",)])
                S.dve(lambda e: e.tensor_tensor(out=wtt[:, :], in0=ps_a[:, 400:408], in1=ncs[:, :], op=ALU.add),
                      reads=[("ps_a", "tot"), ("ncs",)], writes=[("wtt",)])
                S.act(lambda e: e.activation(out=wtt[:, :], in_=wtt[:, :], func=AF.Exp), reads=[("wtt",)], writes=[("wtt",)])
                S.dve(lambda e: e.tensor_tensor(out=dtw[:, :], in0=dtt[:, :], in1=wtt[:, :], op=ALU.mult), reads=[("dtt",), ("wtt",)], writes=[("dtw",)])
                S.pe(lambda e, cs=cs: e.matmul(ps_a[:, 256:384], lhsT=BTb[:, cs], rhs=CTb[:, cs], start=True, stop=True),
                     reads=[("BTb",), ("CTb",)], writes=[("ps_a", "cb")])
                S.dve(lambda e: e.tensor_tensor(out=cbTf[:, :], in0=ps_a[:, 256:384], in1=triU, op=ALU.mult), reads=[("ps_a", "cb"), ("cf",)], writes=[("cbTf",)])
                for r in range(8):
                    half = slice((r % 2) * 64, (r % 2) * 64 + 64)
                    S.act(lambda e, r=r, half=half: e.activation(out=xdtp[:, r, half], in_=xtokf[:, r * 64:(r + 1) * 64], func=AF.Copy, scale=dtt[:, r:r + 1]),
                          reads=[("xtokf",), ("dtt",)], writes=[("xdtp", r)])
                    S.dve(lambda e, r=r: e.tensor_scalar(out=xdtw[:, r * 64:(r + 1) * 64], in0=xtokf[:, r * 64:(r + 1) * 64], scalar1=dtw[:, r:r + 1],
                                                        scalar2=None, op0=ALU.mult),
                          reads=[("xtokf",), ("dtw",)], writes=[("xdtw",)])
                S.pe(lambda e: e.matmul(ps_s[:, :], lhsT=Btokb[:, :], rhs=xdtw[:, :], start=True, stop=True),
                     reads=[("Btokb",), ("xdtw",)], writes=[("ps_s",)])
                S.act(lambda e: e.activation(out=stf[:, :], in_=ps_s[:, :], func=AF.Copy), reads=[("ps_s",)], writes=[("stf",)])
                for r in range(8):
                    hb = hq % 2
                    hq += 1
                    hc = r // 2
                    S.dve(lambda e, hb=hb, r=r: e.tensor_scalar(out=arp[hb][:, :], in0=onesf[:, :], scalar1=adt[:, r:r + 1], scalar2=None, op0=ALU.mult),
                          reads=[("onesf",), ("adt",)], writes=[("arp", hb)])
                    S.pe(lambda e, hb=hb: e.matmul(ps_a[:, 0:128], lhsT=arp[hb][:, :], rhs=triU, start=True, stop=True),
                         reads=[("arp", hb), ("cf",)], writes=[("ps_a", "A")])
                    S.act(lambda e, hb=hb: e.activation(out=Ef[hb][:, :], in_=ps_a[:, 0:128], func=AF.Exp), reads=[("ps_a", "A")], writes=[("Ef", hb)])
                    S.act(lambda e, hb=hb, r=r: e.activation(out=decf[hb][:, :], in_=ps_a[:, 0:128], func=AF.Identity),
                          reads=[("ps_a", "A")], writes=[("decf", hb)])
                    S.dve(lambda e, hb=hb, r=r: e.scalar_tensor_tensor(out=decf[hb][:, :], in0=decf[hb][:, :], scalar=ncs[:, r:r + 1], in1=triU,
                                                                      op0=ALU.add, op1=ALU.mult),
                          reads=[("decf", hb), ("ncs",), ("cf",)], writes=[("decf", hb)])
                    S.act(lambda e, hb=hb: e.activation(out=decf[hb][:, :], in_=decf[hb][:, :], func=AF.Exp),
                          reads=[("decf", hb)], writes=[("decf", hb)])
                    S.dve(lambda e, hb=hb: e.tensor_tensor(out=MT[hb][:, :], in0=decf[hb][:, :], in1=cbTf[:, :], op=ALU.mult),
                          reads=[("decf", hb), ("cbTf",)], writes=[("MT", hb)])
                    S.pool(lambda e, hb=hb, cs=cs: e.tensor_tensor(out=CTs[hb][:, :], in0=CTf[:, cs], in1=Ef[hb][:, :], op=ALU.mult),
                           reads=[("CTf",), ("Ef", hb)], writes=[("CTs", hb)])
                    S.pe(lambda e, hb=hb, r=r, hc=hc, cs=cs: e.matmul(ps_y[hc][:, cs], lhsT=xdtp[:, r, :], rhs=MT[hb][:, :], start=(r % 2 == 0), stop=False),
                         reads=[("xdtp", r), ("MT", hb)], writes=[("ps_y", hc)])
                    S.pe(lambda e, hb=hb, r=r, hc=hc, cs=cs: e.matmul(ps_y[hc][:, cs], lhsT=Sbp[:, r, :], rhs=CTs[hb][:, :], start=False, stop=(r % 2 == 1)),
                         reads=[("Sbp", r), ("CTs", hb)], writes=[("ps_y", hc)])
                    S.dve(lambda e, hb=hb, r=r: e.scalar_tensor_tensor(out=Sf[:, r * 64:(r + 1) * 64], in0=Sf[:, r * 64:(r + 1) * 64], scalar=Ef[hb][:, 127:128],
                                                                      in1=stf[:, r * 64:(r + 1) * 64], op0=ALU.mult, op1=ALU.add),
                          reads=[("Sf", r), ("Ef", hb), ("stf",)], writes=[("Sf", r)])
                    half = slice((r % 2) * 64, (r % 2) * 64 + 64)
                    S.act(lambda e, r=r, half=half: e.activation(out=Sbp[:, r, half], in_=Sf[:, r * 64:(r + 1) * 64], func=AF.Copy),
                          reads=[("Sf", r)], writes=[("Sbp", r)])
            for hc in range(4):
                S.dve(lambda e, hc=hc: e.tensor_scalar(out=yf[:, :], in0=xcf[:, hc, :], scalar1=dct[:, hc:hc + 1], scalar2=None, op0=ALU.mult),
                      reads=[("xcf", hc), ("prm",)], writes=[("yf",)])
                S.dve(lambda e, hc=hc: e.tensor_tensor(out=yf[:, :], in0=ps_y[hc][:, :], in1=yf[:, :], op=ALU.add),
                      reads=[("yf",), ("ps_y", hc)], writes=[("yf",)])
                S.dve(lambda e, hc=hc: e.tensor_tensor(out=yg[:, hc, :], in0=yf[:, :], in1=zs[:, hc, :], op=ALU.mult),
                      reads=[("yf",), ("zs", hc)], writes=[("yg", hc)])
                S.act(lambda e, hc=hc: e.activation(out=sqb[:, hc, :], in_=yg[:, hc, :], func=AF.Square), reads=[("yg", hc)], writes=[("sqb", hc)])
            for hc in range(4):
                S.pe(lambda e, hc=hc: e.matmul(ps_p[0][:, :], lhsT=onesb[:, :], rhs=sqb[:, hc, :], start=(hc == 0), stop=(hc == 3)),
                     reads=[("sqb", hc), ("onesb",)], writes=[("ps_p", 0)])
            S.dve(lambda e: e.tensor_scalar(out=rstd[:, :], in0=ps_p[0][:, :], scalar1=1.0 / 512, scalar2=LN_EPS, op0=ALU.mult, op1=ALU.add),
                  reads=[("ps_p", 0)], writes=[("rstd",)])
            S.act(lambda e: e.activation(out=rstd[:, :], in_=rstd[:, :], func=AF.Ln), reads=[("rstd",)], writes=[("rstd",)])
            S.act(lambda e: e.activation(out=rstd[:, :], in_=rstd[:, :], func=AF.Exp, scale=-0.5), reads=[("rstd",)], writes=[("rstd",)])
            for hc in range(4):
                yb = hc % 2
                S.dve(lambda e, hc=hc, yb=yb: e.tensor_tensor(out=ytmp[yb][:, :], in0=yg[:, hc, :], in1=rstd[:, :], op=ALU.mult),
                      reads=[("yg", hc), ("rstd",)], writes=[("ytmp", yb)])
                S.act(lambda e, hc=hc, yb=yb: e.activation(out=yo[yb][:, :], in_=ytmp[yb][:, :], func=AF.Copy, scale=nwt[:, hc:hc + 1]),
                      reads=[("ytmp", yb), ("prm",)], writes=[("yo", yb)])
                S.dma(lambda e, hc=hc, yb=yb, t0=t0: e.dma_start(out=YT[hc * 128:(hc + 1) * 128, t0:t0 + 512], in_=yo[yb][:, :]),
                      reads=[("yo", yb)], writes=[("YT", hc, tt)])
        S.emit()


def stage_outproj(S, OTin, wout, Hin, Hout, lng, lnb, T, Kdim, TP=1024):
    nc = S.nc
    S.new_stage()
    Kc = Kdim // 128
    npass = T // TP
    ntile = TP // 512
    with contextlib.ExitStack() as es:
        sb = lambda name, shape, dt: es.enter_context(nc.sbuf_tensor(name + S.sfx, shape, dt))
        x = sb("x", [128, 8, TP], F32)
        ob = sb("ob", [128, Kc, TP], BF16)
        wob = sb("wob", [128, Kc, 1024], BF16)
        stg = sb("stg", [128, 8, 256], F32)
        scr = dict(ybf=sb("ybf", [128, 8, 512], BF16), sqb=sb("sqb", [128, 8, 512], BF16),
                   mean=sb("mean", [128, 512], F32), rstd=sb("rstd", [128, 512], F32),
                   nmr=sb("nmr", [128, 512], F32), tmp=sb("lntmp", [128, 2, 512], F32))
        gam = sb("gam", [128, 8], F32)
        bet = sb("bet", [128, 8], F32)
        ones_bf = sb("ones_bf", [128, 128], BF16)
        ps_o = [es.enter_context(nc.psum_tensor("ps_o%d" % i + S.sfx, [128, 512], F32)) for i in range(2)]
        ps_s = es.enter_context(nc.psum_tensor("ps_s" + S.sfx, [128, 512], F32))
        ps_q = es.enter_context(nc.psum_tensor("ps_q" + S.sfx, [128, 512], F32))
        S.dma(lambda e: [e.dma_start(out=gam[:, :], in_=lng), e.dma_start(out=bet[:, :], in_=lnb)], writes=[("lnp",)], n=2)
        S.pool(lambda e: e.memset(ones_bf[:, :], 1.0), writes=[("ones",)])
        wv = wout.rearrange("(c p) d -> p c d", p=128)
        for k0 in range(0, Kc, 8):
            for c0 in range(0, 1024, 256):
                S.dma(lambda e, k0=k0, c0=c0: e.dma_start(out=stg[:, :, :], in_=wv[:, k0:k0 + 8, c0:c0 + 256]), writes=[("stg_shared",)])
                S.pool(lambda e, k0=k0, c0=c0: e.tensor_copy(out=wob[:, k0:k0 + 8, c0:c0 + 256], in_=stg[:, :, :]),
                       reads=[("stg_shared",)], writes=[("wob",)])
        Hin_v = Hin.rearrange("(c p) t -> p c t", p=128)
        Hout_v = Hout.rearrange("(c p) t -> p c t", p=128)
        O_v = OTin.rearrange("(c p) t -> p c t", p=128)
        oq = 0
        for p in range(npass):
            t0 = p * TP
            for c in range(8):
                S.dma(lambda e, c=c, t0=t0: e.dma_start(out=x[:, c, :], in_=Hin_v[:, c, t0:t0 + TP]),
                      writes=[("x", c, t) for t in range(ntile)])
            for c in range(Kc):
                S.dma(lambda e, c=c, t0=t0: e.dma_start(out=ob[:, c, :], in_=O_v[:, c, t0:t0 + TP]), writes=[("ob", c)])
            for d in range(8):
                for t in range(ntile):
                    q = oq % 2
                    oq += 1
                    sl = slice(t * 512, (t + 1) * 512)
                    for k in range(Kc):
                        S.pe(lambda e, q=q, k=k, d=d, sl=sl: e.matmul(ps_o[q][:, :], lhsT=wob[:, k, d * 128:(d + 1) * 128], rhs=ob[:, k, sl],
                                                                     start=(k == 0), stop=(k == Kc - 1)),
                             reads=[("wob",), ("ob", k)], writes=[("ps_o", q)])
                    S.dve(lambda e, q=q, d=d, sl=sl: e.scalar_tensor_tensor(out=x[:, d, sl], in0=x[:, d, sl], scalar=ALPHA, in1=ps_o[q][:, :],
                                                                           op0=ALU.mult, op1=ALU.add),
                          reads=[("ps_o", q), ("x", d, t)], writes=[("x", d, t)])
            ln_feature_major(S, x, "x", ntile, gam, bet, ones_bf, scr, ps_s, ps_q, "op")
            for c in range(8):
                S.dma(lambda e, c=c, t0=t0: e.dma_start(out=Hout_v[:, c, t0:t0 + TP], in_=x[:, c, :]),
                      reads=[("x", c, t) for t in range(ntile)], writes=[("Hout", p, c)])
        S.emit()


T_CORE = 2048
NCORES = 8


def _new_nc():
    return bass.Bass("TRN2", target_bir_lowering=False)


def _din(nc, name, shape, dt=F32):
    return nc.dram_tensor(name, list(shape), dt, kind="ExternalInput").ap()


def _dout(nc, name, shape, dt=F32):
    return nc.dram_tensor(name, list(shape), dt, kind="ExternalOutput").ap()


def _ffn_inputs(nc, tag):
    return dict(wg=_din(nc, "wg" + tag, [D, DFF]), wu=_din(nc, "wu" + tag, [D, DFF]), wd=_din(nc, "wd" + tag, [DFF, D]),
                lng=_din(nc, "lng" + tag, [128, 8]), lnb=_din(nc, "lnb" + tag, [128, 8]))


def build_ffn_prog():
    nc = _new_nc()
    Hin = _din(nc, "Hin", [D, T_CORE])
    f = _ffn_inputs(nc, "0")
    Hout = _dout(nc, "Hout", [D, T_CORE])
    with contextlib.ExitStack() as es:
        S = Sched(nc)
        S.setup(es)
        stage_ffn(S, Hin, Hout, f["wg"], f["wu"], f["wd"], f["lng"], f["lnb"], T_CORE)
    return nc


def build_post_prog(Kdim, n_ffn):
    nc = _new_nc()
    Hin = _din(nc, "Hin", [D, T_CORE])
    OTin = _din(nc, "OTin", [Kdim, T_CORE], BF16)
    wout = _din(nc, "wout", [Kdim, D])
    lng = _din(nc, "lngm", [128, 8])
    lnb = _din(nc, "lnbm", [128, 8])
    fs = [_ffn_inputs(nc, str(i)) for i in range(n_ffn)]
    Hout = _dout(nc, "Hout", [D, T_CORE])
    scratch = [nc.dram_tensor("hscr%d" % i, [D, T_CORE], F32).ap() for i in range(n_ffn)]
    with contextlib.ExitStack() as es:
        S = Sched(nc)
        S.setup(es)
        cur = scratch[0] if n_ffn > 0 else Hout
        stage_outproj(S, OTin, wout, Hin, cur, lng, lnb, T_CORE, Kdim)
        for i in range(n_ffn):
            nxt = Hout if i == n_ffn - 1 else scratch[i + 1]
            f = fs[i]
            stage_ffn(S, cur, nxt, f["wg"], f["wu"], f["wd"], f["lng"], f["lnb"], T_CORE)
            cur = nxt
    return nc


def build_attn_prog(kind):
    nc = _new_nc()
    Hfull = _din(nc, "Hfull", [D, SEQ])
    wq = _din(nc, "wq", [D, 256])
    wk = _din(nc, "wk", [D, 256])
    wv = _din(nc, "wv", [D, 256])
    wf = _din(nc, "wf", [D, 4]) if kind == "fox" else None
    bfr = _din(nc, "bfr", [128, 4]) if kind == "fox" else None
    cst = _din(nc, "cst", [128, 384])
    oh = _din(nc, "oh", [32, 32 * 128]) if kind == "moba" else None
    OT = _dout(nc, "OT", [256, SEQ], BF16)
    with contextlib.ExitStack() as es:
        S = Sched(nc)
        S.setup(es)
        stage_attn(S, kind, Hfull, wq, wk, wv, wf, bfr, cst, oh, OT)
    return nc


def build_ssd_prog():
    nc = _new_nc()
    Hfull = _din(nc, "Hfull", [D, SEQ])
    wz = _din(nc, "wz", [D, 512])
    wx = _din(nc, "wx", [D, 512])
    wB = _din(nc, "wB", [D, 128])
    wC = _din(nc, "wC", [D, 128])
    wdt = _din(nc, "wdt", [D, 8])
    cw = _din(nc, "cw", [128, 6, 4])
    cbias = _din(nc, "cbias", [128, 6])
    dtb = _din(nc, "dtb", [128, 8])
    alog = _din(nc, "alog", [128, 8])
    dcol = _din(nc, "dcol", [128, 4])
    nw = _din(nc, "nw", [128, 4])
    cst = _din(nc, "cst", [128, 384])
    YT = _dout(nc, "OT", [512, SEQ], BF16)
    with contextlib.ExitStack() as es:
        S = Sched(nc)
        S.setup(es)
        stage_ssd(S, Hfull, wz, wx, wB, wC, wdt, cw, cbias, dtb, alog, dcol, nw, cst, YT)
    return nc


def _consts():
    ident = np.eye(128, dtype=np.float32)
    triU = np.triu(np.ones((128, 128), np.float32))
    trim = np.where(np.arange(128)[:, None] > np.arange(128)[None, :], NEG, 0.0).astype(np.float32)
    cst = np.ascontiguousarray(np.concatenate([ident, triU, trim], axis=1))
    oh = np.zeros((32, 32, 128), np.float32)
    for n in range(32):
        oh[n, n, :] = 1.0
    return cst, oh.reshape(32, 32 * 128)


def _c(a):
    return np.ascontiguousarray(a, dtype=np.float32)


def _pc(v):
    return _c(np.asarray(v).reshape(8, 128).T)


def _ffn_map(inp, layer, half, tag):
    return {"wg" + tag: _c(inp["ffn_w_gate"][layer, half]), "wu" + tag: _c(inp["ffn_w_up"][layer, half]),
            "wd" + tag: _c(inp["ffn_w_down"][layer, half]),
            "lng" + tag: _pc(inp["ln_g"][layer, 2 * half]), "lnb" + tag: _pc(inp["ln_b"][layer, 2 * half])}


def _ssd_maps(inp, j, g, cst):
    w_in = inp["ssm_w_in"][j]
    DI = 2048
    cwf = inp["ssm_conv_w"][j]
    cbf = inp["ssm_conv_b"][j]
    chans = np.concatenate([np.arange(g * 512, (g + 1) * 512), np.arange(DI + g * 128, DI + (g + 1) * 128),
                            np.arange(DI + 512 + g * 128, DI + 512 + (g + 1) * 128)])
    cw = cwf[:, chans].reshape(4, 6, 128).transpose(2, 1, 0)
    cb = cbf[chans].reshape(6, 128).T
    hs = slice(g * 8, (g + 1) * 8)
    rep = lambda v: np.broadcast_to(np.asarray(v)[None, :], (128, len(v)))
    dcol = np.repeat(inp["ssm_d"][j][hs], 64).reshape(4, 128).T
    nw = inp["ssm_norm_w"][j][g * 512:(g + 1) * 512].reshape(4, 128).T
    m = dict(wz=w_in[:, g * 512:(g + 1) * 512], wx=w_in[:, DI + g * 512:DI + (g + 1) * 512],
             wB=w_in[:, 2 * DI + g * 128:2 * DI + (g + 1) * 128], wC=w_in[:, 2 * DI + 512 + g * 128:2 * DI + 512 + (g + 1) * 128],
             wdt=w_in[:, 2 * DI + 1024 + g * 8:2 * DI + 1024 + (g + 1) * 8],
             cw=cw, cbias=cb, dtb=rep(inp["ssm_dt_bias"][j][hs]), alog=rep(inp["ssm_a_log"][j][hs]), dcol=dcol, nw=nw, cst=cst)
    return {k: _c(v) for k, v in m.items()}


def _attn_maps(inp, kind, g, cst, oh):
    w_in = inp["fox_w_in"][0] if kind == "fox" else inp["moba_w_in"][0]
    m = dict(wq=_c(w_in[:, g * 256:(g + 1) * 256]), wk=_c(w_in[:, 1024 + g * 256:1024 + (g + 1) * 256]),
             wv=_c(w_in[:, 2048 + g * 256:2048 + (g + 1) * 256]), cst=cst)
    if kind == "fox":
        m["wf"] = _c(w_in[:, 3072 + g * 4:3072 + (g + 1) * 4])
        m["bfr"] = _c(np.broadcast_to(inp["fox_b_f"][0][g * 4:(g + 1) * 4][None, :], (128, 4)))
    else:
        m["oh"] = oh
    return m


def _run(nc, in_maps):
    res = run_bass_kernel_spmd(nc, in_maps, core_ids=list(range(NCORES)))
    return res.results


_DBG = None


def kernel(**inputs):
    inp = {k: np.asarray(v) for k, v in inputs.items()}
    x = inp["x"]
    cst, oh = _consts()
    progs = {}

    def prog(key, builder):
        if key not in progs:
            progs[key] = builder()
        return progs[key]

    H = [_c(x[c // 4, (c % 4) * T_CORE:(c % 4 + 1) * T_CORE].T) for c in range(NCORES)]
    maps = []
    for c in range(NCORES):
        m = {"Hin": H[c]}
        m.update(_ffn_map(inp, 0, 0, "0"))
        maps.append(m)
    r = _run(prog("ffn", build_ffn_prog), maps)
    H = [r[c]["Hout"] for c in range(NCORES)]
    if _DBG is not None:
        _DBG("A", H)
    for layer in range(4):
        kindi, j = layer % 3, layer // 3
        kind = ("ssd", "fox", "moba")[kindi]
        Hfull = [_c(np.concatenate([H[b * 4 + q] for q in range(4)], axis=1)) for b in range(2)]
        maps = []
        for c in range(NCORES):
            b, g = c // 4, c % 4
            m = _ssd_maps(inp, j, g, cst) if kind == "ssd" else _attn_maps(inp, kind, g, cst, oh)
            m["Hfull"] = Hfull[b]
            maps.append(m)
        r = _run(prog(kind, build_ssd_prog if kind == "ssd" else (lambda kind=kind: build_attn_prog(kind))), maps)
        Kdim = 2048 if kind == "ssd" else 1024
        Ofull = [np.concatenate([r[b * 4 + g]["OT"] for g in range(4)], axis=0) for b in range(2)]
        w_out = {"ssd": inp["ssm_w_out"], "fox": inp["fox_w_out"], "moba": inp["moba_w_out"]}[kind][j]
        if _DBG is not None:
            _DBG(("mix", layer), (Ofull, w_out))
        n_ffn = 2 if layer < 3 else 1
        maps = []
        for c in range(NCORES):
            b, q = c // 4, c % 4
            m = {"Hin": H[c], "OTin": np.ascontiguousarray(Ofull[b][:, q * T_CORE:(q + 1) * T_CORE]), "wout": _c(w_out),
                 "lngm": _pc(inp["ln_g"][layer, 1]), "lnbm": _pc(inp["ln_b"][layer, 1])}
            m.update(_ffn_map(inp, layer, 1, "0"))
            if n_ffn == 2:
                m.update(_ffn_map(inp, layer + 1, 0, "1"))
            maps.append(m)
        r = _run(prog(("post", Kdim, n_ffn), lambda Kdim=Kdim, n_ffn=n_ffn: build_post_prog(Kdim, n_ffn)), maps)
        H = [r[c]["Hout"] for c in range(NCORES)]
        if _DBG is not None:
            _DBG(("post", layer), H)
    out = np.empty((2, SEQ, D), np.float32)
    for c in range(NCORES):
        out[c // 4, (c % 4) * T_CORE:(c % 4 + 1) * T_CORE] = H[c].T
    return out
```

```python
import contextlib
import numpy as np
import concourse.bass as bass
import concourse.mybir as mybir
from concourse.bass_utils import run_bass_kernel_spmd

F32 = mybir.dt.float32
BF16 = mybir.dt.bfloat16
AF = mybir.ActivationFunctionType
ALU = mybir.AluOpType
AX = mybir.AxisListType

COMPUTE = ("pe", "act", "dve", "pool")
N_DMA_SEMS = 12


class _Op:
    __slots__ = ("eng", "fn", "deps", "is_dma", "ndma", "signal", "sem", "val", "prev")

    def __init__(self, eng, fn, is_dma, ndma):
        self.eng = eng
        self.fn = fn
        self.deps = set()
        self.is_dma = is_dma
        self.ndma = ndma
        self.signal = False
        self.sem = None
        self.val = None


class Sched:
    def __init__(self, nc):
        self.nc = nc
        self.ops = []
        self.last_w = {}
        self.readers = {}
        self.nstage = 0
        self.sfx = ""

    def new_stage(self):
        self.nstage += 1
        self.sfx = "_s%d" % self.nstage

    def op(self, eng, fn, reads=(), writes=(), dma=0):
        def _isps(k):
            return isinstance(k[0], str) and k[0].startswith("ps_")

        def _norm(k):
            return tuple(x for i, x in enumerate(k) if i == 0 or not isinstance(x, str)) if _isps(k) else k
        writes = [_norm(k) for k in writes] + [_norm(k) for k in reads if _isps(k)]
        reads = [k for k in reads if not _isps(k)]
        o = _Op(eng, fn, dma > 0, dma)
        idx = len(self.ops)
        for k in reads:
            w = self.last_w.get(k)
            if w is not None:
                o.deps.add(w)
        for k in writes:
            w = self.last_w.get(k)
            if w is not None:
                o.deps.add(w)
            for r in self.readers.get(k, ()):
                o.deps.add(r)
        for k in reads:
            self.readers.setdefault(k, []).append(idx)
        for k in writes:
            self.last_w[k] = idx
            self.readers[k] = []
        o.deps.discard(idx)
        self.ops.append(o)
        return idx

    def pe(self, fn, reads=(), writes=()):
        return self.op("pe", fn, reads, writes)

    def act(self, fn, reads=(), writes=()):
        return self.op("act", fn, reads, writes)

    def dve(self, fn, reads=(), writes=()):
        return self.op("dve", fn, reads, writes)

    def pool(self, fn, reads=(), writes=()):
        return self.op("pool", fn, reads, writes)

    def dma(self, fn, reads=(), writes=(), n=1, q="sp"):
        return self.op(q, fn, reads, writes, dma=n)

    def setup(self, es):
        nc = self.nc
        self.sems = {e: es.enter_context(nc.semaphore("s_" + e)) for e in COMPUTE}
        self.dsems = [es.enter_context(nc.semaphore("d%d" % i)) for i in range(N_DMA_SEMS)]
        self.cnt = {e: 0 for e in COMPUTE}
        self.dtot = [0] * N_DMA_SEMS
        self.rr = 0

    def emit(self):
        nc = self.nc
        ops = self.ops
        for o in ops:
            if o.eng == "pe":
                o.deps = {d for d in o.deps if not (ops[d].eng == "pe" and not ops[d].is_dma)}
        for o in ops:
            for d in o.deps:
                ops[d].signal = True
        engs = {"pe": nc.tensor, "act": nc.scalar, "dve": nc.vector, "pool": nc.gpsimd, "sp": nc.sync}
        by_eng = {e: [] for e in engs}
        for i, o in enumerate(ops):
            by_eng[o.eng].append(i)
        for e in COMPUTE:
            comp = [i for i in by_eng[e] if not ops[i].is_dma]
            if comp:
                ops[comp[-1]].signal = True
        sems, dsems, cnt, dtot = self.sems, self.dsems, self.cnt, self.dtot
        for o in ops:
            if o.is_dma:
                si = self.rr % N_DMA_SEMS
                self.rr += 1
                o.sem = ("d", si)
                o.prev = dtot[si]
                dtot[si] += 16 * o.ndma
                o.val = dtot[si]
            elif o.signal:
                cnt[o.eng] += 1
                o.sem = ("c", o.eng)
                o.val = cnt[o.eng]
        final = [(("c", e), cnt[e]) for e in COMPUTE] + [(("d", i), dtot[i]) for i in range(N_DMA_SEMS)]

        def semobj(s):
            return dsems[s[1]] if s[0] == "d" else sems[s[1]]

        def run_engine(ename, eng):
            waited = {}

            def wait(s, v):
                if v <= 0:
                    return
                if waited.get(s, 0) >= v:
                    return
                eng.wait_ge(semobj(s), v)
                waited[s] = v
            for i in by_eng[ename]:
                o = ops[i]
                for d in sorted(o.deps):
                    po = ops[d]
                    wait(po.sem, po.val)
                if o.is_dma:
                    wait(o.sem, o.prev)
                    insts = o.fn(eng)
                    if not isinstance(insts, (list, tuple)):
                        insts = [insts]
                    assert len(insts) == o.ndma, (len(insts), o.ndma)
                    for ins in insts:
                        ins.then_inc(semobj(o.sem), 16)
                else:
                    ins = o.fn(eng)
                    if o.signal:
                        ins.then_inc(semobj(o.sem), 1)
            for (s_, v_) in final:
                wait(s_, v_)

        with nc.Block() as block:
            @block.tensor
            def _(e):
                run_engine("pe", e)

            @block.scalar
            def _(e):
                run_engine("act", e)

            @block.vector
            def _(e):
                run_engine("dve", e)

            @block.gpsimd
            def _(e):
                run_engine("pool", e)

            @block.sync
            def _(e):
                run_engine("sp", e)
        self.ops = []
        self.last_w = {}
        self.readers = {}


D = 1024
DFF = 2816
NF = DFF // 128
LN_EPS = 1e-5
ALPHA = 8.0 ** 0.25


def ln_feature_major(S, x, keyx, ntile, gam, bet, ones_bf, scr, ps_s, ps_q, tag, out_bf=None, key_bf=None):
    ybf, sqb, mean, rstd, nmr, tmp = scr["ybf"], scr["sqb"], scr["mean"], scr["rstd"], scr["nmr"], scr["tmp"]
    for t in range(ntile):
        sl = slice(t * 512, (t + 1) * 512)
        for d in range(8):
            S.act(lambda e, d=d, sl=sl: e.activation(out=ybf[:, d, :], in_=x[:, d, sl], func=AF.Copy),
                  reads=[(keyx, d, t)], writes=[("ybf", d)])
            S.act(lambda e, d=d, sl=sl: e.activation(out=sqb[:, d, :], in_=x[:, d, sl], func=AF.Square),
                  reads=[(keyx, d, t)], writes=[("sqb", d)])
        for d in range(8):
            S.pe(lambda e, d=d: e.matmul(ps_s[:, :], lhsT=ones_bf[:, :], rhs=ybf[:, d, :], start=(d == 0), stop=(d == 7)),
                 reads=[("ybf", d), ("ones",)], writes=[("ps_s",)])
        for d in range(8):
            S.pe(lambda e, d=d: e.matmul(ps_q[:, :], lhsT=ones_bf[:, :], rhs=sqb[:, d, :], start=(d == 0), stop=(d == 7)),
                 reads=[("sqb", d), ("ones",)], writes=[("ps_q",)])
        S.dve(lambda e: e.tensor_scalar(out=mean[:, :], in0=ps_s[:, :], scalar1=1.0 / D, scalar2=None, op0=ALU.mult),
              reads=[("ps_s",)], writes=[("mean",)])
        S.dve(lambda e: e.tensor_tensor(out=nmr[:, :], in0=mean[:, :], in1=mean[:, :], op=ALU.mult),
              reads=[("mean",)], writes=[("nmr",)])
        S.dve(lambda e: e.scalar_tensor_tensor(out=rstd[:, :], in0=ps_q[:, :], scalar=1.0 / D, in1=nmr[:, :],
                                               op0=ALU.mult, op1=ALU.subtract),
              reads=[("ps_q",), ("nmr",)], writes=[("rstd",)])
        S.dve(lambda e: e.tensor_scalar(out=rstd[:, :], in0=rstd[:, :], scalar1=LN_EPS, scalar2=None, op0=ALU.add),
              reads=[("rstd",)], writes=[("rstd",)])
        S.act(lambda e: e.activation(out=rstd[:, :], in_=rstd[:, :], func=AF.Ln),
              reads=[("rstd",)], writes=[("rstd",)])
        S.act(lambda e: e.activation(out=rstd[:, :], in_=rstd[:, :], func=AF.Exp, scale=-0.5),
              reads=[("rstd",)], writes=[("rstd",)])
        S.dve(lambda e: e.scalar_tensor_tensor(out=nmr[:, :], in0=mean[:, :], scalar=-1.0, in1=rstd[:, :],
                                               op0=ALU.mult, op1=ALU.mult),
              reads=[("mean",), ("rstd",)], writes=[("nmr",)])
        for d in range(8):
            S.dve(lambda e, d=d, sl=sl: e.tensor_tensor(out=tmp[:, d % 2, :], in0=x[:, d, sl], in1=rstd[:, :], op=ALU.mult),
                  reads=[(keyx, d, t), ("rstd",)], writes=[("lntmp", d % 2)])
            S.pool(lambda e, d=d: e.tensor_tensor(out=tmp[:, d % 2, :], in0=tmp[:, d % 2, :], in1=nmr[:, :], op=ALU.add),
                   reads=[("lntmp", d % 2), ("nmr",)], writes=[("lntmp", d % 2)])
            S.act(lambda e, d=d, sl=sl: e.activation(out=x[:, d, sl], in_=tmp[:, d % 2, :], func=AF.Identity,
                                                     scale=gam[:, d:d + 1], bias=bet[:, d:d + 1]),
                  reads=[("lntmp", d % 2), ("lnp",)], writes=[(keyx, d, t)])
            if out_bf is not None:
                S.act(lambda e, d=d, sl=sl: e.activation(out=out_bf[:, d, sl], in_=tmp[:, d % 2, :], func=AF.Identity,
                                                         scale=gam[:, d:d + 1], bias=bet[:, d:d + 1]),
                      reads=[("lntmp", d % 2), ("lnp",)], writes=[(key_bf, d, t)])


def stage_ffn(S, Hin, Hout, wg, wu, wd, lng, lnb, T, TP=1024):
    nc = S.nc
    S.new_stage()
    npass = T // TP
    ntile = TP // 512
    with contextlib.ExitStack() as es:
        sb = lambda name, shape, dt: es.enter_context(nc.sbuf_tensor(name + S.sfx, shape, dt))
        x = sb("x", [128, 8, TP], F32)
        xbf = sb("xbf", [128, 8, TP], BF16)
        hmid = sb("hmid", [128, NF, TP], BF16)
        stg = [sb("stg%d" % i, [128, 8, 256], F32) for i in range(2)]
        wgb = [sb("wgb%d" % i, [128, 8, 256], BF16) for i in range(2)]
        wub = [sb("wub%d" % i, [128, 8, 256], BF16) for i in range(2)]
        wdb = [sb("wdb%d" % i, [128, NF, 256], BF16) for i in range(2)]
        sg = [sb("sg%d" % i, [128, 512], F32) for i in range(2)]
        otmp = [sb("otmp%d" % i, [128, 512], F32) for i in range(2)]
        scr = dict(ybf=sb("ybf", [128, 8, 512], BF16), sqb=sb("sqb", [128, 8, 512], BF16),
                   mean=sb("mean", [128, 512], F32), rstd=sb("rstd", [128, 512], F32),
                   nmr=sb("nmr", [128, 512], F32), tmp=sb("lntmp", [128, 2, 512], F32))
        gam = sb("gam", [128, 8], F32)
        bet = sb("bet", [128, 8], F32)
        ones_bf = sb("ones_bf", [128, 128], BF16)
        ps_g = [es.enter_context(nc.psum_tensor("ps_g%d" % i + S.sfx, [128, 512], F32)) for i in range(2)]
        ps_u = [es.enter_context(nc.psum_tensor("ps_u%d" % i + S.sfx, [128, 512], F32)) for i in range(2)]
        ps_o = [es.enter_context(nc.psum_tensor("ps_o%d" % i + S.sfx, [128, 512], F32)) for i in range(2)]
        ps_s = es.enter_context(nc.psum_tensor("ps_s" + S.sfx, [128, 512], F32))
        ps_q = es.enter_context(nc.psum_tensor("ps_q" + S.sfx, [128, 512], F32))

        S.dma(lambda e: [e.dma_start(out=gam[:, :], in_=lng),
                         e.dma_start(out=bet[:, :], in_=lnb)],
              writes=[("lnp",)], n=2)
        S.pool(lambda e: e.memset(ones_bf[:, :], 1.0), writes=[("ones",)])
        Hin_v = Hin.rearrange("(c p) t -> p c t", p=128)
        Hout_v = Hout.rearrange("(c p) t -> p c t", p=128)
        wg_v = wg.rearrange("(c p) f -> p c f", p=128)
        wu_v = wu.rearrange("(c p) f -> p c f", p=128)
        wd_v = wd.rearrange("(c p) d -> p c d", p=128)
        stg_i = 0
        wi = 0
        gq = 0
        for p in range(npass):
            t0 = p * TP
            for c in range(8):
                S.dma(lambda e, c=c, t0=t0: e.dma_start(out=x[:, c, :], in_=Hin_v[:, c, t0:t0 + TP]),
                      writes=[("x", c, t) for t in range(ntile)])
                for t in range(ntile):
                    S.act(lambda e, c=c, t=t: e.activation(out=xbf[:, c, t * 512:(t + 1) * 512], in_=x[:, c, t * 512:(t + 1) * 512], func=AF.Copy),
                          reads=[("x", c, t)], writes=[("xbf", c, t)])
            for fg in range(NF // 2):
                f0 = fg * 256
                b = wi % 2
                wi += 1
                for (wv, wb, nm) in ((wg_v, wgb, "wgb"), (wu_v, wub, "wub")):
                    s = stg_i % 2
                    stg_i += 1
                    S.dma(lambda e, wv=wv, s=s, f0=f0: e.dma_start(out=stg[s][:, :, :], in_=wv[:, :, f0:f0 + 256]),
                          writes=[("stg", s)])
                    S.pool(lambda e, wb=wb, b=b, s=s: e.tensor_copy(out=wb[b][:, :, :], in_=stg[s][:, :, :]),
                           reads=[("stg", s)], writes=[(nm, b)])
                for fc in range(2):
                    f = fg * 2 + fc
                    for t in range(ntile):
                        q = gq % 2
                        gq += 1
                        sl = slice(t * 512, (t + 1) * 512)
                        for k in range(8):
                            S.pe(lambda e, q=q, b=b, k=k, fc=fc, sl=sl: e.matmul(
                                ps_g[q][:, :], lhsT=wgb[b][:, k, fc * 128:(fc + 1) * 128], rhs=xbf[:, k, sl],
                                start=(k == 0), stop=(k == 7)),
                                reads=[("wgb", b), ("xbf", k, t)], writes=[("ps_g", q)])
                        for k in range(8):
                            S.pe(lambda e, q=q, b=b, k=k, fc=fc, sl=sl: e.matmul(
                                ps_u[q][:, :], lhsT=wub[b][:, k, fc * 128:(fc + 1) * 128], rhs=xbf[:, k, sl],
                                start=(k == 0), stop=(k == 7)),
                                reads=[("wub", b), ("xbf", k, t)], writes=[("ps_u", q)])
                        S.act(lambda e, q=q: e.activation(out=sg[q][:, :], in_=ps_g[q][:, :], func=AF.Silu),
                              reads=[("ps_g", q)], writes=[("sg", q)])
                        S.dve(lambda e, q=q, f=f, sl=sl: e.tensor_tensor(out=hmid[:, f, sl], in0=sg[q][:, :], in1=ps_u[q][:, :], op=ALU.mult),
                              reads=[("sg", q), ("ps_u", q)], writes=[("hmid", f, t)])
            oq = 0
            for dg in range(4):
                d0 = dg * 256
                b = dg % 2
                for (c0, nch) in ((0, 8), (8, 8), (16, 6)):
                    s = stg_i % 2
                    stg_i += 1
                    S.dma(lambda e, s=s, c0=c0, nch=nch, d0=d0: e.dma_start(out=stg[s][:, 0:nch, :], in_=wd_v[:, c0:c0 + nch, d0:d0 + 256]),
                          writes=[("stg", s)])
                    S.pool(lambda e, b=b, s=s, c0=c0, nch=nch: e.tensor_copy(out=wdb[b][:, c0:c0 + nch, :], in_=stg[s][:, 0:nch, :]),
                           reads=[("stg", s)], writes=[("wdb", b, c0)])
                for dc in range(2):
                    d = dg * 2 + dc
                    for t in range(ntile):
                        q = oq % 2
                        oq += 1
                        sl = slice(t * 512, (t + 1) * 512)
                        for f in range(NF):
                            S.pe(lambda e, q=q, b=b, f=f, dc=dc, sl=sl: e.matmul(
                                ps_o[q][:, :], lhsT=wdb[b][:, f, dc * 128:(dc + 1) * 128], rhs=hmid[:, f, sl],
                                start=(f == 0), stop=(f == NF - 1)),
                                reads=[("wdb", b, (f // 8) * 8), ("hmid", f, t)], writes=[("ps_o", q)])
                        S.act(lambda e, q=q: e.activation(out=otmp[q][:, :], in_=ps_o[q][:, :], func=AF.Copy, scale=0.5),
                              reads=[("ps_o", q)], writes=[("otmp", q)])
                        S.dve(lambda e, q=q, d=d, sl=sl: e.scalar_tensor_tensor(out=x[:, d, sl], in0=x[:, d, sl], scalar=ALPHA, in1=otmp[q][:, :],
                                                                                op0=ALU.mult, op1=ALU.add),
                              reads=[("otmp", q), ("x", d, t)], writes=[("x", d, t)])
            ln_feature_major(S, x, "x", ntile, gam, bet, ones_bf, scr, ps_s, ps_q, "ffn")
            for c in range(8):
                S.dma(lambda e, c=c, t0=t0: e.dma_start(out=Hout_v[:, c, t0:t0 + TP], in_=x[:, c, :]),
                      reads=[("x", c, t) for t in range(ntile)], writes=[("Hout", p, c)])
        S.emit()


SEQ = 8192
NBLK = SEQ // 128
NEG = -30000.0


def load_cast_weight(S, w_dram_view, dst_bf, stg, key, ncol, nrowchunks=8):
    for c0 in range(0, ncol, 256):
        w = min(256, ncol - c0)
        S.dma(lambda e, c0=c0, w=w: e.dma_start(out=stg[:, 0:nrowchunks, 0:w], in_=w_dram_view[:, :, c0:c0 + w]),
              writes=[("stg_shared",)])
        S.pool(lambda e, c0=c0, w=w: e.tensor_copy(out=dst_bf[:, :, c0:c0 + w], in_=stg[:, 0:nrowchunks, 0:w]),
               reads=[("stg_shared",)], writes=[(key,)])


def stage_attn(S, kind, Hfull, wq, wk, wv, wf, bfr, cst, oh, OT, dbg=None):
    nc = S.nc
    S.new_stage()
    fox = kind == "fox"
    with contextlib.ExitStack() as es:
        sb = lambda name, shape, dt: es.enter_context(nc.sbuf_tensor(name + S.sfx, shape, dt))
        pst = lambda name, shape, dt=F32: es.enter_context(nc.psum_tensor(name + S.sfx, shape, dt))
        QT = sb("QT", [128, 2, SEQ], BF16)
        KT = sb("KT", [128, 2, SEQ], BF16)
        V = sb("V", [128, NBLK, 4, 66], BF16)
        xt = [sb("xt%d" % i, [128, 8, 512], F32) for i in range(2)]
        xb = [sb("xb%d" % i, [128, 8, 512], BF16) for i in range(2)]
        stg = sb("stg", [128, 8, 256], F32)
        wqb = sb("wqb", [128, 8, 256], BF16)
        wkb = sb("wkb", [128, 8, 256], BF16)
        wvb = sb("wvb", [128, 8, 256], BF16)
        cf = sb("cf", [128, 384], F32)
        identb = sb("identb", [128, 128], BF16)
        trimb = sb("trimb", [128, 128], BF16)
        onesf = sb("onesf", [128, 128], F32)
        onecol = sb("onecol", [128, 1], F32)
        NSB = 4
        Pt = [sb("Pt%d" % i, [128, 512], BF16) for i in range(NSB)]
        rc = sb("rc", [128, 512], F32)
        rb = sb("rb", [128, 512], F32)
        ot = [sb("ot%d" % i, [128, 512], BF16) for i in range(2)]
        ps_p = [pst("ps_p%d" % i, [128, 512]) for i in range(2)]
        ps_S = [pst("ps_S%d" % i, [128, 512]) for i in range(2)]
        ps_O = [pst("ps_O%d" % i, [128, 512]) for i in range(2)]
        ps_x = pst("ps_x", [128, 512])
        ps_y = pst("ps_y", [128, 512])
        if fox:
            wfb = sb("wfb", [128, 8, 4], BF16)
            wfs = sb("wfs", [128, 8, 4], F32)
            bft = sb("bft", [128, 4], F32)
            lf = sb("lf", [128, NBLK, 4], F32)
            hsA = sb("hsA", [128, NBLK, 4], F32)
            hsB = sb("hsB", [128, NBLK, 4], F32)
            tot = sb("tot", [128, NBLK, 4], F32)
            cum = sb("cum", [128, NBLK, 4], F32)
            ncum = sb("ncum", [128, NBLK, 4], F32)
            dg = [sb("dg%d" % i, [128, 128], F32) for i in range(2)]
            cqb = [sb("cqb%d" % i, [128, 512], F32) for i in range(2)]
            Sb = [sb("Sb%d" % i, [128, 512], F32) for i in range(NSB)]
        else:
            ohs = sb("ohs", [32, 32 * 128], F32)
            ohb = sb("ohb", [32, 32, 128], BF16)
            id30 = sb("id30", [128, 128], BF16)
            kmf = sb("kmf", [128, 2, 32], F32)
            kmb = sb("kmb", [128, 2, 32], BF16)
            G = sb("G", [128, 32], F32)
            mx = sb("mx", [128, 8], F32)
            selm = sb("selm", [128, 4, 32], BF16)
            selbT = [sb("selbT%d" % i, [32, 512], BF16) for i in range(2)]

        ident = cf[:, 0:128]
        triU = cf[:, 128:256]
        S.dma(lambda e: e.dma_start(out=cf[:, :], in_=cst), writes=[("cf",)])
        S.pool(lambda e: e.tensor_copy(out=identb[:, :], in_=cf[:, 0:128]), reads=[("cf",)], writes=[("identb",)])
        S.pool(lambda e: e.tensor_copy(out=trimb[:, :], in_=cf[:, 256:384]), reads=[("cf",)], writes=[("trimb",)])
        S.pool(lambda e: e.memset(onesf[:, :], 1.0), writes=[("onesf",)])
        S.pool(lambda e: e.memset(onecol[:, :], 1.0), writes=[("onecol",)])
        S.pool(lambda e: e.memset(V[:, :, :, 64:66], 1.0), writes=[("Vones",)])
        Hv = Hfull.rearrange("(c p) t -> p c t", p=128)
        load_cast_weight(S, wq.rearrange("(c p) f -> p c f", p=128), wqb, stg, "wqb", 256)
        load_cast_weight(S, wk.rearrange("(c p) f -> p c f", p=128), wkb, stg, "wkb", 256)
        load_cast_weight(S, wv.rearrange("(c p) f -> p c f", p=128), wvb, stg, "wvb", 256)
        if fox:
            S.dma(lambda e: [e.dma_start(out=wfs[:, :, :], in_=wf.rearrange("(c p) f -> p c f", p=128)),
                             e.dma_start(out=bft[:, :], in_=bfr)], writes=[("wfs",), ("bft",)], n=2)
            S.pool(lambda e: e.tensor_copy(out=wfb[:, :, :], in_=wfs[:, :, :]), reads=[("wfs",)], writes=[("wfb",)])
        else:
            S.dma(lambda e: e.dma_start(out=ohs[:, :], in_=oh), writes=[("ohs",)])
            S.pool(lambda e: e.tensor_copy(out=ohb[:, :, :], in_=ohs[:, :].rearrange("p (n m) -> p n m", m=128)),
                   reads=[("ohs",)], writes=[("ohb",)])
            S.pool(lambda e: e.tensor_scalar(out=id30[:, :], in0=cf[:, 0:128], scalar1=-NEG, scalar2=None, op0=ALU.mult),
                   reads=[("cf",)], writes=[("id30",)])

        pq = 0
        for tt in range(SEQ // 512):
            t0 = tt * 512
            b = tt % 2
            S.dma(lambda e, b=b, t0=t0: e.dma_start(out=xt[b][:, :, :], in_=Hv[:, :, t0:t0 + 512]), writes=[("xt", b)])
            S.pool(lambda e, b=b: e.tensor_copy(out=xb[b][:, :, :], in_=xt[b][:, :, :]), reads=[("xt", b)], writes=[("xb", b)])
            for (wb, wkey, dst, dkey, scale) in ((wqb, "wqb", QT, "QT", 0.125), (wkb, "wkb", KT, "KT", 1.0)):
                for hp in range(2):
                    q = pq % 2
                    pq += 1
                    for k in range(8):
                        S.pe(lambda e, q=q, wb=wb, k=k, hp=hp, b=b: e.matmul(
                            ps_p[q][:, :], lhsT=wb[:, k, hp * 128:(hp + 1) * 128], rhs=xb[b][:, k, :], start=(k == 0), stop=(k == 7)),
                            reads=[(wkey,), ("xb", b)], writes=[("ps_p", q)])
                    S.act(lambda e, q=q, dst=dst, hp=hp, t0=t0, scale=scale: e.activation(
                        out=dst[:, hp, t0:t0 + 512], in_=ps_p[q][:, :], func=AF.Copy, scale=scale),
                        reads=[("ps_p", q)], writes=[(dkey, hp, tt)])
            for s in range(4):
                blk = tt * 4 + s
                q = pq % 2
                pq += 1
                for k in range(8):
                    S.pe(lambda e, q=q, k=k, s=s, b=b: e.matmul(
                        ps_p[q][:, 0:256], lhsT=xb[b][:, k, s * 128:(s + 1) * 128], rhs=wvb[:, k, :], start=(k == 0), stop=(k == 7)),
                        reads=[("wvb",), ("xb", b)], writes=[("ps_p", q)])
                S.dve(lambda e, q=q, blk=blk: e.tensor_copy(out=V[:, blk, :, 0:64], in_=ps_p[q][:, 0:256].rearrange("p (h d) -> p h d", h=4)),
                      reads=[("ps_p", q)], writes=[("V", blk)])
                if fox:
                    q = pq % 2
                    pq += 1
                    for k in range(8):
                        S.pe(lambda e, q=q, k=k, s=s, b=b: e.matmul(
                            ps_p[q][:, 0:4], lhsT=xb[b][:, k, s * 128:(s + 1) * 128], rhs=wfb[:, k, :], start=(k == 0), stop=(k == 7)),
                            reads=[("wfb",), ("xb", b)], writes=[("ps_p", q)])
                    S.dve(lambda e, q=q, blk=blk: e.tensor_tensor(out=lf[:, blk, :], in0=ps_p[q][:, 0:4], in1=bft[:, :], op=ALU.add),
                          reads=[("ps_p", q), ("bft",)], writes=[("lf",)])
        if fox:
            lf2 = lf[:, :, :].rearrange("p n h -> p (n h)")
            S.act(lambda e: e.activation(out=lf2, in_=lf2, func=AF.Exp, scale=-1.0), reads=[("lf",)], writes=[("lf",)])
            S.act(lambda e: e.activation(out=lf2, in_=lf2, func=AF.Ln, bias=onecol[:, 0:1]), reads=[("lf",), ("onecol",)], writes=[("lf",)])
            S.dve(lambda e: e.tensor_scalar(out=lf2, in0=lf2, scalar1=-1.0, scalar2=None, op0=ALU.mult), reads=[("lf",)], writes=[("lf",)])
            S.pe(lambda e: e.matmul(ps_x[:, 0:256], lhsT=triU, rhs=lf2, start=True, stop=True), reads=[("lf",), ("cf",)], writes=[("ps_x",)])
            S.pe(lambda e: e.matmul(ps_y[:, 0:256], lhsT=onesf[:, :], rhs=lf2, start=True, stop=True), reads=[("lf",), ("onesf",)], writes=[("ps_y",)])
            S.dve(lambda e: e.tensor_copy(out=tot[:, :, :].rearrange("p n h -> p (n h)"), in_=ps_y[:, 0:256]), reads=[("ps_y",)], writes=[("tot",)])
            S.dve(lambda e: e.tensor_copy(out=hsA[:, :, :], in_=tot[:, :, :]), reads=[("tot",)], writes=[("hsA",)])
            A, B, ka, kb_ = hsA, hsB, "hsA", "hsB"
            for sft in (1, 2, 4, 8, 16, 32):
                S.dve(lambda e, A=A, B=B, sft=sft: e.tensor_tensor(out=B[:, sft:, :], in0=A[:, sft:, :], in1=A[:, 0:NBLK - sft, :], op=ALU.add),
                      reads=[(ka,)], writes=[(kb_,)])
                S.dve(lambda e, A=A, B=B, sft=sft: e.tensor_copy(out=B[:, 0:sft, :], in_=A[:, 0:sft, :]),
                      reads=[(ka,)], writes=[(kb_,)])
                A, B, ka, kb_ = B, A, kb_, ka
            S.dve(lambda e, A=A: e.tensor_tensor(out=cum[:, :, :], in0=A[:, :, :], in1=tot[:, :, :], op=ALU.subtract),
                  reads=[(ka,), ("tot",)], writes=[("cum",)])
            S.dve(lambda e: e.tensor_tensor(out=cum[:, :, :].rearrange("p n h -> p (n h)"), in0=cum[:, :, :].rearrange("p n h -> p (n h)"),
                                            in1=ps_x[:, 0:256], op=ALU.add),
                  reads=[("cum",), ("ps_x",)], writes=[("cum",)])
            S.dve(lambda e: e.tensor_scalar(out=ncum[:, :, :], in0=cum[:, :, :], scalar1=-1.0, scalar2=None, op0=ALU.mult),
                  reads=[("cum",)], writes=[("ncum",)])
        else:
            for hp in range(2):
                S.dve(lambda e, hp=hp: e.tensor_reduce(out=kmf[:, hp, :], in_=KT[:, hp, :].rearrange("p (n l) -> p n l", l=256),
                                                       axis=AX.X, op=ALU.add),
                      reads=[("KT", hp, tt) for tt in range(SEQ // 512)], writes=[("kmf",)])
            S.dve(lambda e: e.tensor_copy(out=kmb[:, :, :], in_=kmf[:, :, :]), reads=[("kmf",)], writes=[("kmb",)])

        if dbg is not None and fox:
            S.pool(lambda e: e.tensor_copy(out=Sb[0][:, :], in_=QT[:, 0, 512:1024]), reads=[("QT", 0, 1)], writes=[("Sb", 0)])
            S.pool(lambda e: e.tensor_copy(out=Sb[1][:, 0:128], in_=KT[:, 0, 256:384]), reads=[("KT", 0, 0)], writes=[("Sb", 1)])
            S.dma(lambda e: [e.dma_start(out=dbg[:, 1536:2048], in_=Sb[0][:, :]), e.dma_start(out=dbg[:, 2048:2176], in_=Sb[1][:, 0:128])],
                  reads=[("Sb", 0), ("Sb", 1)], writes=[("dbg3",)], n=2)
            S.dma(lambda e: [e.dma_start(out=dbg[:, 0:256], in_=cum[:, :, :].rearrange("p n h -> p (n h)")),
                             e.dma_start(out=dbg[:, 256:512], in_=lf[:, :, :].rearrange("p n h -> p (n h)"))],
                  reads=[("cum",), ("lf",)], writes=[("dbg",)], n=2)
        sqi = 0
        oqi = 0
        NT = SEQ // 512
        pending = []

        def drain(n):
            for _ in range(n):
                if pending:
                    pending.pop(0)[1]()

        def flush(kind=None):
            while pending and (kind is None or any(k == kind for k, _ in pending)):
                pending.pop(0)[1]()

        def prologue_steps(h, qt):
            hp = h // 2
            rows = slice((h % 2) * 64, (h % 2) * 64 + 64)
            q0 = qt * 512
            cb = qt % 2
            steps = []
            if fox:
                for s_ in range(4):
                    blk = 4 * qt + s_

                    def st(s_=s_, blk=blk):
                        S.dve(lambda e: e.tensor_scalar(out=dg[s_ % 2][:, :], in0=ident, scalar1=cum[:, blk, h:h + 1], scalar2=None, op0=ALU.mult),
                              reads=[("cum",), ("cf",)], writes=[("dg", s_ % 2)])
                        S.pe(lambda e: e.matmul(ps_x[:, s_ * 128:(s_ + 1) * 128], lhsT=onesf[:, :], rhs=dg[s_ % 2][:, :], start=True, stop=True),
                             reads=[("dg", s_ % 2), ("onesf",)], writes=[("ps_x",)])
                    steps.append(st)

                def fin():
                    S.act(lambda e: e.activation(out=cqb[cb][:, :], in_=ps_x[:, :], func=AF.Copy), reads=[("ps_x",)], writes=[("cqb", cb)])
                steps.append(fin)
            else:
                for s_ in range(4):
                    qb = 4 * qt + s_
                    own = qb // 2

                    def st_a(s_=s_, own=own):
                        if own >= 3:
                            S.pe(lambda e: e.matmul(ps_x[:, s_ * 32:(s_ + 1) * 32], lhsT=QT[rows, hp, q0 + s_ * 128:q0 + (s_ + 1) * 128], rhs=kmb[rows, hp, :],
                                                    start=True, stop=True),
                                 reads=[("QT", hp, qt), ("kmb",)], writes=[("ps_x",)])

                    def st_b(s_=s_, own=own):
                        if own >= 3:
                            S.dve(lambda e: e.tensor_copy(out=G[:, 0:own], in_=ps_x[:, s_ * 32:s_ * 32 + own]), reads=[("ps_x",)], writes=[("G",)])
                            S.dve(lambda e: e.max(out=mx[:, :], in_=G[:, :]), reads=[("G",)], writes=[("mx",)])
                            S.dve(lambda e: e.tensor_scalar(out=selm[:, s_, :], in0=G[:, :], scalar1=mx[:, 2:3], scalar2=-1.0, op0=ALU.is_ge, op1=ALU.add),
                                  reads=[("G",), ("mx",)], writes=[("selm", s_)])
                            S.dve(lambda e: e.memset(selm[:, s_, own:own + 1], 0.0), writes=[("selm", s_)])
                        else:
                            S.dve(lambda e: e.memset(selm[:, s_, :], -1.0), writes=[("selm", s_)])
                            S.dve(lambda e: e.memset(selm[:, s_, 0:own + 1], 0.0), writes=[("selm", s_)])

                    def st_c(s_=s_):
                        S.pe(lambda e: e.matmul(ps_y[0:32, s_ * 128:(s_ + 1) * 128], lhsT=selm[:, s_, :], rhs=id30[:, :], start=True, stop=True),
                             reads=[("selm", s_), ("id30",)], writes=[("ps_y",)])
                    steps += [st_a, st_b, st_c]

                def fin():
                    S.act(lambda e: e.activation(out=selbT[cb][:, :], in_=ps_y[0:32, :], func=AF.Copy), reads=[("ps_y",)], writes=[("selbT", cb)])
                steps.append(fin)
            return [("pro", f) for f in steps]

        def finalize_steps(h, qt, oq):
            q0 = qt * 512

            def f1():
                S.dve(lambda e: e.reciprocal(out=rc[64:65, :], in_=ps_O[oq][64:65, :]), reads=[("ps_O", oq)], writes=[("rc",)])

            def f2():
                S.pe(lambda e: e.matmul(ps_x[0:64, :], lhsT=onesf[64:65, 0:64], rhs=rc[64:65, :], start=True, stop=True),
                     reads=[("rc",), ("onesf",)], writes=[("ps_x",)])

            def f3():
                S.act(lambda e: e.activation(out=rb[0:64, :], in_=ps_x[0:64, :], func=AF.Copy), reads=[("ps_x",)], writes=[("rb",)])

            def f4():
                S.dve(lambda e: e.tensor_tensor(out=ot[oq][0:64, :], in0=ps_O[oq][0:64, :], in1=rb[0:64, :], op=ALU.mult),
                      reads=[("ps_O", oq), ("rb",)], writes=[("ot", oq)])
                S.dma(lambda e: e.dma_start(out=OT[h * 64:(h + 1) * 64, q0:q0 + 512], in_=ot[oq][0:64, :]),
                      reads=[("ot", oq)], writes=[("OT", h, qt)])
            return [("fin", f) for f in (f1, f2, f3, f4)]

        for h in range(4):
            hp = h // 2
            rows = slice((h % 2) * 64, (h % 2) * 64 + 64)
            if not fox:
                S.dve(lambda e: e.memset(G[:, :], -1e30), writes=[("G",)])
            for _, f in prologue_steps(h, 0):
                f()
            for qt in range(NT):
                q0 = qt * 512
                cb = qt % 2
                if qt + 1 < NT:
                    pending.extend(prologue_steps(h, qt + 1))
                nkb = 4 * qt + 4
                oq = oqi % 2
                oqi += 1
                unit_sq = {}

                def emit_S(kb, qt=qt, q0=q0, cb=cb, h=h, hp=hp, rows=rows):
                    nonlocal sqi
                    c0 = max(0, kb - 4 * qt) * 128
                    diag = kb >= 4 * qt
                    sq = sqi % NSB
                    sqi += 1
                    unit_sq[kb] = sq
                    psS, psSk = ((ps_S[0], ("ps_S", 0)), (ps_S[1], ("ps_S", 1)), (ps_p[0], ("ps_p", 0)), (ps_p[1], ("ps_p", 1)))[sq]
                    last_s = not diag and fox
                    S.pe(lambda e: e.matmul(psS[:, c0:512], lhsT=KT[rows, hp, kb * 128:(kb + 1) * 128], rhs=QT[rows, hp, q0 + c0:q0 + 512],
                                            start=True, stop=last_s),
                         reads=[("KT", hp, kb // 4), ("QT", hp, qt)], writes=[psSk])
                    if not fox:
                        S.pe(lambda e: e.matmul(psS[:, c0:512], lhsT=ohb[:, kb // 2, :], rhs=selbT[cb][:, c0:512], start=False, stop=(not diag)),
                             reads=[("ohb",), ("selbT", cb)], writes=[psSk])
                    if diag:
                        S.pe(lambda e: e.matmul(psS[:, c0:c0 + 128], lhsT=identb[:, :], rhs=trimb[:, :], start=False, stop=True),
                             reads=[("identb",), ("trimb",)], writes=[psSk])
                    if fox:
                        S.dve(lambda e: e.tensor_tensor(out=Sb[sq][:, c0:512], in0=psS[:, c0:512], in1=cqb[cb][:, c0:512], op=ALU.add),
                              reads=[psSk, ("cqb", cb)], writes=[("Sb", sq)])
                        S.act(lambda e: e.activation(out=Pt[sq][:, c0:512], in_=Sb[sq][:, c0:512], func=AF.Exp, bias=ncum[:, kb, h:h + 1]),
                              reads=[("Sb", sq), ("ncum",)], writes=[("Pt", sq)])
                    else:
                        S.act(lambda e: e.activation(out=Pt[sq][:, c0:512], in_=psS[:, c0:512], func=AF.Exp),
                              reads=[psSk], writes=[("Pt", sq)])

                def emit_PV(kb, qt=qt, h=h, oq=oq, nkb=nkb):
                    c0 = max(0, kb - 4 * qt) * 128
                    sq = unit_sq[kb]
                    S.pe(lambda e: e.matmul(ps_O[oq][0:65, c0:512], lhsT=V[:, kb, h, 0:65], rhs=Pt[sq][:, c0:512], start=(kb == 0), stop=(kb == nkb - 1)),
                         reads=[("V", kb), ("Vones",), ("Pt", sq)], writes=[("ps_O", oq)])

                LOOK = NSB - 1
                for i in range(min(LOOK, nkb)):
                    emit_S(i)
                for i in range(nkb):
                    if i + LOOK < nkb:
                        emit_S(i + LOOK)
                    emit_PV(i)
                    drain(2)
                flush("pro")
                pending.extend(finalize_steps(h, qt, oq))
            flush()
        S.emit()


def stage_ssd(S, Hfull, wz, wx, wB, wC, wdt, cw, cbias, dtb, alog, dcol, nw, cst, YT, ntiles=SEQ // 512):
    nc = S.nc
    S.new_stage()
    with contextlib.ExitStack() as es:
        sb = lambda name, shape, dt: es.enter_context(nc.sbuf_tensor(name + S.sfx, shape, dt))
        pst = lambda name, shape, dt=F32: es.enter_context(nc.psum_tensor(name + S.sfx, shape, dt))
        xt = [sb("xt%d" % i, [128, 8, 512], F32) for i in range(2)]
        xb = [sb("xb%d" % i, [128, 8, 512], BF16) for i in range(2)]
        stg = sb("stg", [128, 8, 256], F32)
        wzb = sb("wzb", [128, 8, 512], BF16)
        wxb = sb("wxb", [128, 8, 512], BF16)
        wBb = sb("wBb", [128, 8, 128], BF16)
        wCb = sb("wCb", [128, 8, 128], BF16)
        wdts = sb("wdts", [128, 8, 8], F32)
        wdtb = sb("wdtb", [128, 8, 8], BF16)
        cf = sb("cf", [128, 384], F32)
        identb = sb("identb", [128, 128], BF16)
        trimb = sb("trimb", [128, 128], BF16)
        onesf = sb("onesf", [128, 128], F32)
        onesb = sb("onesb", [128, 128], BF16)
        onecol = sb("onecol", [128, 1], F32)
        cwt = sb("cwt", [128, 6, 4], F32)
        cbt = sb("cbt", [128, 6], F32)
        dtbt = sb("dtbt", [128, 8], F32)
        arep = sb("arep", [128, 8], F32)
        dct = sb("dct", [128, 4], F32)
        nwt = sb("nwt", [128, 4], F32)
        pre = sb("pre", [128, 6, 515], F32)
        acc = [sb("acc%d" % i, [128, 512], F32) for i in range(2)]
        xcf = sb("xcf", [128, 4, 512], F32)
        xcb = sb("xcb", [128, 4, 512], BF16)
        BTb = sb("BTb", [128, 512], BF16)
        CTb = sb("CTb", [128, 512], BF16)
        CTf = sb("CTf", [128, 512], F32)
        zs = sb("zs", [128, 4, 512], F32)
        xtokf = sb("xtokf", [128, 512], F32)
        Btokb = sb("Btokb", [128, 128], BF16)
        dtp = sb("dtp", [128, 8], F32)
        dtt = sb("dtt", [128, 8], F32)
        adt = sb("adt", [128, 8], F32)
        ncs = sb("ncs", [128, 8], F32)
        wtt = sb("wtt", [128, 8], F32)
        dtw = sb("dtw", [128, 8], F32)
        cbTf = sb("cbTf", [128, 128], F32)
        xdtp = sb("xdtp", [128, 8, 128], BF16)
        xdtw = sb("xdtw", [128, 512], BF16)
        arpa = sb("arpa", [128, 8, 128], F32)
        Efa = sb("Efa", [128, 8, 128], F32)
        Asb = sb("Asb", [128, 8, 128], F32)
        MTa = sb("MTa", [128, 8, 128], BF16)
        CTsa = sb("CTsa", [128, 8, 128], BF16)
        Sf = sb("Sf", [128, 512], F32)
        stf = sb("stf", [128, 512], F32)
        Sbp = sb("Sbp", [128, 8, 128], BF16)
        yf = sb("yf", [128, 512], F32)
        yg = sb("yg", [128, 4, 512], F32)
        sqb = sb("sqb", [128, 4, 512], BF16)
        rstd = sb("rstd", [128, 512], F32)
        ytmp = [sb("ytmp%d" % i, [128, 512], F32) for i in range(2)]
        yo = [sb("yo%d" % i, [128, 512], BF16) for i in range(2)]
        ps_p = [pst("ps_p%d" % i, [128, 512]) for i in range(2)]
        ps_y = [pst("ps_y%d" % i, [128, 512]) for i in range(4)]
        ps_a = pst("ps_a", [128, 512])
        ps_s = pst("ps_s", [128, 512])
        triU = cf[:, 128:256]

        S.dma(lambda e: e.dma_start(out=cf[:, :], in_=cst), writes=[("cf",)])
        S.pool(lambda e: e.tensor_copy(out=identb[:, :], in_=cf[:, 0:128]), reads=[("cf",)], writes=[("identb",)])
        S.pool(lambda e: e.tensor_copy(out=trimb[:, :], in_=cf[:, 256:384]), reads=[("cf",)], writes=[("trimb",)])
        S.pool(lambda e: e.memset(onesf[:, :], 1.0), writes=[("onesf",)])
        S.pool(lambda e: e.memset(onesb[:, :], 1.0), writes=[("onesb",)])
        S.pool(lambda e: e.memset(onecol[:, :], 1.0), writes=[("onecol",)])
        S.pool(lambda e: e.memset(pre[:, :, 0:3], 0.0), writes=[("pre", cc) for cc in range(6)])
        S.pool(lambda e: e.memset(Sf[:, :], 0.0), writes=[("Sf",)])
        S.pool(lambda e: e.memset(Sbp[:, :, :], 0.0), writes=[("Sbp",)])
        S.pool(lambda e: e.memset(xdtp[:, :, :], 0.0), writes=[("xdtp",)])
        S.dma(lambda e: [e.dma_start(out=cwt[:, :, :], in_=cw), e.dma_start(out=cbt[:, :], in_=cbias),
                         e.dma_start(out=dtbt[:, :], in_=dtb), e.dma_start(out=arep[:, :], in_=alog),
                         e.dma_start(out=dct[:, :], in_=dcol), e.dma_start(out=nwt[:, :], in_=nw),
                         e.dma_start(out=wdts[:, :, :], in_=wdt.rearrange("(c p) f -> p c f", p=128))],
              writes=[("prm",)], n=7)
        S.pool(lambda e: e.tensor_copy(out=wdtb[:, :, :], in_=wdts[:, :, :]), reads=[("prm",)], writes=[("wdtb",)])
        S.act(lambda e: e.activation(out=arep[:, :], in_=arep[:, :], func=AF.Exp), reads=[("prm",)], writes=[("arep",)])
        S.dve(lambda e: e.tensor_scalar(out=arep[:, :], in0=arep[:, :], scalar1=-1.0, scalar2=None, op0=ALU.mult), reads=[("arep",)], writes=[("arep",)])
        Hv = Hfull.rearrange("(c p) t -> p c t", p=128)
        load_cast_weight(S, wz.rearrange("(c p) f -> p c f", p=128), wzb, stg, "wzb", 512)
        load_cast_weight(S, wx.rearrange("(c p) f -> p c f", p=128), wxb, stg, "wxb", 512)
        load_cast_weight(S, wB.rearrange("(c p) f -> p c f", p=128), wBb, stg, "wBb", 128)
        load_cast_weight(S, wC.rearrange("(c p) f -> p c f", p=128), wCb, stg, "wCb", 128)

        pq = 0
        hq = 0
        for tt in range(ntiles):
            t0 = tt * 512
            b = tt % 2
            S.dma(lambda e, b=b, t0=t0: e.dma_start(out=xt[b][:, :, :], in_=Hv[:, :, t0:t0 + 512]), writes=[("xt", b)])
            S.pool(lambda e, b=b: e.tensor_copy(out=xb[b][:, :, :], in_=xt[b][:, :, :]), reads=[("xt", b)], writes=[("xb", b)])
            for hc in range(4):
                q = pq % 2
                pq += 1
                for k in range(8):
                    S.pe(lambda e, q=q, k=k, hc=hc, b=b: e.matmul(ps_p[q][:, :], lhsT=wzb[:, k, hc * 128:(hc + 1) * 128], rhs=xb[b][:, k, :],
                                                                 start=(k == 0), stop=(k == 7)),
                         reads=[("wzb",), ("xb", b)], writes=[("ps_p", q)])
                S.act(lambda e, q=q, hc=hc: e.activation(out=zs[:, hc, :], in_=ps_p[q][:, :], func=AF.Silu),
                      reads=[("ps_p", q)], writes=[("zs", hc)])
            for cc in range(6):
                q = pq % 2
                pq += 1
                if cc < 4:
                    wsel, wkey, csl = wxb, "wxb", slice(cc * 128, (cc + 1) * 128)
                elif cc == 4:
                    wsel, wkey, csl = wBb, "wBb", slice(0, 128)
                else:
                    wsel, wkey, csl = wCb, "wCb", slice(0, 128)
                for k in range(8):
                    S.pe(lambda e, q=q, k=k, wsel=wsel, csl=csl, b=b: e.matmul(ps_p[q][:, :], lhsT=wsel[:, k, csl], rhs=xb[b][:, k, :],
                                                                            start=(k == 0), stop=(k == 7)),
                         reads=[(wkey,), ("xb", b)], writes=[("ps_p", q)])
                S.dve(lambda e, q=q, cc=cc: e.tensor_copy(out=pre[:, cc, 3:515], in_=ps_p[q][:, :]), reads=[("ps_p", q)], writes=[("pre", cc)])
                a = acc[cc % 2]
                ak = ("acc", cc % 2)
                S.dve(lambda e, a=a, cc=cc: e.tensor_scalar(out=a[:, :], in0=pre[:, cc, 0:512], scalar1=cwt[:, cc, 0:1], scalar2=None, op0=ALU.mult),
                      reads=[("pre", cc), ("prm",)], writes=[ak])
                for kk in range(1, 4):
                    S.dve(lambda e, a=a, cc=cc, kk=kk: e.scalar_tensor_tensor(out=a[:, :], in0=pre[:, cc, kk:kk + 512], scalar=cwt[:, cc, kk:kk + 1],
                                                                             in1=a[:, :], op0=ALU.mult, op1=ALU.add),
                          reads=[("pre", cc), ak], writes=[ak])
                S.pool(lambda e, cc=cc: e.tensor_copy(out=pre[:, cc, 0:3], in_=pre[:, cc, 512:515]), reads=[("pre", cc)], writes=[("pre", cc)])
                if cc < 4:
                    S.act(lambda e, a=a, cc=cc: e.activation(out=xcf[:, cc, :], in_=a[:, :], func=AF.Silu, bias=cbt[:, cc:cc + 1]),
                          reads=[ak, ("prm",)], writes=[("xcf", cc)])
                    S.pool(lambda e, cc=cc: e.tensor_copy(out=xcb[:, cc, :], in_=xcf[:, cc, :]), reads=[("xcf", cc)], writes=[("xcb", cc)])
                elif cc == 4:
                    S.act(lambda e, a=a, cc=cc: e.activation(out=BTb[:, :], in_=a[:, :], func=AF.Silu, bias=cbt[:, cc:cc + 1]),
                          reads=[ak, ("prm",)], writes=[("BTb",)])
                else:
                    S.act(lambda e, a=a, cc=cc: e.activation(out=CTf[:, :], in_=a[:, :], func=AF.Silu, bias=cbt[:, cc:cc + 1]),
                          reads=[ak, ("prm",)], writes=[("CTf",)])
                    S.pool(lambda e: e.tensor_copy(out=CTb[:, :], in_=CTf[:, :]), reads=[("CTf",)], writes=[("CTb",)])
            for c in range(4):
                cs = slice(c * 128, (c + 1) * 128)
                for cc in range(4):
                    S.pe(lambda e, cc=cc, cs=cs: e.matmul(ps_p[0][:, cc * 128:(cc + 1) * 128], lhsT=xcb[:, cc, cs], rhs=identb[:, :], start=True, stop=True),
                         reads=[("xcb", cc), ("identb",)], writes=[("ps_p", 0)])
                S.pe(lambda e, cs=cs: e.matmul(ps_p[1][:, 0:128], lhsT=BTb[:, cs], rhs=identb[:, :], start=True, stop=True),
                     reads=[("BTb",), ("identb",)], writes=[("ps_p", 1)])
                S.act(lambda e: e.activation(out=xtokf[:, :], in_=ps_p[0][:, :], func=AF.Copy), reads=[("ps_p", 0)], writes=[("xtokf",)])
                S.dve(lambda e: e.tensor_copy(out=Btokb[:, :], in_=ps_p[1][:, 0:128]), reads=[("ps_p", 1)], writes=[("Btokb",)])
                for k in range(8):
                    S.pe(lambda e, k=k, cs=cs, b=b: e.matmul(ps_a[:, 384:392], lhsT=xb[b][:, k, cs], rhs=wdtb[:, k, :], start=(k == 0), stop=(k == 7)),
                         reads=[("xb", b), ("wdtb",)], writes=[("ps_a", "dt")])
                S.dve(lambda e: e.tensor_tensor(out=dtp[:, :], in0=ps_a[:, 384:392], in1=dtbt[:, :], op=ALU.add),
                      reads=[("ps_a", "dt"), ("prm",)], writes=[("dtp",)])
                S.act(lambda e: e.activation(out=dtp[:, :], in_=dtp[:, :], func=AF.Exp), reads=[("dtp",)], writes=[("dtp",)])
                S.act(lambda e: e.activation(out=dtt[:, :], in_=dtp[:, :], func=AF.Ln, bias=onecol[:, 0:1]), reads=[("dtp",), ("onecol",)], writes=[("dtt",)])
                S.dve(lambda e: e.tensor_tensor(out=adt[:, :], in0=dtt[:, :], in1=arep[:, :], op=ALU.mult), reads=[("dtt",), ("arep",)], writes=[("adt",)])
                S.pe(lambda e: e.matmul(ps_a[:, 392:400], lhsT=triU, rhs=adt[:, :], start=True, stop=True), reads=[("adt",), ("cf",)], writes=[("ps_a", "cs")])
                S.pe(lambda e: e.matmul(ps_a[:, 400:408], lhsT=onesf[:, :], rhs=adt[:, :], start=True, stop=True), reads=[("adt",), ("onesf",)], writes=[("ps_a", "tot")])
                S.dve(lambda e: e.tensor_scalar(out=ncs[:, :], in0=ps_a[:, 392:400], scalar1=-1.0, scalar2=None, op0=ALU.mult),
                      reads=[("ps_a", "cs")], writes=[("ncs",)])
                S.dve(lambda e: e.tensor_tensor(out=wtt[:, :], in0=ps_a[:, 400:408], in1=ncs[:, :], op=ALU.add),
                      reads=[("ps_a", "tot"), ("ncs",)], writes=[("wtt",)])
                S.act(lambda e: e.activation(out=wtt[:, :], in_=wtt[:, :], func=AF.Exp), reads=[("wtt",)], writes=[("wtt",)])
                S.dve(lambda e: e.tensor_tensor(out=dtw[:, :], in0=dtt[:, :], in1=wtt[:, :], op=ALU.mult), reads=[("dtt",), ("wtt",)], writes=[("dtw",)])
                S.pe(lambda e, cs=cs: e.matmul(ps_a[:, 256:384], lhsT=BTb[:, cs], rhs=CTb[:, cs], start=True, stop=True),
                     reads=[("BTb",), ("CTb",)], writes=[("ps_a", "cb")])
                S.dve(lambda e: e.tensor_tensor(out=cbTf[:, :], in0=ps_a[:, 256:384], in1=triU, op=ALU.mult), reads=[("ps_a", "cb"), ("cf",)], writes=[("cbTf",)])
                xv = xtokf[:, :].rearrange("p (r d) -> p r d", d=64)
                S.dve(lambda e, xv=xv: e.tensor_tensor(out=xdtp[:, 0:8:2, 0:64], in0=xv[:, 0:8:2, :],
                                                       in1=dtt[:, 0:8:2].unsqueeze(2).to_broadcast([128, 4, 64]), op=ALU.mult),
                      reads=[("xtokf",), ("dtt",)], writes=[("xdtp",)])
                S.pool(lambda e, xv=xv: e.tensor_tensor(out=xdtp[:, 1:8:2, 64:128], in0=xv[:, 1:8:2, :],
                                                        in1=dtt[:, 1:8:2].unsqueeze(2).to_broadcast([128, 4, 64]), op=ALU.mult),
                       reads=[("xtokf",), ("dtt",)], writes=[("xdtp",)])
                S.pool(lambda e, xv=xv: e.tensor_tensor(out=xdtw[:, :].rearrange("p (r d) -> p r d", d=64), in0=xv,
                                                        in1=dtw[:, :].unsqueeze(2).to_broadcast([128, 8, 64]), op=ALU.mult),
                       reads=[("xtokf",), ("dtw",)], writes=[("xdtw",)])
                S.pe(lambda e: e.matmul(ps_s[:, :], lhsT=Btokb[:, :], rhs=xdtw[:, :], start=True, stop=True),
                     reads=[("Btokb",), ("xdtw",)], writes=[("ps_s",)])
                S.act(lambda e: e.activation(out=stf[:, :], in_=ps_s[:, :], func=AF.Copy), reads=[("ps_s",)], writes=[("stf",)])
                S.dve(lambda e: e.tensor_copy(out=arpa[:, :, :], in_=adt[:, :].unsqueeze(2).to_broadcast([128, 8, 128])),
                      reads=[("adt",)], writes=[("arpa",)])
                for hh, (bank, bkey) in enumerate(((ps_p[1], ("ps_p", 1)), (ps_s, ("ps_s",)))):
                    for r4 in range(4):
                        r = hh * 4 + r4
                        S.pe(lambda e, bank=bank, r=r, r4=r4: e.matmul(bank[:, r4 * 128:(r4 + 1) * 128], lhsT=arpa[:, r, :], rhs=triU, start=True, stop=True),
                             reads=[("arpa",), ("cf",)], writes=[bkey])
                    hs4 = slice(hh * 4, hh * 4 + 4)
                    S.act(lambda e, bank=bank, hs4=hs4: e.activation(out=Efa[:, hs4, :].rearrange("p r l -> p (r l)"), in_=bank[:, :], func=AF.Exp),
                          reads=[bkey], writes=[("Efa", hh)])
                    S.act(lambda e, bank=bank, hs4=hs4: e.activation(out=Asb[:, hs4, :].rearrange("p r l -> p (r l)"), in_=bank[:, :], func=AF.Copy),
                          reads=[bkey], writes=[("Asb", hh)])
                S.dve(lambda e: e.tensor_tensor(out=Asb[:, :, :], in0=Asb[:, :, :], in1=ncs[:, :].unsqueeze(2).to_broadcast([128, 8, 128]), op=ALU.add),
                      reads=[("Asb", 0), ("Asb", 1), ("ncs",)], writes=[("Asb", 0), ("Asb", 1)])
                S.dve(lambda e: e.tensor_tensor(out=Asb[:, :, :], in0=Asb[:, :, :], in1=triU.unsqueeze(1).to_broadcast([128, 8, 128]), op=ALU.mult),
                      reads=[("Asb", 0), ("Asb", 1), ("cf",)], writes=[("Asb", 0), ("Asb", 1)])
                S.act(lambda e: e.activation(out=Asb[:, :, :].rearrange("p r l -> p (r l)"), in_=Asb[:, :, :].rearrange("p r l -> p (r l)"), func=AF.Exp),
                      reads=[("Asb", 0), ("Asb", 1)], writes=[("Asb", 0), ("Asb", 1)])
                S.dve(lambda e: e.tensor_tensor(out=MTa[:, :, :], in0=Asb[:, :, :], in1=cbTf[:, :].unsqueeze(1).to_broadcast([128, 8, 128]), op=ALU.mult),
                      reads=[("Asb", 0), ("Asb", 1), ("cbTf",)], writes=[("MTa",)])
                S.pool(lambda e, cs=cs: e.tensor_tensor(out=CTsa[:, :, :], in0=Efa[:, :, :], in1=CTf[:, cs].unsqueeze(1).to_broadcast([128, 8, 128]), op=ALU.mult),
                       reads=[("Efa", 0), ("Efa", 1), ("CTf",)], writes=[("CTsa",)])
                for r in range(8):
                    hc = r // 2
                    S.pe(lambda e, r=r, hc=hc, cs=cs: e.matmul(ps_y[hc][:, cs], lhsT=xdtp[:, r, :], rhs=MTa[:, r, :], start=(r % 2 == 0), stop=False),
                         reads=[("xdtp",), ("MTa",)], writes=[("ps_y", hc)])
                    S.pe(lambda e, r=r, hc=hc, cs=cs: e.matmul(ps_y[hc][:, cs], lhsT=Sbp[:, r, :], rhs=CTsa[:, r, :], start=False, stop=(r % 2 == 1)),
                         reads=[("Sbp",), ("CTsa",)], writes=[("ps_y", hc)])
                Sv = Sf[:, :].rearrange("p (r d) -> p r d", d=64)
                S.dve(lambda e, Sv=Sv: e.tensor_tensor(out=Sv, in0=Sv, in1=Efa[:, :, 127:128].to_broadcast([128, 8, 64]), op=ALU.mult),
                      reads=[("Sf",), ("Efa", 0), ("Efa", 1)], writes=[("Sf",)])
                S.dve(lambda e: e.tensor_tensor(out=Sf[:, :], in0=Sf[:, :], in1=stf[:, :], op=ALU.add),
                      reads=[("Sf",), ("stf",)], writes=[("Sf",)])
                S.act(lambda e, Sv=Sv: e.activation(out=Sbp[:, 0:8:2, 0:64], in_=Sv[:, 0:8:2, :], func=AF.Copy), reads=[("Sf",)], writes=[("Sbp",)])
                S.act(lambda e, Sv=Sv: e.activation(out=Sbp[:, 1:8:2, 64:128], in_=Sv[:, 1:8:2, :], func=AF.Copy), reads=[("Sf",)], writes=[("Sbp",)])
            for hc in range(4):
                S.dve(lambda e, hc=hc: e.tensor_scalar(out=yf[:, :], in0=xcf[:, hc, :], scalar1=dct[:, hc:hc + 1], scalar2=None, op0=ALU.mult),
                      reads=[("xcf", hc), ("prm",)], writes=[("yf",)])
                S.dve(lambda e, hc=hc: e.tensor_tensor(out=yf[:, :], in0=ps_y[hc][:, :], in1=yf[:, :], op=ALU.add),
                      reads=[("yf",), ("ps_y", hc)], writes=[("yf",)])
                S.dve(lambda e, hc=hc: e.tensor_tensor(out=yg[:, hc, :], in0=yf[:, :], in1=zs[:, hc, :], op=ALU.mult),
                      reads=[("yf",), ("zs", hc)], writes=[("yg", hc)])
                S.act(lambda e, hc=hc: e.activation(out=sqb[:, hc, :], in_=yg[:, hc, :], func=AF.Square), reads=[("yg", hc)], writes=[("sqb", hc)])
            for hc in range(4):
                S.pe(lambda e, hc=hc: e.matmul(ps_p[0][:, :], lhsT=onesb[:, :], rhs=sqb[:, hc, :], start=(hc == 0), stop=(hc == 3)),
                     reads=[("sqb", hc), ("onesb",)], writes=[("ps_p", 0)])
            S.dve(lambda e: e.tensor_scalar(out=rstd[:, :], in0=ps_p[0][:, :], scalar1=1.0 / 512, scalar2=LN_EPS, op0=ALU.mult, op1=ALU.add),
                  reads=[("ps_p", 0)], writes=[("rstd",)])
            S.act(lambda e: e.activation(out=rstd[:, :], in_=rstd[:, :], func=AF.Ln), reads=[("rstd",)], writes=[("rstd",)])
            S.act(lambda e: e.activation(out=rstd[:, :], in_=rstd[:, :], func=AF.Exp, scale=-0.5), reads=[("rstd",)], writes=[("rstd",)])
            for hc in range(4):
                yb = hc % 2
                S.dve(lambda e, hc=hc, yb=yb: e.tensor_tensor(out=ytmp[yb][:, :], in0=yg[:, hc, :], in1=rstd[:, :], op=ALU.mult),
                      reads=[("yg", hc), ("rstd",)], writes=[("ytmp", yb)])
                S.act(lambda e, hc=hc, yb=yb: e.activation(out=yo[yb][:, :], in_=ytmp[yb][:, :], func=AF.Copy, scale=nwt[:, hc:hc + 1]),
                      reads=[("ytmp", yb), ("prm",)], writes=[("yo", yb)])
                S.dma(lambda e, hc=hc, yb=yb, t0=t0: e.dma_start(out=YT[hc * 128:(hc + 1) * 128, t0:t0 + 512], in_=yo[yb][:, :]),
                      reads=[("yo", yb)], writes=[("YT", hc, tt)])
        S.emit()


def stage_outproj(S, OTin, wout, Hin, Hout, lng, lnb, T, Kdim, TP=1024):
    nc = S.nc
    S.new_stage()
    Kc = Kdim // 128
    npass = T // TP
    ntile = TP // 512
    with contextlib.ExitStack() as es:
        sb = lambda name, shape, dt: es.enter_context(nc.sbuf_tensor(name + S.sfx, shape, dt))
        x = sb("x", [128, 8, TP], F32)
        ob = sb("ob", [128, Kc, TP], BF16)
        wob = sb("wob", [128, Kc, 1024], BF16)
        stg = sb("stg", [128, 8, 256], F32)
        scr = dict(ybf=sb("ybf", [128, 8, 512], BF16), sqb=sb("sqb", [128, 8, 512], BF16),
                   mean=sb("mean", [128, 512], F32), rstd=sb("rstd", [128, 512], F32),
                   nmr=sb("nmr", [128, 512], F32), tmp=sb("lntmp", [128, 2, 512], F32))
        gam = sb("gam", [128, 8], F32)
        bet = sb("bet", [128, 8], F32)
        ones_bf = sb("ones_bf", [128, 128], BF16)
        ps_o = [es.enter_context(nc.psum_tensor("ps_o%d" % i + S.sfx, [128, 512], F32)) for i in range(2)]
        ps_s = es.enter_context(nc.psum_tensor("ps_s" + S.sfx, [128, 512], F32))
        ps_q = es.enter_context(nc.psum_tensor("ps_q" + S.sfx, [128, 512], F32))
        S.dma(lambda e: [e.dma_start(out=gam[:, :], in_=lng), e.dma_start(out=bet[:, :], in_=lnb)], writes=[("lnp",)], n=2)
        S.pool(lambda e: e.memset(ones_bf[:, :], 1.0), writes=[("ones",)])
        wv = wout.rearrange("(c p) d -> p c d", p=128)
        for k0 in range(0, Kc, 8):
            for c0 in range(0, 1024, 256):
                S.dma(lambda e, k0=k0, c0=c0: e.dma_start(out=stg[:, :, :], in_=wv[:, k0:k0 + 8, c0:c0 + 256]), writes=[("stg_shared",)])
                S.pool(lambda e, k0=k0, c0=c0: e.tensor_copy(out=wob[:, k0:k0 + 8, c0:c0 + 256], in_=stg[:, :, :]),
                       reads=[("stg_shared",)], writes=[("wob",)])
        Hin_v = Hin.rearrange("(c p) t -> p c t", p=128)
        Hout_v = Hout.rearrange("(c p) t -> p c t", p=128)
        O_v = OTin.rearrange("(c p) t -> p c t", p=128)
        oq = 0
        for p in range(npass):
            t0 = p * TP
            for c in range(8):
                S.dma(lambda e, c=c, t0=t0: e.dma_start(out=x[:, c, :], in_=Hin_v[:, c, t0:t0 + TP]),
                      writes=[("x", c, t) for t in range(ntile)])
            for c in range(Kc):
                S.dma(lambda e, c=c, t0=t0: e.dma_start(out=ob[:, c, :], in_=O_v[:, c, t0:t0 + TP]), writes=[("ob", c)])
            for d in range(8):
                for t in range(ntile):
                    q = oq % 2
                    oq += 1
                    sl = slice(t * 512, (t + 1) * 512)
                    for k in range(Kc):
                        S.pe(lambda e, q=q, k=k, d=d, sl=sl: e.matmul(ps_o[q][:, :], lhsT=wob[:, k, d * 128:(d + 1) * 128], rhs=ob[:, k, sl],
                                                                     start=(k == 0), stop=(k == Kc - 1)),
                             reads=[("wob",), ("ob", k)], writes=[("ps_o", q)])
                    S.dve(lambda e, q=q, d=d, sl=sl: e.scalar_tensor_tensor(out=x[:, d, sl], in0=x[:, d, sl], scalar=ALPHA, in1=ps_o[q][:, :],
                                                                           op0=ALU.mult, op1=ALU.add),
                          reads=[("ps_o", q), ("x", d, t)], writes=[("x", d, t)])
            ln_feature_major(S, x, "x", ntile, gam, bet, ones_bf, scr, ps_s, ps_q, "op")
            for c in range(8):
                S.dma(lambda e, c=c, t0=t0: e.dma_start(out=Hout_v[:, c, t0:t0 + TP], in_=x[:, c, :]),
                      reads=[("x", c, t) for t in range(ntile)], writes=[("Hout", p, c)])
        S.emit()


T_CORE = 2048
NCORES = 8


def _new_nc():
    return bass.Bass("TRN2", target_bir_lowering=False)


def _din(nc, name, shape, dt=F32):
    return nc.dram_tensor(name, list(shape), dt, kind="ExternalInput").ap()


def _dout(nc, name, shape, dt=F32):
    return nc.dram_tensor(name, list(shape), dt, kind="ExternalOutput").ap()


def _ffn_inputs(nc, tag):
    return dict(wg=_din(nc, "wg" + tag, [D, DFF]), wu=_din(nc, "wu" + tag, [D, DFF]), wd=_din(nc, "wd" + tag, [DFF, D]),
                lng=_din(nc, "lng" + tag, [128, 8]), lnb=_din(nc, "lnb" + tag, [128, 8]))


def build_ffn_prog():
    nc = _new_nc()
    Hin = _din(nc, "Hin", [D, T_CORE])
    f = _ffn_inputs(nc, "0")
    Hout = _dout(nc, "Hout", [D, T_CORE])
    with contextlib.ExitStack() as es:
        S = Sched(nc)
        S.setup(es)
        stage_ffn(S, Hin, Hout, f["wg"], f["wu"], f["wd"], f["lng"], f["lnb"], T_CORE)
    return nc


def build_post_prog(Kdim, n_ffn):
    nc = _new_nc()
    Hin = _din(nc, "Hin", [D, T_CORE])
    OTin = _din(nc, "OTin", [Kdim, T_CORE], BF16)
    wout = _din(nc, "wout", [Kdim, D])
    lng = _din(nc, "lngm", [128, 8])
    lnb = _din(nc, "lnbm", [128, 8])
    fs = [_ffn_inputs(nc, str(i)) for i in range(n_ffn)]
    Hout = _dout(nc, "Hout", [D, T_CORE])
    scratch = [nc.dram_tensor("hscr%d" % i, [D, T_CORE], F32).ap() for i in range(n_ffn)]
    with contextlib.ExitStack() as es:
        S = Sched(nc)
        S.setup(es)
        cur = scratch[0] if n_ffn > 0 else Hout
        stage_outproj(S, OTin, wout, Hin, cur, lng, lnb, T_CORE, Kdim)
        for i in range(n_ffn):
            nxt = Hout if i == n_ffn - 1 else scratch[i + 1]
            f = fs[i]
            stage_ffn(S, cur, nxt, f["wg"], f["wu"], f["wd"], f["lng"], f["lnb"], T_CORE)
            cur = nxt
    return nc


def build_attn_prog(kind):
    nc = _new_nc()
    Hfull = _din(nc, "Hfull", [D, SEQ])
    wq = _din(nc, "wq", [D, 256])
    wk = _din(nc, "wk", [D, 256])
    wv = _din(nc, "wv", [D, 256])
    wf = _din(nc, "wf", [D, 4]) if kind == "fox" else None
    bfr = _din(nc, "bfr", [128, 4]) if kind == "fox" else None
    cst = _din(nc, "cst", [128, 384])
    oh = _din(nc, "oh", [32, 32 * 128]) if kind == "moba" else None
    OT = _dout(nc, "OT", [256, SEQ], BF16)
    with contextlib.ExitStack() as es:
        S = Sched(nc)
        S.setup(es)
        stage_attn(S, kind, Hfull, wq, wk, wv, wf, bfr, cst, oh, OT)
    return nc


def build_ssd_prog():
    nc = _new_nc()
    Hfull = _din(nc, "Hfull", [D, SEQ])
    wz = _din(nc, "wz", [D, 512])
    wx = _din(nc, "wx", [D, 512])
    wB = _din(nc, "wB", [D, 128])
    wC = _din(nc, "wC", [D, 128])
    wdt = _din(nc, "wdt", [D, 8])
    cw = _din(nc, "cw", [128, 6, 4])
    cbias = _din(nc, "cbias", [128, 6])
    dtb = _din(nc, "dtb", [128, 8])
    alog = _din(nc, "alog", [128, 8])
    dcol = _din(nc, "dcol", [128, 4])
    nw = _din(nc, "nw", [128, 4])
    cst = _din(nc, "cst", [128, 384])
    YT = _dout(nc, "OT", [512, SEQ], BF16)
    with contextlib.ExitStack() as es:
        S = Sched(nc)
        S.setup(es)
        stage_ssd(S, Hfull, wz, wx, wB, wC, wdt, cw, cbias, dtb, alog, dcol, nw, cst, YT)
    return nc


def _consts():
    ident = np.eye(128, dtype=np.float32)
    triU = np.triu(np.ones((128, 128), np.float32))
    trim = np.where(np.arange(128)[:, None] > np.arange(128)[None, :], NEG, 0.0).astype(np.float32)
    cst = np.ascontiguousarray(np.concatenate([ident, triU, trim], axis=1))
    oh = np.zeros((32, 32, 128), np.float32)
    for n in range(32):
        oh[n, n, :] = 1.0
    return cst, oh.reshape(32, 32 * 128)


def _c(a):
    return np.ascontiguousarray(a, dtype=np.float32)


def _pc(v):
    return _c(np.asarray(v).reshape(8, 128).T)


def _ffn_map(inp, layer, half, tag):
    return {"wg" + tag: _c(inp["ffn_w_gate"][layer, half]), "wu" + tag: _c(inp["ffn_w_up"][layer, half]),
            "wd" + tag: _c(inp["ffn_w_down"][layer, half]),
            "lng" + tag: _pc(inp["ln_g"][layer, 2 * half]), "lnb" + tag: _pc(inp["ln_b"][layer, 2 * half])}


def _ssd_maps(inp, j, g, cst):
    w_in = inp["ssm_w_in"][j]
    DI = 2048
    cwf = inp["ssm_conv_w"][j]
    cbf = inp["ssm_conv_b"][j]
    chans = np.concatenate([np.arange(g * 512, (g + 1) * 512), np.arange(DI + g * 128, DI + (g + 1) * 128),
                            np.arange(DI + 512 + g * 128, DI + 512 + (g + 1) * 128)])
    cw = cwf[:, chans].reshape(4, 6, 128).transpose(2, 1, 0)
    cb = cbf[chans].reshape(6, 128).T
    hs = slice(g * 8, (g + 1) * 8)
    rep = lambda v: np.broadcast_to(np.asarray(v)[None, :], (128, len(v)))
    dcol = np.repeat(inp["ssm_d"][j][hs], 64).reshape(4, 128).T
    nw = inp["ssm_norm_w"][j][g * 512:(g + 1) * 512].reshape(4, 128).T
    m = dict(wz=w_in[:, g * 512:(g + 1) * 512], wx=w_in[:, DI + g * 512:DI + (g + 1) * 512],
             wB=w_in[:, 2 * DI + g * 128:2 * DI + (g + 1) * 128], wC=w_in[:, 2 * DI + 512 + g * 128:2 * DI + 512 + (g + 1) * 128],
             wdt=w_in[:, 2 * DI + 1024 + g * 8:2 * DI + 1024 + (g + 1) * 8],
             cw=cw, cbias=cb, dtb=rep(inp["ssm_dt_bias"][j][hs]), alog=rep(inp["ssm_a_log"][j][hs]), dcol=dcol, nw=nw, cst=cst)
    return {k: _c(v) for k, v in m.items()}


def _attn_maps(inp, kind, g, cst, oh):
    w_in = inp["fox_w_in"][0] if kind == "fox" else inp["moba_w_in"][0]
    m = dict(wq=_c(w_in[:, g * 256:(g + 1) * 256]), wk=_c(w_in[:, 1024 + g * 256:1024 + (g + 1) * 256]),
             wv=_c(w_in[:, 2048 + g * 256:2048 + (g + 1) * 256]), cst=cst)
    if kind == "fox":
        m["wf"] = _c(w_in[:, 3072 + g * 4:3072 + (g + 1) * 4])
        m["bfr"] = _c(np.broadcast_to(inp["fox_b_f"][0][g * 4:(g + 1) * 4][None, :], (128, 4)))
    else:
        m["oh"] = oh
    return m


def _run(nc, in_maps):
    res = run_bass_kernel_spmd(nc, in_maps, core_ids=list(range(NCORES)))
    return res.results


_DBG = None


def kernel(**inputs):
    inp = {k: np.asarray(v) for k, v in inputs.items()}
    x = inp["x"]
    cst, oh = _consts()
    progs = {}

    def prog(key, builder):
        if key not in progs:
            progs[key] = builder()
        return progs[key]

    H = [_c(x[c // 4, (c % 4) * T_CORE:(c % 4 + 1) * T_CORE].T) for c in range(NCORES)]
    maps = []
    for c in range(NCORES):
        m = {"Hin": H[c]}
        m.update(_ffn_map(inp, 0, 0, "0"))
        maps.append(m)
    r = _run(prog("ffn", build_ffn_prog), maps)
    H = [r[c]["Hout"] for c in range(NCORES)]
    if _DBG is not None:
        _DBG("A", H)
    for layer in range(4):
        kindi, j = layer % 3, layer // 3
        kind = ("ssd", "fox", "moba")[kindi]
        Hfull = [_c(np.concatenate([H[b * 4 + q] for q in range(4)], axis=1)) for b in range(2)]
        maps = []
        for c in range(NCORES):
            b, g = c // 4, c % 4
            m = _ssd_maps(inp, j, g, cst) if kind == "ssd" else _attn_maps(inp, kind, g, cst, oh)
            m["Hfull"] = Hfull[b]
            maps.append(m)
        r = _run(prog(kind, build_ssd_prog if kind == "ssd" else (lambda kind=kind: build_attn_prog(kind))), maps)
        Kdim = 2048 if kind == "ssd" else 1024
        Ofull = [np.concatenate([r[b * 4 + g]["OT"] for g in range(4)], axis=0) for b in range(2)]
        w_out = {"ssd": inp["ssm_w_out"], "fox": inp["fox_w_out"], "moba": inp["moba_w_out"]}[kind][j]
        if _DBG is not None:
            _DBG(("mix", layer), (Ofull, w_out))
        n_ffn = 2 if layer < 3 else 1
        maps = []
        for c in range(NCORES):
            b, q = c // 4, c % 4
            m = {"Hin": H[c], "OTin": np.ascontiguousarray(Ofull[b][:, q * T_CORE:(q + 1) * T_CORE]), "wout": _c(w_out),
                 "lngm": _pc(inp["ln_g"][layer, 1]), "lnbm": _pc(inp["ln_b"][layer, 1])}
            m.update(_ffn_map(inp, layer, 1, "0"))
            if n_ffn == 2:
                m.update(_ffn_map(inp, layer + 1, 0, "1"))
            maps.append(m)
        r = _run(prog(("post", Kdim, n_ffn), lambda Kdim=Kdim, n_ffn=n_ffn: build_post_prog(Kdim, n_ffn)), maps)
        H = [r[c]["Hout"] for c in range(NCORES)]
        if _DBG is not None:
            _DBG(("post", layer), H)
    out = np.empty((2, SEQ, D), np.float32)
    for c in range(NCORES):
        out[c // 4, (c % 4) * T_CORE:(c % 4 + 1) * T_CORE] = H[c].T
    return out
```

```python
import contextlib
import numpy as np
import concourse.bass as bass
import concourse.mybir as mybir
from concourse.bass_utils import run_bass_kernel_spmd

F32 = mybir.dt.float32
BF16 = mybir.dt.bfloat16
AF = mybir.ActivationFunctionType
ALU = mybir.AluOpType
AX = mybir.AxisListType

COMPUTE = ("pe", "act", "dve", "pool")
N_DMA_SEMS = 12


class _Op:
    __slots__ = ("eng", "fn", "deps", "is_dma", "ndma", "signal", "sem", "val", "prev")

    def __init__(self, eng, fn, is_dma, ndma):
        self.eng = eng
        self.fn = fn
        self.deps = set()
        self.is_dma = is_dma
        self.ndma = ndma
        self.signal = False
        self.sem = None
        self.val = None


class Sched:
    def __init__(self, nc):
        self.nc = nc
        self.ops = []
        self.last_w = {}
        self.readers = {}
        self.nstage = 0
        self.sfx = ""

    def new_stage(self):
        self.nstage += 1
        self.sfx = "_s%d" % self.nstage

    def op(self, eng, fn, reads=(), writes=(), dma=0):
        def _isps(k):
            return isinstance(k[0], str) and k[0].startswith("ps_")

        def _norm(k):
            return tuple(x for i, x in enumerate(k) if i == 0 or not isinstance(x, str)) if _isps(k) else k
        writes = [_norm(k) for k in writes] + [_norm(k) for k in reads if _isps(k)]
        reads = [k for k in reads if not _isps(k)]
        o = _Op(eng, fn, dma > 0, dma)
        idx = len(self.ops)
        for k in reads:
            w = self.last_w.get(k)
            if w is not None:
                o.deps.add(w)
        for k in writes:
            w = self.last_w.get(k)
            if w is not None:
                o.deps.add(w)
            for r in self.readers.get(k, ()):
                o.deps.add(r)
        for k in reads:
            self.readers.setdefault(k, []).append(idx)
        for k in writes:
            self.last_w[k] = idx
            self.readers[k] = []
        o.deps.discard(idx)
        self.ops.append(o)
        return idx

    def pe(self, fn, reads=(), writes=()):
        return self.op("pe", fn, reads, writes)

    def act(self, fn, reads=(), writes=()):
        return self.op("act", fn, reads, writes)

    def dve(self, fn, reads=(), writes=()):
        return self.op("dve", fn, reads, writes)

    def pool(self, fn, reads=(), writes=()):
        return self.op("pool", fn, reads, writes)

    def dma(self, fn, reads=(), writes=(), n=1, q="sp"):
        return self.op(q, fn, reads, writes, dma=n)

    def setup(self, es):
        nc = self.nc
        self.sems = {e: es.enter_context(nc.semaphore("s_" + e)) for e in COMPUTE}
        self.dsems = [es.enter_context(nc.semaphore("d%d" % i)) for i in range(N_DMA_SEMS)]
        self.cnt = {e: 0 for e in COMPUTE}
        self.dtot = [0] * N_DMA_SEMS
        self.rr = 0

    def emit(self):
        nc = self.nc
        ops = self.ops
        for o in ops:
            if o.eng == "pe":
                o.deps = {d for d in o.deps if not (ops[d].eng == "pe" and not ops[d].is_dma)}
        for o in ops:
            for d in o.deps:
                ops[d].signal = True
        engs = {"pe": nc.tensor, "act": nc.scalar, "dve": nc.vector, "pool": nc.gpsimd, "sp": nc.sync}
        by_eng = {e: [] for e in engs}
        for i, o in enumerate(ops):
            by_eng[o.eng].append(i)
        for e in COMPUTE:
            comp = [i for i in by_eng[e] if not ops[i].is_dma]
            if comp:
                ops[comp[-1]].signal = True
        sems, dsems, cnt, dtot = self.sems, self.dsems, self.cnt, self.dtot
        for o in ops:
            if o.is_dma:
                si = self.rr % N_DMA_SEMS
                self.rr += 1
                o.sem = ("d", si)
                o.prev = dtot[si]
                dtot[si] += 16 * o.ndma
                o.val = dtot[si]
            elif o.signal:
                cnt[o.eng] += 1
                o.sem = ("c", o.eng)
                o.val = cnt[o.eng]
        final = [(("c", e), cnt[e]) for e in COMPUTE] + [(("d", i), dtot[i]) for i in range(N_DMA_SEMS)]

        def semobj(s):
            return dsems[s[1]] if s[0] == "d" else sems[s[1]]

        def run_engine(ename, eng):
            waited = {}

            def wait(s, v):
                if v <= 0:
                    return
                if waited.get(s, 0) >= v:
                    return
                eng.wait_ge(semobj(s), v)
                waited[s] = v
            for i in by_eng[ename]:
                o = ops[i]
                for d in sorted(o.deps):
                    po = ops[d]
                    wait(po.sem, po.val)
                if o.is_dma:
                    wait(o.sem, o.prev)
                    insts = o.fn(eng)
                    if not isinstance(insts, (list, tuple)):
                        insts = [insts]
                    assert len(insts) == o.ndma, (len(insts), o.ndma)
                    for ins in insts:
                        ins.then_inc(semobj(o.sem), 16)
                else:
                    ins = o.fn(eng)
                    if o.signal:
                        ins.then_inc(semobj(o.sem), 1)
            for (s_, v_) in final:
                wait(s_, v_)

        with nc.Block() as block:
            @block.tensor
            def _(e):
                run_engine("pe", e)

            @block.scalar
            def _(e):
                run_engine("act", e)

            @block.vector
            def _(e):
                run_engine("dve", e)

            @block.gpsimd
            def _(e):
                run_engine("pool", e)

            @block.sync
            def _(e):
                run_engine("sp", e)
        self.ops = []
        self.last_w = {}
        self.readers = {}


D = 1024
DFF = 2816
NF = DFF // 128
LN_EPS = 1e-5
ALPHA = 8.0 ** 0.25


def ln_feature_major(S, x, keyx, ntile, gam, bet, ones_bf, scr, ps_s, ps_q, tag, out_bf=None, key_bf=None):
    ybf, sqb, mean, rstd, nmr, tmp = scr["ybf"], scr["sqb"], scr["mean"], scr["rstd"], scr["nmr"], scr["tmp"]
    for t in range(ntile):
        sl = slice(t * 512, (t + 1) * 512)
        for d in range(8):
            S.act(lambda e, d=d, sl=sl: e.activation(out=ybf[:, d, :], in_=x[:, d, sl], func=AF.Copy),
                  reads=[(keyx, d, t)], writes=[("ybf", d)])
            S.act(lambda e, d=d, sl=sl: e.activation(out=sqb[:, d, :], in_=x[:, d, sl], func=AF.Square),
                  reads=[(keyx, d, t)], writes=[("sqb", d)])
        for d in range(8):
            S.pe(lambda e, d=d: e.matmul(ps_s[:, :], lhsT=ones_bf[:, :], rhs=ybf[:, d, :], start=(d == 0), stop=(d == 7)),
                 reads=[("ybf", d), ("ones",)], writes=[("ps_s",)])
        for d in range(8):
            S.pe(lambda e, d=d: e.matmul(ps_q[:, :], lhsT=ones_bf[:, :], rhs=sqb[:, d, :], start=(d == 0), stop=(d == 7)),
                 reads=[("sqb", d), ("ones",)], writes=[("ps_q",)])
        S.dve(lambda e: e.tensor_scalar(out=mean[:, :], in0=ps_s[:, :], scalar1=1.0 / D, scalar2=None, op0=ALU.mult),
              reads=[("ps_s",)], writes=[("mean",)])
        S.dve(lambda e: e.tensor_tensor(out=nmr[:, :], in0=mean[:, :], in1=mean[:, :], op=ALU.mult),
              reads=[("mean",)], writes=[("nmr",)])
        S.dve(lambda e: e.scalar_tensor_tensor(out=rstd[:, :], in0=ps_q[:, :], scalar=1.0 / D, in1=nmr[:, :],
                                               op0=ALU.mult, op1=ALU.subtract),
              reads=[("ps_q",), ("nmr",)], writes=[("rstd",)])
        S.dve(lambda e: e.tensor_scalar(out=rstd[:, :], in0=rstd[:, :], scalar1=LN_EPS, scalar2=None, op0=ALU.add),
              reads=[("rstd",)], writes=[("rstd",)])
        S.act(lambda e: e.activation(out=rstd[:, :], in_=rstd[:, :], func=AF.Ln),
              reads=[("rstd",)], writes=[("rstd",)])
        S.act(lambda e: e.activation(out=rstd[:, :], in_=rstd[:, :], func=AF.Exp, scale=-0.5),
              reads=[("rstd",)], writes=[("rstd",)])
        S.dve(lambda e: e.scalar_tensor_tensor(out=nmr[:, :], in0=mean[:, :], scalar=-1.0, in1=rstd[:, :],
                                               op0=ALU.mult, op1=ALU.mult),
              reads=[("mean",), ("rstd",)], writes=[("nmr",)])
        for d in range(8):
            S.dve(lambda e, d=d, sl=sl: e.tensor_tensor(out=tmp[:, d % 2, :], in0=x[:, d, sl], in1=rstd[:, :], op=ALU.mult),
                  reads=[(keyx, d, t), ("rstd",)], writes=[("lntmp", d % 2)])
            S.pool(lambda e, d=d: e.tensor_tensor(out=tmp[:, d % 2, :], in0=tmp[:, d % 2, :], in1=nmr[:, :], op=ALU.add),
                   reads=[("lntmp", d % 2), ("nmr",)], writes=[("lntmp", d % 2)])
            S.act(lambda e, d=d, sl=sl: e.activation(out=x[:, d, sl], in_=tmp[:, d % 2, :], func=AF.Identity,
                                                     scale=gam[:, d:d + 1], bias=bet[:, d:d + 1]),
                  reads=[("lntmp", d % 2), ("lnp",)], writes=[(keyx, d, t)])
            if out_bf is not None:
                S.act(lambda e, d=d, sl=sl: e.activation(out=out_bf[:, d, sl], in_=tmp[:, d % 2, :], func=AF.Identity,
                                                         scale=gam[:, d:d + 1], bias=bet[:, d:d + 1]),
                      reads=[("lntmp", d % 2), ("lnp",)], writes=[(key_bf, d, t)])


def stage_ffn(S, Hin, Hout, wg, wu, wd, lng, lnb, T, TP=1024):
    nc = S.nc
    S.new_stage()
    npass = T // TP
    ntile = TP // 512
    with contextlib.ExitStack() as es:
        sb = lambda name, shape, dt: es.enter_context(nc.sbuf_tensor(name + S.sfx, shape, dt))
        x = sb("x", [128, 8, TP], F32)
        xbf = sb("xbf", [128, 8, TP], BF16)
        hmid = sb("hmid", [128, NF, TP], BF16)
        stg = [sb("stg%d" % i, [128, 8, 256], F32) for i in range(2)]
        wgb = [sb("wgb%d" % i, [128, 8, 256], BF16) for i in range(2)]
        wub = [sb("wub%d" % i, [128, 8, 256], BF16) for i in range(2)]
        wdb = [sb("wdb%d" % i, [128, NF, 256], BF16) for i in range(2)]
        sg = [sb("sg%d" % i, [128, 512], F32) for i in range(2)]
        otmp = [sb("otmp%d" % i, [128, 512], F32) for i in range(2)]
        scr = dict(ybf=sb("ybf", [128, 8, 512], BF16), sqb=sb("sqb", [128, 8, 512], BF16),
                   mean=sb("mean", [128, 512], F32), rstd=sb("rstd", [128, 512], F32),
                   nmr=sb("nmr", [128, 512], F32), tmp=sb("lntmp", [128, 2, 512], F32))
        gam = sb("gam", [128, 8], F32)
        bet = sb("bet", [128, 8], F32)
        ones_bf = sb("ones_bf", [128, 128], BF16)
        ps_g = [es.enter_context(nc.psum_tensor("ps_g%d" % i + S.sfx, [128, 512], F32)) for i in range(2)]
        ps_u = [es.enter_context(nc.psum_tensor("ps_u%d" % i + S.sfx, [128, 512], F32)) for i in range(2)]
        ps_o = [es.enter_context(nc.psum_tensor("ps_o%d" % i + S.sfx, [128, 512], F32)) for i in range(2)]
        ps_s = es.enter_context(nc.psum_tensor("ps_s" + S.sfx, [128, 512], F32))
        ps_q = es.enter_context(nc.psum_tensor("ps_q" + S.sfx, [128, 512], F32))

        S.dma(lambda e: [e.dma_start(out=gam[:, :], in_=lng),
                         e.dma_start(out=bet[:, :], in_=lnb)],
              writes=[("lnp",)], n=2)
        S.pool(lambda e: e.memset(ones_bf[:, :], 1.0), writes=[("ones",)])
        Hin_v = Hin.rearrange("(c p) t -> p c t", p=128)
        Hout_v = Hout.rearrange("(c p) t -> p c t", p=128)
        wg_v = wg.rearrange("(c p) f -> p c f", p=128)
        wu_v = wu.rearrange("(c p) f -> p c f", p=128)
        wd_v = wd.rearrange("(c p) d -> p c d", p=128)
        stg_i = 0
        wi = 0
        gq = 0
        for p in range(npass):
            t0 = p * TP
            for c in range(8):
                S.dma(lambda e, c=c, t0=t0: e.dma_start(out=x[:, c, :], in_=Hin_v[:, c, t0:t0 + TP]),
                      writes=[("x", c, t) for t in range(ntile)])
                for t in range(ntile):
                    S.act(lambda e, c=c, t=t: e.activation(out=xbf[:, c, t * 512:(t + 1) * 512], in_=x[:, c, t * 512:(t + 1) * 512], func=AF.Copy),
                          reads=[("x", c, t)], writes=[("xbf", c, t)])
            for fg in range(NF // 2):
                f0 = fg * 256
                b = wi % 2
                wi += 1
                for (wv, wb, nm) in ((wg_v, wgb, "wgb"), (wu_v, wub, "wub")):
                    s = stg_i % 2
                    stg_i += 1
                    S.dma(lambda e, wv=wv, s=s, f0=f0: e.dma_start(out=stg[s][:, :, :], in_=wv[:, :, f0:f0 + 256]),
                          writes=[("stg", s)])
                    S.pool(lambda e, wb=wb, b=b, s=s: e.tensor_copy(out=wb[b][:, :, :], in_=stg[s][:, :, :]),
                           reads=[("stg", s)], writes=[(nm, b)])
                for fc in range(2):
                    f = fg * 2 + fc
                    for t in range(ntile):
                        q = gq % 2
                        gq += 1
                        sl = slice(t * 512, (t + 1) * 512)
                        for k in range(8):
                            S.pe(lambda e, q=q, b=b, k=k, fc=fc, sl=sl: e.matmul(
                                ps_g[q][:, :], lhsT=wgb[b][:, k, fc * 128:(fc + 1) * 128], rhs=xbf[:, k, sl],
                                start=(k == 0), stop=(k == 7)),
                                reads=[("wgb", b), ("xbf", k, t)], writes=[("ps_g", q)])
                        for k in range(8):
                            S.pe(lambda e, q=q, b=b, k=k, fc=fc, sl=sl: e.matmul(
                                ps_u[q][:, :], lhsT=wub[b][:, k, fc * 128:(fc + 1) * 128], rhs=xbf[:, k, sl],
                                start=(k == 0), stop=(k == 7)),
                                reads=[("wub", b), ("xbf", k, t)], writes=[("ps_u", q)])
                        S.act(lambda e, q=q: e.activation(out=sg[q][:, :], in_=ps_g[q][:, :], func=AF.Silu),
                              reads=[("ps_g", q)], writes=[("sg", q)])
                        S.dve(lambda e, q=q, f=f, sl=sl: e.tensor_tensor(out=hmid[:, f, sl], in0=sg[q][:, :], in1=ps_u[q][:, :], op=ALU.mult),
                              reads=[("sg", q), ("ps_u", q)], writes=[("hmid", f, t)])
            oq = 0
            for dg in range(4):
                d0 = dg * 256
                b = dg % 2
                for (c0, nch) in ((0, 8), (8, 8), (16, 6)):
                    s = stg_i % 2
                    stg_i += 1
                    S.dma(lambda e, s=s, c0=c0, nch=nch, d0=d0: e.dma_start(out=stg[s][:, 0:nch, :], in_=wd_v[:, c0:c0 + nch, d0:d0 + 256]),
                          writes=[("stg", s)])
                    S.pool(lambda e, b=b, s=s, c0=c0, nch=nch: e.tensor_copy(out=wdb[b][:, c0:c0 + nch, :], in_=stg[s][:, 0:nch, :]),
                           reads=[("stg", s)], writes=[("wdb", b, c0)])
                for dc in range(2):
                    d = dg * 2 + dc
                    for t in range(ntile):
                        q = oq % 2
                        oq += 1
                        sl = slice(t * 512, (t + 1) * 512)
                        for f in range(NF):
                            S.pe(lambda e, q=q, b=b, f=f, dc=dc, sl=sl: e.matmul(
                                ps_o[q][:, :], lhsT=wdb[b][:, f, dc * 128:(dc + 1) * 128], rhs=hmid[:, f, sl],
                                start=(f == 0), stop=(f == NF - 1)),
                                reads=[("wdb", b, (f // 8) * 8), ("hmid", f, t)], writes=[("ps_o", q)])
                        S.act(lambda e, q=q: e.activation(out=otmp[q][:, :], in_=ps_o[q][:, :], func=AF.Copy, scale=0.5),
                              reads=[("ps_o", q)], writes=[("otmp", q)])
                        S.dve(lambda e, q=q, d=d, sl=sl: e.scalar_tensor_tensor(out=x[:, d, sl], in0=x[:, d, sl], scalar=ALPHA, in1=otmp[q][:, :],
                                                                                op0=ALU.mult, op1=ALU.add),
                              reads=[("otmp", q), ("x", d, t)], writes=[("x", d, t)])
            ln_feature_major(S, x, "x", ntile, gam, bet, ones_bf, scr, ps_s, ps_q, "ffn")
            for c in range(8):
                S.dma(lambda e, c=c, t0=t0: e.dma_start(out=Hout_v[:, c, t0:t0 + TP], in_=x[:, c, :]),
                      reads=[("x", c, t) for t in range(ntile)], writes=[("Hout", p, c)])
        S.emit()


SEQ = 8192
NBLK = SEQ // 128
NEG = -30000.0


def load_cast_weight(S, w_dram_view, dst_bf, stg, key, ncol, nrowchunks=8):
    for c0 in range(0, ncol, 256):
        w = min(256, ncol - c0)
        S.dma(lambda e, c0=c0, w=w: e.dma_start(out=stg[:, 0:nrowchunks, 0:w], in_=w_dram_view[:, :, c0:c0 + w]),
              writes=[("stg_shared",)])
        S.pool(lambda e, c0=c0, w=w: e.tensor_copy(out=dst_bf[:, :, c0:c0 + w], in_=stg[:, 0:nrowchunks, 0:w]),
               reads=[("stg_shared",)], writes=[(key,)])


def stage_attn(S, kind, Hfull, wq, wk, wv, wf, bfr, cst, oh, OT, dbg=None):
    nc = S.nc
    S.new_stage()
    fox = kind == "fox"
    with contextlib.ExitStack() as es:
        sb = lambda name, shape, dt: es.enter_context(nc.sbuf_tensor(name + S.sfx, shape, dt))
        pst = lambda name, shape, dt=F32: es.enter_context(nc.psum_tensor(name + S.sfx, shape, dt))
        QT = sb("QT", [128, 2, SEQ], BF16)
        KT = sb("KT", [128, 2, SEQ], BF16)
        V = sb("V", [128, NBLK, 4, 66], BF16)
        xt = [sb("xt%d" % i, [128, 8, 512], F32) for i in range(2)]
        xb = [sb("xb%d" % i, [128, 8, 512], BF16) for i in range(2)]
        stg = sb("stg", [128, 8, 256], F32)
        wqb = sb("wqb", [128, 8, 256], BF16)
        wkb = sb("wkb", [128, 8, 256], BF16)
        wvb = sb("wvb", [128, 8, 256], BF16)
        cf = sb("cf", [128, 384], F32)
        identb = sb("identb", [128, 128], BF16)
        trimb = sb("trimb", [128, 128], BF16)
        onesf = sb("onesf", [128, 128], F32)
        onecol = sb("onecol", [128, 1], F32)
        NSB = 4
        QTz = [[sb("QTz%d%d" % (e_, c_), [128, 512], BF16) for c_ in range(2)] for e_ in range(2)]
        Pt = [sb("Pt%d" % i, [128, 512], BF16) for i in range(NSB)]
        rc = sb("rc", [128, 512], F32)
        rb = sb("rb", [128, 512], F32)
        ot = [sb("ot%d" % i, [128, 512], BF16) for i in range(2)]
        ps_p = [pst("ps_p%d" % i, [128, 512]) for i in range(2)]
        ps_S = [pst("ps_S%d" % i, [128, 512]) for i in range(2)]
        ps_O = [pst("ps_O%d" % i, [128, 512]) for i in range(2)]
        ps_x = pst("ps_x", [128, 512])
        ps_y = pst("ps_y", [128, 512])
        if fox:
            wfb = sb("wfb", [128, 8, 4], BF16)
            wfs = sb("wfs", [128, 8, 4], F32)
            bft = sb("bft", [128, 4], F32)
            lf = sb("lf", [128, NBLK, 4], F32)
            hsA = sb("hsA", [128, NBLK, 4], F32)
            hsB = sb("hsB", [128, NBLK, 4], F32)
            tot = sb("tot", [128, NBLK, 4], F32)
            cum = sb("cum", [128, NBLK, 4], F32)
            ncum = sb("ncum", [128, NBLK, 4], F32)
            dg = [sb("dg%d" % i, [128, 128], F32) for i in range(2)]
            cqb = [sb("cqb%d" % i, [128, 512], F32) for i in range(2)]
            Sb = [sb("Sb%d" % i, [128, 512], F32) for i in range(NSB)]
        else:
            ohb = sb("ohb", [128, 32, 128], BF16)
            id30 = sb("id30", [128, 128], BF16)
            kmf = sb("kmf", [128, 2, 32], F32)
            kmb = sb("kmb", [128, 2, 32], BF16)
            G = sb("G", [128, 32], F32)
            mx = sb("mx", [128, 8], F32)
            selm = sb("selm", [128, 4, 32], BF16)
            selbT = [sb("selbT%d" % i, [128, 512], BF16) for i in range(2)]

        ident = cf[:, 0:128]
        triU = cf[:, 128:256]
        S.dma(lambda e: e.dma_start(out=cf[:, :], in_=cst), writes=[("cf",)])
        S.pool(lambda e: e.tensor_copy(out=identb[:, :], in_=cf[:, 0:128]), reads=[("cf",)], writes=[("identb",)])
        S.pool(lambda e: e.tensor_copy(out=trimb[:, :], in_=cf[:, 256:384]), reads=[("cf",)], writes=[("trimb",)])
        S.pool(lambda e: e.memset(onesf[:, :], 1.0), writes=[("onesf",)])
        S.pool(lambda e: e.memset(onecol[:, :], 1.0), writes=[("onecol",)])
        S.pool(lambda e: e.memset(V[:, :, :, 64:66], 1.0), writes=[("Vones",)])
        for e_ in range(2):
            for c_ in range(2):
                S.pool(lambda e, e_=e_, c_=c_: e.memset(QTz[e_][c_][:, :], 0.0), writes=[("QTz", e_, c_)])
        if not fox:
            for c_ in range(2):
                S.pool(lambda e, c_=c_: e.memset(selbT[c_][:, :], 0.0), writes=[("selbT", c_)])
        Hv = Hfull.rearrange("(c p) t -> p c t", p=128)
        load_cast_weight(S, wq.rearrange("(c p) f -> p c f", p=128), wqb, stg, "wqb", 256)
        load_cast_weight(S, wk.rearrange("(c p) f -> p c f", p=128), wkb, stg, "wkb", 256)
        load_cast_weight(S, wv.rearrange("(c p) f -> p c f", p=128), wvb, stg, "wvb", 256)
        if fox:
            S.dma(lambda e: [e.dma_start(out=wfs[:, :, :], in_=wf.rearrange("(c p) f -> p c f", p=128)),
                             e.dma_start(out=bft[:, :], in_=bfr)], writes=[("wfs",), ("bft",)], n=2)
            S.pool(lambda e: e.tensor_copy(out=wfb[:, :, :], in_=wfs[:, :, :]), reads=[("wfs",)], writes=[("wfb",)])
        else:
            stgf = stg[:, :, :].rearrange("p a b -> p (a b)")
            for hf in range(2):
                S.dma(lambda e, hf=hf: e.dma_start(out=stgf[:, :], in_=oh[:, hf * 2048:(hf + 1) * 2048]), writes=[("stg_shared",)])
                S.pool(lambda e, hf=hf: e.tensor_copy(out=ohb[:, hf * 16:(hf + 1) * 16, :], in_=stgf[:, :].rearrange("p (n m) -> p n m", m=128)),
                       reads=[("stg_shared",)], writes=[("ohb",)])
            S.pool(lambda e: e.tensor_scalar(out=id30[:, :], in0=cf[:, 0:128], scalar1=-NEG, scalar2=None, op0=ALU.mult),
                   reads=[("cf",)], writes=[("id30",)])

        pq = 0
        for tt in range(SEQ // 512):
            t0 = tt * 512
            b = tt % 2
            S.dma(lambda e, b=b, t0=t0: e.dma_start(out=xt[b][:, :, :], in_=Hv[:, :, t0:t0 + 512]), writes=[("xt", b)])
            S.pool(lambda e, b=b: e.tensor_copy(out=xb[b][:, :, :], in_=xt[b][:, :, :]), reads=[("xt", b)], writes=[("xb", b)])
            for (wb, wkey, dst, dkey, scale) in ((wqb, "wqb", QT, "QT", 0.125), (wkb, "wkb", KT, "KT", 1.0)):
                for hp in range(2):
                    q = pq % 2
                    pq += 1
                    for k in range(8):
                        S.pe(lambda e, q=q, wb=wb, k=k, hp=hp, b=b: e.matmul(
                            ps_p[q][:, :], lhsT=wb[:, k, hp * 128:(hp + 1) * 128], rhs=xb[b][:, k, :], start=(k == 0), stop=(k == 7)),
                            reads=[(wkey,), ("xb", b)], writes=[("ps_p", q)])
                    S.act(lambda e, q=q, dst=dst, hp=hp, t0=t0, scale=scale: e.activation(
                        out=dst[:, hp, t0:t0 + 512], in_=ps_p[q][:, :], func=AF.Copy, scale=scale),
                        reads=[("ps_p", q)], writes=[(dkey, hp, tt)])
            for s in range(4):
                blk = tt * 4 + s
                q = pq % 2
                pq += 1
                for k in range(8):
                    S.pe(lambda e, q=q, k=k, s=s, b=b: e.matmul(
                        ps_p[q][:, 0:256], lhsT=xb[b][:, k, s * 128:(s + 1) * 128], rhs=wvb[:, k, :], start=(k == 0), stop=(k == 7)),
                        reads=[("wvb",), ("xb", b)], writes=[("ps_p", q)])
                S.dve(lambda e, q=q, blk=blk: e.tensor_copy(out=V[:, blk, :, 0:64], in_=ps_p[q][:, 0:256].rearrange("p (h d) -> p h d", h=4)),
                      reads=[("ps_p", q)], writes=[("V", blk)])
                if fox:
                    q = pq % 2
                    pq += 1
                    for k in range(8):
                        S.pe(lambda e, q=q, k=k, s=s, b=b: e.matmul(
                            ps_p[q][:, 0:4], lhsT=xb[b][:, k, s * 128:(s + 1) * 128], rhs=wfb[:, k, :], start=(k == 0), stop=(k == 7)),
                            reads=[("wfb",), ("xb", b)], writes=[("ps_p", q)])
                    S.dve(lambda e, q=q, blk=blk: e.tensor_tensor(out=lf[:, blk, :], in0=ps_p[q][:, 0:4], in1=bft[:, :], op=ALU.add),
                          reads=[("ps_p", q), ("bft",)], writes=[("lf",)])
        if fox:
            lf2 = lf[:, :, :].rearrange("p n h -> p (n h)")
            S.act(lambda e: e.activation(out=lf2, in_=lf2, func=AF.Exp, scale=-1.0), reads=[("lf",)], writes=[("lf",)])
            S.act(lambda e: e.activation(out=lf2, in_=lf2, func=AF.Ln, bias=onecol[:, 0:1]), reads=[("lf",), ("onecol",)], writes=[("lf",)])
            S.dve(lambda e: e.tensor_scalar(out=lf2, in0=lf2, scalar1=-1.0, scalar2=None, op0=ALU.mult), reads=[("lf",)], writes=[("lf",)])
            S.pe(lambda e: e.matmul(ps_x[:, 0:256], lhsT=triU, rhs=lf2, start=True, stop=True), reads=[("lf",), ("cf",)], writes=[("ps_x",)])
            S.pe(lambda e: e.matmul(ps_y[:, 0:256], lhsT=onesf[:, :], rhs=lf2, start=True, stop=True), reads=[("lf",), ("onesf",)], writes=[("ps_y",)])
            S.dve(lambda e: e.tensor_copy(out=tot[:, :, :].rearrange("p n h -> p (n h)"), in_=ps_y[:, 0:256]), reads=[("ps_y",)], writes=[("tot",)])
            S.dve(lambda e: e.tensor_copy(out=hsA[:, :, :], in_=tot[:, :, :]), reads=[("tot",)], writes=[("hsA",)])
            A, B, ka, kb_ = hsA, hsB, "hsA", "hsB"
            for sft in (1, 2, 4, 8, 16, 32):
                S.dve(lambda e, A=A, B=B, sft=sft: e.tensor_tensor(out=B[:, sft:, :], in0=A[:, sft:, :], in1=A[:, 0:NBLK - sft, :], op=ALU.add),
                      reads=[(ka,)], writes=[(kb_,)])
                S.dve(lambda e, A=A, B=B, sft=sft: e.tensor_copy(out=B[:, 0:sft, :], in_=A[:, 0:sft, :]),
                      reads=[(ka,)], writes=[(kb_,)])
                A, B, ka, kb_ = B, A, kb_, ka
            S.dve(lambda e, A=A: e.tensor_tensor(out=cum[:, :, :], in0=A[:, :, :], in1=tot[:, :, :], op=ALU.subtract),
                  reads=[(ka,), ("tot",)], writes=[("cum",)])
            S.dve(lambda e: e.tensor_tensor(out=cum[:, :, :].rearrange("p n h -> p (n h)"), in0=cum[:, :, :].rearrange("p n h -> p (n h)"),
                                            in1=ps_x[:, 0:256], op=ALU.add),
                  reads=[("cum",), ("ps_x",)], writes=[("cum",)])
            S.dve(lambda e: e.tensor_scalar(out=ncum[:, :, :], in0=cum[:, :, :], scalar1=-1.0, scalar2=None, op0=ALU.mult),
                  reads=[("cum",)], writes=[("ncum",)])
        else:
            for hp in range(2):
                S.dve(lambda e, hp=hp: e.tensor_reduce(out=kmf[:, hp, :], in_=KT[:, hp, :].rearrange("p (n l) -> p n l", l=256),
                                                       axis=AX.X, op=ALU.add),
                      reads=[("KT", hp, tt) for tt in range(SEQ // 512)], writes=[("kmf",)])
            S.dve(lambda e: e.tensor_copy(out=kmb[:, :, :], in_=kmf[:, :, :]), reads=[("kmf",)], writes=[("kmb",)])

        if dbg is not None and fox:
            S.pool(lambda e: e.tensor_copy(out=Sb[0][:, :], in_=QT[:, 0, 512:1024]), reads=[("QT", 0, 1)], writes=[("Sb", 0)])
            S.pool(lambda e: e.tensor_copy(out=Sb[1][:, 0:128], in_=KT[:, 0, 256:384]), reads=[("KT", 0, 0)], writes=[("Sb", 1)])
            S.dma(lambda e: [e.dma_start(out=dbg[:, 1536:2048], in_=Sb[0][:, :]), e.dma_start(out=dbg[:, 2048:2176], in_=Sb[1][:, 0:128])],
                  reads=[("Sb", 0), ("Sb", 1)], writes=[("dbg3",)], n=2)
            S.dma(lambda e: [e.dma_start(out=dbg[:, 0:256], in_=cum[:, :, :].rearrange("p n h -> p (n h)")),
                             e.dma_start(out=dbg[:, 256:512], in_=lf[:, :, :].rearrange("p n h -> p (n h)"))],
                  reads=[("cum",), ("lf",)], writes=[("dbg",)], n=2)
        sqi = 0
        oqi = 0
        NT = SEQ // 512
        pending = []

        def drain(n):
            for _ in range(n):
                if pending:
                    pending.pop(0)[1]()

        def flush(kind=None):
            while pending and (kind is None or any(k == kind for k, _ in pending)):
                pending.pop(0)[1]()

        def prologue_steps(h, qt):
            hp = h // 2
            rows = slice((h % 2) * 64, (h % 2) * 64 + 64)
            q0 = qt * 512
            cb = qt % 2
            steps = []
            e_ = h % 2

            def st_q():
                S.pool(lambda e: e.tensor_copy(out=QTz[e_][cb][rows, :], in_=QT[rows, hp, q0:q0 + 512]),
                       reads=[("QT", hp, qt)], writes=[("QTz", e_, cb)])
            steps.append(st_q)
            if fox:
                for s_ in range(4):
                    blk = 4 * qt + s_

                    def st(s_=s_, blk=blk):
                        S.dve(lambda e: e.tensor_scalar(out=dg[s_ % 2][:, :], in0=ident, scalar1=cum[:, blk, h:h + 1], scalar2=None, op0=ALU.mult),
                              reads=[("cum",), ("cf",)], writes=[("dg", s_ % 2)])
                        S.pe(lambda e: e.matmul(ps_x[:, s_ * 128:(s_ + 1) * 128], lhsT=onesf[:, :], rhs=dg[s_ % 2][:, :], start=True, stop=True),
                             reads=[("dg", s_ % 2), ("onesf",)], writes=[("ps_x",)])
                    steps.append(st)

                def fin():
                    S.act(lambda e: e.activation(out=cqb[cb][:, :], in_=ps_x[:, :], func=AF.Copy), reads=[("ps_x",)], writes=[("cqb", cb)])
                steps.append(fin)
            else:
                for s_ in range(4):
                    qb = 4 * qt + s_
                    own = qb // 2

                    def st_a(s_=s_, own=own):
                        if own >= 3:
                            S.pe(lambda e: e.matmul(ps_x[:, s_ * 32:(s_ + 1) * 32], lhsT=QT[rows, hp, q0 + s_ * 128:q0 + (s_ + 1) * 128], rhs=kmb[rows, hp, :],
                                                    start=True, stop=True),
                                 reads=[("QT", hp, qt), ("kmb",)], writes=[("ps_x",)])

                    def st_b(s_=s_, own=own):
                        if own >= 3:
                            S.dve(lambda e: e.tensor_copy(out=G[:, 0:own], in_=ps_x[:, s_ * 32:s_ * 32 + own]), reads=[("ps_x",)], writes=[("G",)])
                            S.dve(lambda e: e.max(out=mx[:, :], in_=G[:, :]), reads=[("G",)], writes=[("mx",)])
                            S.dve(lambda e: e.tensor_scalar(out=selm[:, s_, :], in0=G[:, :], scalar1=mx[:, 2:3], scalar2=-1.0, op0=ALU.is_ge, op1=ALU.add),
                                  reads=[("G",), ("mx",)], writes=[("selm", s_)])
                            S.dve(lambda e: e.memset(selm[:, s_, own:own + 1], 0.0), writes=[("selm", s_)])
                        else:
                            S.dve(lambda e: e.memset(selm[:, s_, :], -1.0), writes=[("selm", s_)])
                            S.dve(lambda e: e.memset(selm[:, s_, 0:own + 1], 0.0), writes=[("selm", s_)])

                    def st_c(s_=s_):
                        S.pe(lambda e: e.matmul(ps_y[0:32, s_ * 128:(s_ + 1) * 128], lhsT=selm[:, s_, :], rhs=id30[:, :], start=True, stop=True),
                             reads=[("selm", s_), ("id30",)], writes=[("ps_y",)])
                    steps += [st_a, st_b, st_c]

                def fin():
                    S.act(lambda e: e.activation(out=selbT[cb][0:32, :], in_=ps_y[0:32, :], func=AF.Copy), reads=[("ps_y",)], writes=[("selbT", cb)])
                steps.append(fin)
            return [("pro", f) for f in steps]

        def finalize_steps(h, qt, oq):
            q0 = qt * 512

            def f1():
                S.dve(lambda e: e.reciprocal(out=rc[64:65, :], in_=ps_O[oq][64:65, :]), reads=[("ps_O", oq)], writes=[("rc",)])

            def f2():
                S.pe(lambda e: e.matmul(ps_x[0:64, :], lhsT=onesf[64:65, 0:64], rhs=rc[64:65, :], start=True, stop=True),
                     reads=[("rc",), ("onesf",)], writes=[("ps_x",)])

            def f3():
                S.act(lambda e: e.activation(out=rb[0:64, :], in_=ps_x[0:64, :], func=AF.Copy), reads=[("ps_x",)], writes=[("rb",)])

            def f4():
                S.dve(lambda e: e.tensor_tensor(out=ot[oq][0:64, :], in0=ps_O[oq][0:64, :], in1=rb[0:64, :], op=ALU.mult),
                      reads=[("ps_O", oq), ("rb",)], writes=[("ot", oq)])
                S.dma(lambda e: e.dma_start(out=OT[h * 64:(h + 1) * 64, q0:q0 + 512], in_=ot[oq][0:64, :]),
                      reads=[("ot", oq)], writes=[("OT", h, qt)])
            return [("fin", f) for f in (f1, f2, f3, f4)]

        for h in range(4):
            hp = h // 2
            rows = slice((h % 2) * 64, (h % 2) * 64 + 64)
            if not fox:
                S.dve(lambda e: e.memset(G[:, :], -1e30), writes=[("G",)])
            for _, f in prologue_steps(h, 0):
                f()
            for qt in range(NT):
                q0 = qt * 512
                cb = qt % 2
                if qt + 1 < NT:
                    pending.extend(prologue_steps(h, qt + 1))
                nkb = 4 * qt + 4
                oq = oqi % 2
                oqi += 1
                unit_sq = {}

                def emit_S(kb, qt=qt, q0=q0, cb=cb, h=h, hp=hp, rows=rows):
                    nonlocal sqi
                    c0 = max(0, kb - 4 * qt) * 128
                    diag = kb >= 4 * qt
                    sq = sqi % NSB
                    sqi += 1
                    unit_sq[kb] = sq
                    psS, psSk = ((ps_S[0], ("ps_S", 0)), (ps_S[1], ("ps_S", 1)), (ps_p[0], ("ps_p", 0)), (ps_p[1], ("ps_p", 1)))[sq]
                    last_s = not diag and fox
                    S.pe(lambda e: e.matmul(psS[:, c0:512], lhsT=KT[:, hp, kb * 128:(kb + 1) * 128], rhs=QTz[h % 2][cb][:, c0:512],
                                            start=True, stop=last_s),
                         reads=[("KT", hp, kb // 4), ("QTz", h % 2, cb)], writes=[psSk])
                    if not fox:
                        S.pe(lambda e: e.matmul(psS[:, c0:512], lhsT=ohb[:, kb // 2, :], rhs=selbT[cb][:, c0:512], start=False, stop=(not diag)),
                             reads=[("ohb",), ("selbT", cb)], writes=[psSk])
                    if diag:
                        S.pe(lambda e: e.matmul(psS[:, c0:c0 + 128], lhsT=identb[:, :], rhs=trimb[:, :], start=False, stop=True),
                             reads=[("identb",), ("trimb",)], writes=[psSk])
                    if fox:
                        S.dve(lambda e: e.tensor_tensor(out=Sb[sq][:, c0:512], in0=psS[:, c0:512], in1=cqb[cb][:, c0:512], op=ALU.add),
                              reads=[psSk, ("cqb", cb)], writes=[("Sb", sq)])
                        S.act(lambda e: e.activation(out=Pt[sq][:, c0:512], in_=Sb[sq][:, c0:512], func=AF.Exp, bias=ncum[:, kb, h:h + 1]),
                              reads=[("Sb", sq), ("ncum",)], writes=[("Pt", sq)])
                    else:
                        S.act(lambda e: e.activation(out=Pt[sq][:, c0:512], in_=psS[:, c0:512], func=AF.Exp),
                              reads=[psSk], writes=[("Pt", sq)])

                def emit_PV(kb, qt=qt, h=h, oq=oq, nkb=nkb):
                    c0 = max(0, kb - 4 * qt) * 128
                    sq = unit_sq[kb]
                    S.pe(lambda e: e.matmul(ps_O[oq][0:65, c0:512], lhsT=V[:, kb, h, 0:65], rhs=Pt[sq][:, c0:512], start=(kb == 0), stop=(kb == nkb - 1)),
                         reads=[("V", kb), ("Vones",), ("Pt", sq)], writes=[("ps_O", oq)])

                LOOK = NSB - 1
                for i in range(min(LOOK, nkb)):
                    emit_S(i)
                for i in range(nkb):
                    if i + LOOK < nkb:
                        emit_S(i + LOOK)
                    emit_PV(i)
                    drain(2)
                flush("pro")
                pending.extend(finalize_steps(h, qt, oq))
            flush()
        S.emit()


def stage_ssd(S, Hfull, wz, wx, wB, wC, wdt, cw, cbias, dtb, alog, dcol, nw, cst, YT, ntiles=SEQ // 512):
    nc = S.nc
    S.new_stage()
    with contextlib.ExitStack() as es:
        sb = lambda name, shape, dt: es.enter_context(nc.sbuf_tensor(name + S.sfx, shape, dt))
        pst = lambda name, shape, dt=F32: es.enter_context(nc.psum_tensor(name + S.sfx, shape, dt))
        xt = [sb("xt%d" % i, [128, 8, 512], F32) for i in range(2)]
        xb = [sb("xb%d" % i, [128, 8, 512], BF16) for i in range(2)]
        stg = sb("stg", [128, 8, 256], F32)
        wzb = sb("wzb", [128, 8, 512], BF16)
        wxb = sb("wxb", [128, 8, 512], BF16)
        wBb = sb("wBb", [128, 8, 128], BF16)
        wCb = sb("wCb", [128, 8, 128], BF16)
        wdts = sb("wdts", [128, 8, 8], F32)
        wdtb = sb("wdtb", [128, 8, 8], BF16)
        cf = sb("cf", [128, 384], F32)
        identb = sb("identb", [128, 128], BF16)
        trimb = sb("trimb", [128, 128], BF16)
        onesf = sb("onesf", [128, 128], F32)
        onesb = sb("onesb", [128, 128], BF16)
        onecol = sb("onecol", [128, 1], F32)
        cwt = sb("cwt", [128, 6, 4], F32)
        cbt = sb("cbt", [128, 6], F32)
        dtbt = sb("dtbt", [128, 8], F32)
        arep = sb("arep", [128, 8], F32)
        dct = sb("dct", [128, 4], F32)
        nwt = sb("nwt", [128, 4], F32)
        pre = sb("pre", [128, 6, 515], F32)
        acc = [sb("acc%d" % i, [128, 512], F32) for i in range(2)]
        xcf = sb("xcf", [128, 4, 512], F32)
        xcb = sb("xcb", [128, 4, 512], BF16)
        BTb = sb("BTb", [128, 512], BF16)
        CTb = sb("CTb", [128, 512], BF16)
        CTf = sb("CTf", [128, 512], F32)
        zs = sb("zs", [128, 4, 512], F32)
        xtokf = sb("xtokf", [128, 512], F32)
        Btokb = sb("Btokb", [128, 128], BF16)
        dtp = sb("dtp", [128, 8], F32)
        dtt = sb("dtt", [128, 8], F32)
        adt = sb("adt", [128, 8], F32)
        ncs = sb("ncs", [128, 8], F32)
        wtt = sb("wtt", [128, 8], F32)
        dtw = sb("dtw", [128, 8], F32)
        cbTf = sb("cbTf", [128, 128], F32)
        xdtp = sb("xdtp", [128, 8, 128], BF16)
        xdtw = sb("xdtw", [128, 512], BF16)
        arpa = sb("arpa", [128, 8, 128], F32)
        Efa = sb("Efa", [128, 8, 128], F32)
        Asb = sb("Asb", [128, 8, 128], F32)
        MTa = sb("MTa", [128, 8, 128], BF16)
        CTsa = sb("CTsa", [128, 8, 128], BF16)
        Sf = sb("Sf", [128, 512], F32)
        stf = sb("stf", [128, 512], F32)
        Sbp = sb("Sbp", [128, 8, 128], BF16)
        yf = sb("yf", [128, 512], F32)
        yg = sb("yg", [128, 4, 512], F32)
        sqb = sb("sqb", [128, 4, 512], BF16)
        rstd = sb("rstd", [128, 512], F32)
        ytmp = [sb("ytmp%d" % i, [128, 512], F32) for i in range(2)]
        yo = [sb("yo%d" % i, [128, 512], BF16) for i in range(2)]
        ps_p = [pst("ps_p%d" % i, [128, 512]) for i in range(2)]
        ps_y = [pst("ps_y%d" % i, [128, 512]) for i in range(4)]
        ps_a = pst("ps_a", [128, 512])
        ps_s = pst("ps_s", [128, 512])
        triU = cf[:, 128:256]

        S.dma(lambda e: e.dma_start(out=cf[:, :], in_=cst), writes=[("cf",)])
        S.pool(lambda e: e.tensor_copy(out=identb[:, :], in_=cf[:, 0:128]), reads=[("cf",)], writes=[("identb",)])
        S.pool(lambda e: e.tensor_copy(out=trimb[:, :], in_=cf[:, 256:384]), reads=[("cf",)], writes=[("trimb",)])
        S.pool(lambda e: e.memset(onesf[:, :], 1.0), writes=[("onesf",)])
        S.pool(lambda e: e.memset(onesb[:, :], 1.0), writes=[("onesb",)])
        S.pool(lambda e: e.memset(onecol[:, :], 1.0), writes=[("onecol",)])
        S.pool(lambda e: e.memset(pre[:, :, 0:3], 0.0), writes=[("pre", cc) for cc in range(6)])
        S.pool(lambda e: e.memset(Sf[:, :], 0.0), writes=[("Sf",)])
        S.pool(lambda e: e.memset(Sbp[:, :, :], 0.0), writes=[("Sbp",)])
        S.pool(lambda e: e.memset(xdtp[:, :, :], 0.0), writes=[("xdtp",)])
        S.dma(lambda e: [e.dma_start(out=cwt[:, :, :], in_=cw), e.dma_start(out=cbt[:, :], in_=cbias),
                         e.dma_start(out=dtbt[:, :], in_=dtb), e.dma_start(out=arep[:, :], in_=alog),
                         e.dma_start(out=dct[:, :], in_=dcol), e.dma_start(out=nwt[:, :], in_=nw),
                         e.dma_start(out=wdts[:, :, :], in_=wdt.rearrange("(c p) f -> p c f", p=128))],
              writes=[("prm",)], n=7)
        S.pool(lambda e: e.tensor_copy(out=wdtb[:, :, :], in_=wdts[:, :, :]), reads=[("prm",)], writes=[("wdtb",)])
        S.act(lambda e: e.activation(out=arep[:, :], in_=arep[:, :], func=AF.Exp), reads=[("prm",)], writes=[("arep",)])
        S.dve(lambda e: e.tensor_scalar(out=arep[:, :], in0=arep[:, :], scalar1=-1.0, scalar2=None, op0=ALU.mult), reads=[("arep",)], writes=[("arep",)])
        Hv = Hfull.rearrange("(c p) t -> p c t", p=128)
        load_cast_weight(S, wz.rearrange("(c p) f -> p c f", p=128), wzb, stg, "wzb", 512)
        load_cast_weight(S, wx.rearrange("(c p) f -> p c f", p=128), wxb, stg, "wxb", 512)
        load_cast_weight(S, wB.rearrange("(c p) f -> p c f", p=128), wBb, stg, "wBb", 128)
        load_cast_weight(S, wC.rearrange("(c p) f -> p c f", p=128), wCb, stg, "wCb", 128)

        pq = 0
        hq = 0
        for tt in range(ntiles):
            t0 = tt * 512
            b = tt % 2
            S.dma(lambda e, b=b, t0=t0: e.dma_start(out=xt[b][:, :, :], in_=Hv[:, :, t0:t0 + 512]), writes=[("xt", b)])
            S.pool(lambda e, b=b: e.tensor_copy(out=xb[b][:, :, :], in_=xt[b][:, :, :]), reads=[("xt", b)], writes=[("xb", b)])
            for hc in range(4):
                q = pq % 2
                pq += 1
                for k in range(8):
                    S.pe(lambda e, q=q, k=k, hc=hc, b=b: e.matmul(ps_p[q][:, :], lhsT=wzb[:, k, hc * 128:(hc + 1) * 128], rhs=xb[b][:, k, :],
                                                                 start=(k == 0), stop=(k == 7)),
                         reads=[("wzb",), ("xb", b)], writes=[("ps_p", q)])
                S.act(lambda e, q=q, hc=hc: e.activation(out=zs[:, hc, :], in_=ps_p[q][:, :], func=AF.Silu),
                      reads=[("ps_p", q)], writes=[("zs", hc)])
            for cc in range(6):
                q = pq % 2
                pq += 1
                if cc < 4:
                    wsel, wkey, csl = wxb, "wxb", slice(cc * 128, (cc + 1) * 128)
                elif cc == 4:
                    wsel, wkey, csl = wBb, "wBb", slice(0, 128)
                else:
                    wsel, wkey, csl = wCb, "wCb", slice(0, 128)
                for k in range(8):
                    S.pe(lambda e, q=q, k=k, wsel=wsel, csl=csl, b=b: e.matmul(ps_p[q][:, :], lhsT=wsel[:, k, csl], rhs=xb[b][:, k, :],
                                                                            start=(k == 0), stop=(k == 7)),
                         reads=[(wkey,), ("xb", b)], writes=[("ps_p", q)])
                S.dve(lambda e, q=q, cc=cc: e.tensor_copy(out=pre[:, cc, 3:515], in_=ps_p[q][:, :]), reads=[("ps_p", q)], writes=[("pre", cc)])
                a = acc[cc % 2]
                ak = ("acc", cc % 2)
                S.dve(lambda e, a=a, cc=cc: e.tensor_scalar(out=a[:, :], in0=pre[:, cc, 0:512], scalar1=cwt[:, cc, 0:1], scalar2=None, op0=ALU.mult),
                      reads=[("pre", cc), ("prm",)], writes=[ak])
                for kk in range(1, 4):
                    S.dve(lambda e, a=a, cc=cc, kk=kk: e.scalar_tensor_tensor(out=a[:, :], in0=pre[:, cc, kk:kk + 512], scalar=cwt[:, cc, kk:kk + 1],
                                                                             in1=a[:, :], op0=ALU.mult, op1=ALU.add),
                          reads=[("pre", cc), ak], writes=[ak])
                S.pool(lambda e, cc=cc: e.tensor_copy(out=pre[:, cc, 0:3], in_=pre[:, cc, 512:515]), reads=[("pre", cc)], writes=[("pre", cc)])
                if cc < 4:
                    S.act(lambda e, a=a, cc=cc: e.activation(out=xcf[:, cc, :], in_=a[:, :], func=AF.Silu, bias=cbt[:, cc:cc + 1]),
                          reads=[ak, ("prm",)], writes=[("xcf", cc)])
                    S.pool(lambda e, cc=cc: e.tensor_copy(out=xcb[:, cc, :], in_=xcf[:, cc, :]), reads=[("xcf", cc)], writes=[("xcb", cc)])
                elif cc == 4:
                    S.act(lambda e, a=a, cc=cc: e.activation(out=BTb[:, :], in_=a[:, :], func=AF.Silu, bias=cbt[:, cc:cc + 1]),
                          reads=[ak, ("prm",)], writes=[("BTb",)])
                else:
                    S.act(lambda e, a=a, cc=cc: e.activation(out=CTf[:, :], in_=a[:, :], func=AF.Silu, bias=cbt[:, cc:cc + 1]),
                          reads=[ak, ("prm",)], writes=[("CTf",)])
                    S.pool(lambda e: e.tensor_copy(out=CTb[:, :], in_=CTf[:, :]), reads=[("CTf",)], writes=[("CTb",)])
            for c in range(4):
                cs = slice(c * 128, (c + 1) * 128)
                for cc in range(4):
                    S.pe(lambda e, cc=cc, cs=cs: e.matmul(ps_p[0][:, cc * 128:(cc + 1) * 128], lhsT=xcb[:, cc, cs], rhs=identb[:, :], start=True, stop=True),
                         reads=[("xcb", cc), ("identb",)], writes=[("ps_p", 0)])
                S.pe(lambda e, cs=cs: e.matmul(ps_p[1][:, 0:128], lhsT=BTb[:, cs], rhs=identb[:, :], start=True, stop=True),
                     reads=[("BTb",), ("identb",)], writes=[("ps_p", 1)])
                S.act(lambda e: e.activation(out=xtokf[:, :], in_=ps_p[0][:, :], func=AF.Copy), reads=[("ps_p", 0)], writes=[("xtokf",)])
                S.dve(lambda e: e.tensor_copy(out=Btokb[:, :], in_=ps_p[1][:, 0:128]), reads=[("ps_p", 1)], writes=[("Btokb",)])
                for k in range(8):
                    S.pe(lambda e, k=k, cs=cs, b=b: e.matmul(ps_a[:, 384:392], lhsT=xb[b][:, k, cs], rhs=wdtb[:, k, :], start=(k == 0), stop=(k == 7)),
                         reads=[("xb", b), ("wdtb",)], writes=[("ps_a", "dt")])
                S.dve(lambda e: e.tensor_tensor(out=dtp[:, :], in0=ps_a[:, 384:392], in1=dtbt[:, :], op=ALU.add),
                      reads=[("ps_a", "dt"), ("prm",)], writes=[("dtp",)])
                S.act(lambda e: e.activation(out=dtp[:, :], in_=dtp[:, :], func=AF.Exp), reads=[("dtp",)], writes=[("dtp",)])
                S.act(lambda e: e.activation(out=dtt[:, :], in_=dtp[:, :], func=AF.Ln, bias=onecol[:, 0:1]), reads=[("dtp",), ("onecol",)], writes=[("dtt",)])
                S.dve(lambda e: e.tensor_tensor(out=adt[:, :], in0=dtt[:, :], in1=arep[:, :], op=ALU.mult), reads=[("dtt",), ("arep",)], writes=[("adt",)])
                S.pe(lambda e: e.matmul(ps_a[:, 392:400], lhsT=triU, rhs=adt[:, :], start=True, stop=True), reads=[("adt",), ("cf",)], writes=[("ps_a", "cs")])
                S.pe(lambda e: e.matmul(ps_a[:, 400:408], lhsT=onesf[:, :], rhs=adt[:, :], start=True, stop=True), reads=[("adt",), ("onesf",)], writes=[("ps_a", "tot")])
                S.dve(lambda e: e.tensor_scalar(out=ncs[:, :], in0=ps_a[:, 392:400], scalar1=-1.0, scalar2=None, op0=ALU.mult),
                      reads=[("ps_a", "cs")], writes=[("ncs",)])
                S.dve(lambda e: e.tensor_tensor(out=wtt[:, :], in0=ps_a[:, 400:408], in1=ncs[:, :], op=ALU.add),
                      reads=[("ps_a", "tot"), ("ncs",)], writes=[("wtt",)])
                S.act(lambda e: e.activation(out=wtt[:, :], in_=wtt[:, :], func=AF.Exp), reads=[("wtt",)], writes=[("wtt",)])
                S.dve(lambda e: e.tensor_tensor(out=dtw[:, :], in0=dtt[:, :], in1=wtt[:, :], op=ALU.mult), reads=[("dtt",), ("wtt",)], writes=[("dtw",)])
                S.pe(lambda e, cs=cs: e.matmul(ps_a[:, 256:384], lhsT=BTb[:, cs], rhs=CTb[:, cs], start=True, stop=True),
                     reads=[("BTb",), ("CTb",)], writes=[("ps_a", "cb")])
                S.dve(lambda e: e.tensor_tensor(out=cbTf[:, :], in0=ps_a[:, 256:384], in1=triU, op=ALU.mult), reads=[("ps_a", "cb"), ("cf",)], writes=[("cbTf",)])
                xv = xtokf[:, :].rearrange("p (r d) -> p r d", d=64)
                S.dve(lambda e, xv=xv: e.tensor_tensor(out=xdtp[:, 0:8:2, 0:64], in0=xv[:, 0:8:2, :],
                                                       in1=dtt[:, 0:8:2].unsqueeze(2).to_broadcast([128, 4, 64]), op=ALU.mult),
                      reads=[("xtokf",), ("dtt",)], writes=[("xdtp",)])
                S.pool(lambda e, xv=xv: e.tensor_tensor(out=xdtp[:, 1:8:2, 64:128], in0=xv[:, 1:8:2, :],
                                                        in1=dtt[:, 1:8:2].unsqueeze(2).to_broadcast([128, 4, 64]), op=ALU.mult),
                       reads=[("xtokf",), ("dtt",)], writes=[("xdtp",)])
                S.pool(lambda e, xv=xv: e.tensor_tensor(out=xdtw[:, :].rearrange("p (r d) -> p r d", d=64), in0=xv,
                                                        in1=dtw[:, :].unsqueeze(2).to_broadcast([128, 8, 64]), op=ALU.mult),
                       reads=[("xtokf",), ("dtw",)], writes=[("xdtw",)])
                S.pe(lambda e: e.matmul(ps_s[:, :], lhsT=Btokb[:, :], rhs=xdtw[:, :], start=True, stop=True),
                     reads=[("Btokb",), ("xdtw",)], writes=[("ps_s",)])
                S.act(lambda e: e.activation(out=stf[:, :], in_=ps_s[:, :], func=AF.Copy), reads=[("ps_s",)], writes=[("stf",)])
                S.dve(lambda e: e.tensor_copy(out=arpa[:, :, :], in_=adt[:, :].unsqueeze(2).to_broadcast([128, 8, 128])),
                      reads=[("adt",)], writes=[("arpa",)])
                for hh, (bank, bkey) in enumerate(((ps_p[1], ("ps_p", 1)), (ps_s, ("ps_s",)))):
                    for r4 in range(4):
                        r = hh * 4 + r4
                        S.pe(lambda e, bank=bank, r=r, r4=r4: e.matmul(bank[:, r4 * 128:(r4 + 1) * 128], lhsT=arpa[:, r, :], rhs=triU, start=True, stop=True),
                             reads=[("arpa",), ("cf",)], writes=[bkey])
                    hs4 = slice(hh * 4, hh * 4 + 4)
                    S.act(lambda e, bank=bank, hs4=hs4: e.activation(out=Efa[:, hs4, :].rearrange("p r l -> p (r l)"), in_=bank[:, :], func=AF.Exp),
                          reads=[bkey], writes=[("Efa", hh)])
                    S.act(lambda e, bank=bank, hs4=hs4: e.activation(out=Asb[:, hs4, :].rearrange("p r l -> p (r l)"), in_=bank[:, :], func=AF.Copy),
                          reads=[bkey], writes=[("Asb", hh)])
                S.dve(lambda e: e.tensor_tensor(out=Asb[:, :, :], in0=Asb[:, :, :], in1=ncs[:, :].unsqueeze(2).to_broadcast([128, 8, 128]), op=ALU.add),
                      reads=[("Asb", 0), ("Asb", 1), ("ncs",)], writes=[("Asb", 0), ("Asb", 1)])
                S.dve(lambda e: e.tensor_tensor(out=Asb[:, :, :], in0=Asb[:, :, :], in1=triU.unsqueeze(1).to_broadcast([128, 8, 128]), op=ALU.mult),
                      reads=[("Asb", 0), ("Asb", 1), ("cf",)], writes=[("Asb", 0), ("Asb", 1)])
                S.act(lambda e: e.activation(out=Asb[:, :, :].rearrange("p r l -> p (r l)"), in_=Asb[:, :, :].rearrange("p r l -> p (r l)"), func=AF.Exp),
                      reads=[("Asb", 0), ("Asb", 1)], writes=[("Asb", 0), ("Asb", 1)])
                S.dve(lambda e: e.tensor_tensor(out=MTa[:, :, :], in0=Asb[:, :, :], in1=cbTf[:, :].unsqueeze(1).to_broadcast([128, 8, 128]), op=ALU.mult),
                      reads=[("Asb", 0), ("Asb", 1), ("cbTf",)], writes=[("MTa",)])
                S.pool(lambda e, cs=cs: e.tensor_tensor(out=CTsa[:, :, :], in0=Efa[:, :, :], in1=CTf[:, cs].unsqueeze(1).to_broadcast([128, 8, 128]), op=ALU.mult),
                       reads=[("Efa", 0), ("Efa", 1), ("CTf",)], writes=[("CTsa",)])
                for r in range(8):
                    hc = r // 2
                    S.pe(lambda e, r=r, hc=hc, cs=cs: e.matmul(ps_y[hc][:, cs], lhsT=xdtp[:, r, :], rhs=MTa[:, r, :], start=(r % 2 == 0), stop=False),
                         reads=[("xdtp",), ("MTa",)], writes=[("ps_y", hc)])
                    S.pe(lambda e, r=r, hc=hc, cs=cs: e.matmul(ps_y[hc][:, cs], lhsT=Sbp[:, r, :], rhs=CTsa[:, r, :], start=False, stop=(r % 2 == 1)),
                         reads=[("Sbp",), ("CTsa",)], writes=[("ps_y", hc)])
                Sv = Sf[:, :].rearrange("p (r d) -> p r d", d=64)
                S.dve(lambda e, Sv=Sv: e.tensor_tensor(out=Sv, in0=Sv, in1=Efa[:, :, 127:128].to_broadcast([128, 8, 64]), op=ALU.mult),
                      reads=[("Sf",), ("Efa", 0), ("Efa", 1)], writes=[("Sf",)])
                S.dve(lambda e: e.tensor_tensor(out=Sf[:, :], in0=Sf[:, :], in1=stf[:, :], op=ALU.add),
                      reads=[("Sf",), ("stf",)], writes=[("Sf",)])
                S.act(lambda e, Sv=Sv: e.activation(out=Sbp[:, 0:8:2, 0:64], in_=Sv[:, 0:8:2, :], func=AF.Copy), reads=[("Sf",)], writes=[("Sbp",)])
                S.act(lambda e, Sv=Sv: e.activation(out=Sbp[:, 1:8:2, 64:128], in_=Sv[:, 1:8:2, :], func=AF.Copy), reads=[("Sf",)], writes=[("Sbp",)])
            for hc in range(4):
                S.dve(lambda e, hc=hc: e.tensor_scalar(out=yf[:, :], in0=xcf[:, hc, :], scalar1=dct[:, hc:hc + 1], scalar2=None, op0=ALU.mult),
                      reads=[("xcf", hc), ("prm",)], writes=[("yf",)])
                S.dve(lambda e, hc=hc: e.tensor_tensor(out=yf[:, :], in0=ps_y[hc][:, :], in1=yf[:, :], op=ALU.add),
                      reads=[("yf",), ("ps_y", hc)], writes=[("yf",)])
                S.dve(lambda e, hc=hc: e.tensor_tensor(out=yg[:, hc, :], in0=yf[:, :], in1=zs[:, hc, :], op=ALU.mult),
                      reads=[("yf",), ("zs", hc)], writes=[("yg", hc)])
                S.act(lambda e, hc=hc: e.activation(out=sqb[:, hc, :], in_=yg[:, hc, :], func=AF.Square), reads=[("yg", hc)], writes=[("sqb", hc)])
            for hc in range(4):
                S.pe(lambda e, hc=hc: e.matmul(ps_p[0][:, :], lhsT=onesb[:, :], rhs=sqb[:, hc, :], start=(hc == 0), stop=(hc == 3)),
                     reads=[("sqb", hc), ("onesb",)], writes=[("ps_p", 0)])
            S.dve(lambda e: e.tensor_scalar(out=rstd[:, :], in0=ps_p[0][:, :], scalar1=1.0 / 512, scalar2=LN_EPS, op0=ALU.mult, op1=ALU.add),
                  reads=[("ps_p", 0)], writes=[("rstd",)])
            S.act(lambda e: e.activation(out=rstd[:, :], in_=rstd[:, :], func=AF.Ln), reads=[("rstd",)], writes=[("rstd",)])
            S.act(lambda e: e.activation(out=rstd[:, :], in_=rstd[:, :], func=AF.Exp, scale=-0.5), reads=[("rstd",)], writes=[("rstd",)])
            for hc in range(4):
                yb = hc % 2
                S.dve(lambda e, hc=hc, yb=yb: e.tensor_tensor(out=ytmp[yb][:, :], in0=yg[:, hc, :], in1=rstd[:, :], op=ALU.mult),
                      reads=[("yg", hc), ("rstd",)], writes=[("ytmp", yb)])
                S.act(lambda e, hc=hc, yb=yb: e.activation(out=yo[yb][:, :], in_=ytmp[yb][:, :], func=AF.Copy, scale=nwt[:, hc:hc + 1]),
                      reads=[("ytmp", yb), ("prm",)], writes=[("yo", yb)])
                S.dma(lambda e, hc=hc, yb=yb, t0=t0: e.dma_start(out=YT[hc * 128:(hc + 1) * 128, t0:t0 + 512], in_=yo[yb][:, :]),
                      reads=[("yo", yb)], writes=[("YT", hc, tt)])
        S.emit()


def stage_outproj(S, OTin, wout, Hin, Hout, lng, lnb, T, Kdim, TP=1024):
    nc = S.nc
    S.new_stage()
    Kc = Kdim // 128
    npass = T // TP
    ntile = TP // 512
    with contextlib.ExitStack() as es:
        sb = lambda name, shape, dt: es.enter_context(nc.sbuf_tensor(name + S.sfx, shape, dt))
        x = sb("x", [128, 8, TP], F32)
        ob = sb("ob", [128, Kc, TP], BF16)
        wob = sb("wob", [128, Kc, 1024], BF16)
        stg = sb("stg", [128, 8, 256], F32)
        scr = dict(ybf=sb("ybf", [128, 8, 512], BF16), sqb=sb("sqb", [128, 8, 512], BF16),
                   mean=sb("mean", [128, 512], F32), rstd=sb("rstd", [128, 512], F32),
                   nmr=sb("nmr", [128, 512], F32), tmp=sb("lntmp", [128, 2, 512], F32))
        gam = sb("gam", [128, 8], F32)
        bet = sb("bet", [128, 8], F32)
        ones_bf = sb("ones_bf", [128, 128], BF16)
        ps_o = [es.enter_context(nc.psum_tensor("ps_o%d" % i + S.sfx, [128, 512], F32)) for i in range(2)]
        ps_s = es.enter_context(nc.psum_tensor("ps_s" + S.sfx, [128, 512], F32))
        ps_q = es.enter_context(nc.psum_tensor("ps_q" + S.sfx, [128, 512], F32))
        S.dma(lambda e: [e.dma_start(out=gam[:, :], in_=lng), e.dma_start(out=bet[:, :], in_=lnb)], writes=[("lnp",)], n=2)
        S.pool(lambda e: e.memset(ones_bf[:, :], 1.0), writes=[("ones",)])
        wv = wout.rearrange("(c p) d -> p c d", p=128)
        for k0 in range(0, Kc, 8):
            for c0 in range(0, 1024, 256):
                S.dma(lambda e, k0=k0, c0=c0: e.dma_start(out=stg[:, :, :], in_=wv[:, k0:k0 + 8, c0:c0 + 256]), writes=[("stg_shared",)])
                S.pool(lambda e, k0=k0, c0=c0: e.tensor_copy(out=wob[:, k0:k0 + 8, c0:c0 + 256], in_=stg[:, :, :]),
                       reads=[("stg_shared",)], writes=[("wob",)])
        Hin_v = Hin.rearrange("(c p) t -> p c t", p=128)
        Hout_v = Hout.rearrange("(c p) t -> p c t", p=128)
        O_v = OTin.rearrange("(c p) t -> p c t", p=128)
        oq = 0
        for p in range(npass):
            t0 = p * TP
            for c in range(8):
                S.dma(lambda e, c=c, t0=t0: e.dma_start(out=x[:, c, :], in_=Hin_v[:, c, t0:t0 + TP]),
                      writes=[("x", c, t) for t in range(ntile)])
            for c in range(Kc):
                S.dma(lambda e, c=c, t0=t0: e.dma_start(out=ob[:, c, :], in_=O_v[:, c, t0:t0 + TP]), writes=[("ob", c)])
            for d in range(8):
                for t in range(ntile):
                    q = oq % 2
                    oq += 1
                    sl = slice(t * 512, (t + 1) * 512)
                    for k in range(Kc):
                        S.pe(lambda e, q=q, k=k, d=d, sl=sl: e.matmul(ps_o[q][:, :], lhsT=wob[:, k, d * 128:(d + 1) * 128], rhs=ob[:, k, sl],
                                                                     start=(k == 0), stop=(k == Kc - 1)),
                             reads=[("wob",), ("ob", k)], writes=[("ps_o", q)])
                    S.dve(lambda e, q=q, d=d, sl=sl: e.scalar_tensor_tensor(out=x[:, d, sl], in0=x[:, d, sl], scalar=ALPHA, in1=ps_o[q][:, :],
                                                                           op0=ALU.mult, op1=ALU.add),
                          reads=[("ps_o", q), ("x", d, t)], writes=[("x", d, t)])
            ln_feature_major(S, x, "x", ntile, gam, bet, ones_bf, scr, ps_s, ps_q, "op")
            for c in range(8):
                S.dma(lambda e, c=c, t0=t0: e.dma_start(out=Hout_v[:, c, t0:t0 + TP], in_=x[:, c, :]),
                      reads=[("x", c, t) for t in range(ntile)], writes=[("Hout", p, c)])
        S.emit()


T_CORE = 2048
NCORES = 8


def _new_nc():
    return bass.Bass("TRN2", target_bir_lowering=False)


def _din(nc, name, shape, dt=F32):
    return nc.dram_tensor(name, list(shape), dt, kind="ExternalInput").ap()


def _dout(nc, name, shape, dt=F32):
    return nc.dram_tensor(name, list(shape), dt, kind="ExternalOutput").ap()


def _ffn_inputs(nc, tag):
    return dict(wg=_din(nc, "wg" + tag, [D, DFF]), wu=_din(nc, "wu" + tag, [D, DFF]), wd=_din(nc, "wd" + tag, [DFF, D]),
                lng=_din(nc, "lng" + tag, [128, 8]), lnb=_din(nc, "lnb" + tag, [128, 8]))


def build_ffn_prog():
    nc = _new_nc()
    Hin = _din(nc, "Hin", [D, T_CORE])
    f = _ffn_inputs(nc, "0")
    Hout = _dout(nc, "Hout", [D, T_CORE])
    with contextlib.ExitStack() as es:
        S = Sched(nc)
        S.setup(es)
        stage_ffn(S, Hin, Hout, f["wg"], f["wu"], f["wd"], f["lng"], f["lnb"], T_CORE)
    return nc


def build_post_prog(Kdim, n_ffn):
    nc = _new_nc()
    Hin = _din(nc, "Hin", [D, T_CORE])
    OTin = _din(nc, "OTin", [Kdim, T_CORE], BF16)
    wout = _din(nc, "wout", [Kdim, D])
    lng = _din(nc, "lngm", [128, 8])
    lnb = _din(nc, "lnbm", [128, 8])
    fs = [_ffn_inputs(nc, str(i)) for i in range(n_ffn)]
    Hout = _dout(nc, "Hout", [D, T_CORE])
    scratch = [nc.dram_tensor("hscr%d" % i, [D, T_CORE], F32).ap() for i in range(n_ffn)]
    with contextlib.ExitStack() as es:
        S = Sched(nc)
        S.setup(es)
        cur = scratch[0] if n_ffn > 0 else Hout
        stage_outproj(S, OTin, wout, Hin, cur, lng, lnb, T_CORE, Kdim)
        for i in range(n_ffn):
            nxt = Hout if i == n_ffn - 1 else scratch[i + 1]
            f = fs[i]
            stage_ffn(S, cur, nxt, f["wg"], f["wu"], f["wd"], f["lng"], f["lnb"], T_CORE)
            cur = nxt
    return nc


def build_attn_prog(kind):
    nc = _new_nc()
    Hfull = _din(nc, "Hfull", [D, SEQ])
    wq = _din(nc, "wq", [D, 256])
    wk = _din(nc, "wk", [D, 256])
    wv = _din(nc, "wv", [D, 256])
    wf = _din(nc, "wf", [D, 4]) if kind == "fox" else None
    bfr = _din(nc, "bfr", [128, 4]) if kind == "fox" else None
    cst = _din(nc, "cst", [128, 384])
    oh = _din(nc, "oh", [128, 32 * 128]) if kind == "moba" else None
    OT = _dout(nc, "OT", [256, SEQ], BF16)
    with contextlib.ExitStack() as es:
        S = Sched(nc)
        S.setup(es)
        stage_attn(S, kind, Hfull, wq, wk, wv, wf, bfr, cst, oh, OT)
    return nc


def build_ssd_prog():
    nc = _new_nc()
    Hfull = _din(nc, "Hfull", [D, SEQ])
    wz = _din(nc, "wz", [D, 512])
    wx = _din(nc, "wx", [D, 512])
    wB = _din(nc, "wB", [D, 128])
    wC = _din(nc, "wC", [D, 128])
    wdt = _din(nc, "wdt", [D, 8])
    cw = _din(nc, "cw", [128, 6, 4])
    cbias = _din(nc, "cbias", [128, 6])
    dtb = _din(nc, "dtb", [128, 8])
    alog = _din(nc, "alog", [128, 8])
    dcol = _din(nc, "dcol", [128, 4])
    nw = _din(nc, "nw", [128, 4])
    cst = _din(nc, "cst", [128, 384])
    YT = _dout(nc, "OT", [512, SEQ], BF16)
    with contextlib.ExitStack() as es:
        S = Sched(nc)
        S.setup(es)
        stage_ssd(S, Hfull, wz, wx, wB, wC, wdt, cw, cbias, dtb, alog, dcol, nw, cst, YT)
    return nc


def _consts():
    ident = np.eye(128, dtype=np.float32)
    triU = np.triu(np.ones((128, 128), np.float32))
    trim = np.where(np.arange(128)[:, None] > np.arange(128)[None, :], NEG, 0.0).astype(np.float32)
    cst = np.ascontiguousarray(np.concatenate([ident, triU, trim], axis=1))
    oh = np.zeros((128, 32, 128), np.float32)
    for n in range(32):
        oh[n, n, :] = 1.0
    return cst, oh.reshape(128, 32 * 128)


def _c(a):
    return np.ascontiguousarray(a, dtype=np.float32)


def _pc(v):
    return _c(np.asarray(v).reshape(8, 128).T)


def _ffn_map(inp, layer, half, tag):
    return {"wg" + tag: _c(inp["ffn_w_gate"][layer, half]), "wu" + tag: _c(inp["ffn_w_up"][layer, half]),
            "wd" + tag: _c(inp["ffn_w_down"][layer, half]),
            "lng" + tag: _pc(inp["ln_g"][layer, 2 * half]), "lnb" + tag: _pc(inp["ln_b"][layer, 2 * half])}


def _ssd_maps(inp, j, g, cst):
    w_in = inp["ssm_w_in"][j]
    DI = 2048
    cwf = inp["ssm_conv_w"][j]
    cbf = inp["ssm_conv_b"][j]
    chans = np.concatenate([np.arange(g * 512, (g + 1) * 512), np.arange(DI + g * 128, DI + (g + 1) * 128),
                            np.arange(DI + 512 + g * 128, DI + 512 + (g + 1) * 128)])
    cw = cwf[:, chans].reshape(4, 6, 128).transpose(2, 1, 0)
    cb = cbf[chans].reshape(6, 128).T
    hs = slice(g * 8, (g + 1) * 8)
    rep = lambda v: np.broadcast_to(np.asarray(v)[None, :], (128, len(v)))
    dcol = np.repeat(inp["ssm_d"][j][hs], 64).reshape(4, 128).T
    nw = inp["ssm_norm_w"][j][g * 512:(g + 1) * 512].reshape(4, 128).T
    m = dict(wz=w_in[:, g * 512:(g + 1) * 512], wx=w_in[:, DI + g * 512:DI + (g + 1) * 512],
             wB=w_in[:, 2 * DI + g * 128:2 * DI + (g + 1) * 128], wC=w_in[:, 2 * DI + 512 + g * 128:2 * DI + 512 + (g + 1) * 128],
             wdt=w_in[:, 2 * DI + 1024 + g * 8:2 * DI + 1024 + (g + 1) * 8],
             cw=cw, cbias=cb, dtb=rep(inp["ssm_dt_bias"][j][hs]), alog=rep(inp["ssm_a_log"][j][hs]), dcol=dcol, nw=nw, cst=cst)
    return {k: _c(v) for k, v in m.items()}


def _attn_maps(inp, kind, g, cst, oh):
    w_in = inp["fox_w_in"][0] if kind == "fox" else inp["moba_w_in"][0]
    m = dict(wq=_c(w_in[:, g * 256:(g + 1) * 256]), wk=_c(w_in[:, 1024 + g * 256:1024 + (g + 1) * 256]),
             wv=_c(w_in[:, 2048 + g * 256:2048 + (g + 1) * 256]), cst=cst)
    if kind == "fox":
        m["wf"] = _c(w_in[:, 3072 + g * 4:3072 + (g + 1) * 4])
        m["bfr"] = _c(np.broadcast_to(inp["fox_b_f"][0][g * 4:(g + 1) * 4][None, :], (128, 4)))
    else:
        m["oh"] = oh
    return m


def _run(nc, in_maps):
    res = run_bass_kernel_spmd(nc, in_maps, core_ids=list(range(NCORES)))
    return res.results


_DBG = None


def kernel(**inputs):
    inp = {k: np.asarray(v) for k, v in inputs.items()}
    x = inp["x"]
    cst, oh = _consts()
    progs = {}

    def prog(key, builder):
        if key not in progs:
            progs[key] = builder()
        return progs[key]

    H = [_c(x[c // 4, (c % 4) * T_CORE:(c % 4 + 1) * T_CORE].T) for c in range(NCORES)]
    maps = []
    for c in range(NCORES):
        m = {"Hin": H[c]}
        m.update(_ffn_map(inp, 0, 0, "0"))
        maps.append(m)
    r = _run(prog("ffn", build_ffn_prog), maps)
    H = [r[c]["Hout"] for c in range(NCORES)]
    if _DBG is not None:
        _DBG("A", H)
    for layer in range(4):
        kindi, j = layer % 3, layer // 3
        kind = ("ssd", "fox", "moba")[kindi]
        Hfull = [_c(np.concatenate([H[b * 4 + q] for q in range(4)], axis=1)) for b in range(2)]
        maps = []
        for c in range(NCORES):
            b, g = c // 4, c % 4
            m = _ssd_maps(inp, j, g, cst) if kind == "ssd" else _attn_maps(inp, kind, g, cst, oh)
            m["Hfull"] = Hfull[b]
            maps.append(m)
        r = _run(prog(kind, build_ssd_prog if kind == "ssd" else (lambda kind=kind: build_attn_prog(kind))), maps)
        Kdim = 2048 if kind == "ssd" else 1024
        Ofull = [np.concatenate([r[b * 4 + g]["OT"] for g in range(4)], axis=0) for b in range(2)]
        w_out = {"ssd": inp["ssm_w_out"], "fox": inp["fox_w_out"], "moba": inp["moba_w_out"]}[kind][j]
        if _DBG is not None:
            _DBG(("mix", layer), (Ofull, w_out))
        n_ffn = 2 if layer < 3 else 1
        maps = []
        for c in range(NCORES):
            b, q = c // 4, c % 4
            m = {"Hin": H[c], "OTin": np.ascontiguousarray(Ofull[b][:, q * T_CORE:(q + 1) * T_CORE]), "wout": _c(w_out),
                 "lngm": _pc(inp["ln_g"][layer, 1]), "lnbm": _pc(inp["ln_b"][layer, 1])}
            m.update(_ffn_map(inp, layer, 1, "0"))
            if n_ffn == 2:
                m.update(_ffn_map(inp, layer + 1, 0, "1"))
            maps.append(m)
        r = _run(prog(("post", Kdim, n_ffn), lambda Kdim=Kdim, n_ffn=n_ffn: build_post_prog(Kdim, n_ffn)), maps)
        H = [r[c]["Hout"] for c in range(NCORES)]
        if _DBG is not None:
            _DBG(("post", layer), H)
    out = np.empty((2, SEQ, D), np.float32)
    for c in range(NCORES):
        out[c // 4, (c % 4) * T_CORE:(c % 4 + 1) * T_CORE] = H[c].T
    return out
```

```python
import contextlib
import numpy as np
import concourse.bass as bass
import concourse.mybir as mybir
from concourse.bass_utils import run_bass_kernel_spmd

F32 = mybir.dt.float32
BF16 = mybir.dt.bfloat16
AF = mybir.ActivationFunctionType
ALU = mybir.AluOpType
AX = mybir.AxisListType

COMPUTE = ("pe", "act", "dve", "pool")
N_DMA_SEMS = 12


class _Op:
    __slots__ = ("eng", "fn", "deps", "is_dma", "ndma", "signal", "sem", "val", "prev")

    def __init__(self, eng, fn, is_dma, ndma):
        self.eng = eng
        self.fn = fn
        self.deps = set()
        self.is_dma = is_dma
        self.ndma = ndma
        self.signal = False
        self.sem = None
        self.val = None


class Sched:
    def __init__(self, nc):
        self.nc = nc
        self.ops = []
        self.last_w = {}
        self.readers = {}
        self.nstage = 0
        self.sfx = ""

    def new_stage(self):
        self.nstage += 1
        self.sfx = "_s%d" % self.nstage

    def op(self, eng, fn, reads=(), writes=(), dma=0):
        def _isps(k):
            return isinstance(k[0], str) and k[0].startswith("ps_")

        def _norm(k):
            return tuple(x for i, x in enumerate(k) if i == 0 or not isinstance(x, str)) if _isps(k) else k
        writes = [_norm(k) for k in writes] + [_norm(k) for k in reads if _isps(k)]
        reads = [k for k in reads if not _isps(k)]
        o = _Op(eng, fn, dma > 0, dma)
        idx = len(self.ops)
        for k in reads:
            w = self.last_w.get(k)
            if w is not None:
                o.deps.add(w)
        for k in writes:
            w = self.last_w.get(k)
            if w is not None:
                o.deps.add(w)
            for r in self.readers.get(k, ()):
                o.deps.add(r)
        for k in reads:
            self.readers.setdefault(k, []).append(idx)
        for k in writes:
            self.last_w[k] = idx
            self.readers[k] = []
        o.deps.discard(idx)
        self.ops.append(o)
        return idx

    def pe(self, fn, reads=(), writes=()):
        return self.op("pe", fn, reads, writes)

    def act(self, fn, reads=(), writes=()):
        return self.op("act", fn, reads, writes)

    def dve(self, fn, reads=(), writes=()):
        return self.op("dve", fn, reads, writes)

    def pool(self, fn, reads=(), writes=()):
        return self.op("pool", fn, reads, writes)

    def dma(self, fn, reads=(), writes=(), n=1, q="sp"):
        return self.op(q, fn, reads, writes, dma=n)

    def setup(self, es):
        nc = self.nc
        self.sems = {e: es.enter_context(nc.semaphore("s_" + e)) for e in COMPUTE}
        self.dsems = [es.enter_context(nc.semaphore("d%d" % i)) for i in range(N_DMA_SEMS)]
        self.cnt = {e: 0 for e in COMPUTE}
        self.dtot = [0] * N_DMA_SEMS
        self.rr = 0

    def emit(self):
        nc = self.nc
        ops = self.ops
        for o in ops:
            if o.eng == "pe":
                o.deps = {d for d in o.deps if not (ops[d].eng == "pe" and not ops[d].is_dma)}
        for o in ops:
            for d in o.deps:
                ops[d].signal = True
        engs = {"pe": nc.tensor, "act": nc.scalar, "dve": nc.vector, "pool": nc.gpsimd, "sp": nc.sync}
        by_eng = {e: [] for e in engs}
        for i, o in enumerate(ops):
            by_eng[o.eng].append(i)
        for e in COMPUTE:
            comp = [i for i in by_eng[e] if not ops[i].is_dma]
            if comp:
                ops[comp[-1]].signal = True
        sems, dsems, cnt, dtot = self.sems, self.dsems, self.cnt, self.dtot
        for o in ops:
            if o.is_dma:
                si = self.rr % N_DMA_SEMS
                self.rr += 1
                o.sem = ("d", si)
                o.prev = dtot[si]
                dtot[si] += 16 * o.ndma
                o.val = dtot[si]
            elif o.signal:
                cnt[o.eng] += 1
                o.sem = ("c", o.eng)
                o.val = cnt[o.eng]
        final = [(("c", e), cnt[e]) for e in COMPUTE] + [(("d", i), dtot[i]) for i in range(N_DMA_SEMS)]

        def semobj(s):
            return dsems[s[1]] if s[0] == "d" else sems[s[1]]

        def run_engine(ename, eng):
            waited = {}

            def wait(s, v):
                if v <= 0:
                    return
                if waited.get(s, 0) >= v:
                    return
                eng.wait_ge(semobj(s), v)
                waited[s] = v
            for i in by_eng[ename]:
                o = ops[i]
                for d in sorted(o.deps):
                    po = ops[d]
                    wait(po.sem, po.val)
                if o.is_dma:
                    wait(o.sem, o.prev)
                    insts = o.fn(eng)
                    if not isinstance(insts, (list, tuple)):
                        insts = [insts]
                    assert len(insts) == o.ndma, (len(insts), o.ndma)
                    for ins in insts:
                        ins.then_inc(semobj(o.sem), 16)
                else:
                    ins = o.fn(eng)
                    if o.signal:
                        ins.then_inc(semobj(o.sem), 1)
            for (s_, v_) in final:
                wait(s_, v_)

        with nc.Block() as block:
            @block.tensor
            def _(e):
                run_engine("pe", e)

            @block.scalar
            def _(e):
                run_engine("act", e)

            @block.vector
            def _(e):
                run_engine("dve", e)

            @block.gpsimd
            def _(e):
                run_engine("pool", e)

            @block.sync
            def _(e):
                run_engine("sp", e)
        self.ops = []
        self.last_w = {}
        self.readers = {}


D = 1024
DFF = 2816
NF = DFF // 128
LN_EPS = 1e-5
ALPHA = 8.0 ** 0.25


def ln_feature_major(S, x, keyx, ntile, gam, bet, ones_bf, scr, ps_s, ps_q, tag, out_bf=None, key_bf=None):
    ybf, sqb, mean, rstd, nmr, tmp = scr["ybf"], scr["sqb"], scr["mean"], scr["rstd"], scr["nmr"], scr["tmp"]
    for t in range(ntile):
        sl = slice(t * 512, (t + 1) * 512)
        for d in range(8):
            S.act(lambda e, d=d, sl=sl: e.activation(out=ybf[:, d, :], in_=x[:, d, sl], func=AF.Copy),
                  reads=[(keyx, d, t)], writes=[("ybf", d)])
            S.act(lambda e, d=d, sl=sl: e.activation(out=sqb[:, d, :], in_=x[:, d, sl], func=AF.Square),
                  reads=[(keyx, d, t)], writes=[("sqb", d)])
        for d in range(8):
            S.pe(lambda e, d=d: e.matmul(ps_s[:, :], lhsT=ones_bf[:, :], rhs=ybf[:, d, :], start=(d == 0), stop=(d == 7)),
                 reads=[("ybf", d), ("ones",)], writes=[("ps_s",)])
        for d in range(8):
            S.pe(lambda e, d=d: e.matmul(ps_q[:, :], lhsT=ones_bf[:, :], rhs=sqb[:, d, :], start=(d == 0), stop=(d == 7)),
                 reads=[("sqb", d), ("ones",)], writes=[("ps_q",)])
        S.dve(lambda e: e.tensor_scalar(out=mean[:, :], in0=ps_s[:, :], scalar1=1.0 / D, scalar2=None, op0=ALU.mult),
              reads=[("ps_s",)], writes=[("mean",)])
        S.dve(lambda e: e.tensor_tensor(out=nmr[:, :], in0=mean[:, :], in1=mean[:, :], op=ALU.mult),
              reads=[("mean",)], writes=[("nmr",)])
        S.dve(lambda e: e.scalar_tensor_tensor(out=rstd[:, :], in0=ps_q[:, :], scalar=1.0 / D, in1=nmr[:, :],
                                               op0=ALU.mult, op1=ALU.subtract),
              reads=[("ps_q",), ("nmr",)], writes=[("rstd",)])
        S.dve(lambda e: e.tensor_scalar(out=rstd[:, :], in0=rstd[:, :], scalar1=LN_EPS, scalar2=None, op0=ALU.add),
              reads=[("rstd",)], writes=[("rstd",)])
        S.act(lambda e: e.activation(out=rstd[:, :], in_=rstd[:, :], func=AF.Ln),
              reads=[("rstd",)], writes=[("rstd",)])
        S.act(lambda e: e.activation(out=rstd[:, :], in_=rstd[:, :], func=AF.Exp, scale=-0.5),
              reads=[("rstd",)], writes=[("rstd",)])
        S.dve(lambda e: e.scalar_tensor_tensor(out=nmr[:, :], in0=mean[:, :], scalar=-1.0, in1=rstd[:, :],
                                               op0=ALU.mult, op1=ALU.mult),
              reads=[("mean",), ("rstd",)], writes=[("nmr",)])
        for d in range(8):
            S.dve(lambda e, d=d, sl=sl: e.tensor_tensor(out=tmp[:, d % 2, :], in0=x[:, d, sl], in1=rstd[:, :], op=ALU.mult),
                  reads=[(keyx, d, t), ("rstd",)], writes=[("lntmp", d % 2)])
            S.pool(lambda e, d=d: e.tensor_tensor(out=tmp[:, d % 2, :], in0=tmp[:, d % 2, :], in1=nmr[:, :], op=ALU.add),
                   reads=[("lntmp", d % 2), ("nmr",)], writes=[("lntmp", d % 2)])
            S.act(lambda e, d=d, sl=sl: e.activation(out=x[:, d, sl], in_=tmp[:, d % 2, :], func=AF.Identity,
                                                     scale=gam[:, d:d + 1], bias=bet[:, d:d + 1]),
                  reads=[("lntmp", d % 2), ("lnp",)], writes=[(keyx, d, t)])
            if out_bf is not None:
                S.act(lambda e, d=d, sl=sl: e.activation(out=out_bf[:, d, sl], in_=tmp[:, d % 2, :], func=AF.Identity,
                                                         scale=gam[:, d:d + 1], bias=bet[:, d:d + 1]),
                      reads=[("lntmp", d % 2), ("lnp",)], writes=[(key_bf, d, t)])


def stage_ffn(S, Hin, Hout, wg, wu, wd, lng, lnb, T, TP=1024):
    nc = S.nc
    S.new_stage()
    npass = T // TP
    ntile = TP // 512
    with contextlib.ExitStack() as es:
        sb = lambda name, shape, dt: es.enter_context(nc.sbuf_tensor(name + S.sfx, shape, dt))
        x = sb("x", [128, 8, TP], F32)
        xbf = sb("xbf", [128, 8, TP], BF16)
        hmid = sb("hmid", [128, NF, TP], BF16)
        stg = [sb("stg%d" % i, [128, 8, 256], F32) for i in range(2)]
        wgb = [sb("wgb%d" % i, [128, 8, 256], BF16) for i in range(2)]
        wub = [sb("wub%d" % i, [128, 8, 256], BF16) for i in range(2)]
        wdb = [sb("wdb%d" % i, [128, NF, 256], BF16) for i in range(2)]
        sg = [sb("sg%d" % i, [128, 512], F32) for i in range(2)]
        otmp = [sb("otmp%d" % i, [128, 512], F32) for i in range(2)]
        scr = dict(ybf=sb("ybf", [128, 8, 512], BF16), sqb=sb("sqb", [128, 8, 512], BF16),
                   mean=sb("mean", [128, 512], F32), rstd=sb("rstd", [128, 512], F32),
                   nmr=sb("nmr", [128, 512], F32), tmp=sb("lntmp", [128, 2, 512], F32))
        gam = sb("gam", [128, 8], F32)
        bet = sb("bet", [128, 8], F32)
        ones_bf = sb("ones_bf", [128, 128], BF16)
        ps_g = [es.enter_context(nc.psum_tensor("ps_g%d" % i + S.sfx, [128, 512], F32)) for i in range(2)]
        ps_u = [es.enter_context(nc.psum_tensor("ps_u%d" % i + S.sfx, [128, 512], F32)) for i in range(2)]
        ps_o = [es.enter_context(nc.psum_tensor("ps_o%d" % i + S.sfx, [128, 512], F32)) for i in range(2)]
        ps_s = es.enter_context(nc.psum_tensor("ps_s" + S.sfx, [128, 512], F32))
        ps_q = es.enter_context(nc.psum_tensor("ps_q" + S.sfx, [128, 512], F32))

        S.dma(lambda e: [e.dma_start(out=gam[:, :], in_=lng),
                         e.dma_start(out=bet[:, :], in_=lnb)],
              writes=[("lnp",)], n=2)
        S.pool(lambda e: e.memset(ones_bf[:, :], 1.0), writes=[("ones",)])
        Hin_v = Hin.rearrange("(c p) t -> p c t", p=128)
        Hout_v = Hout.rearrange("(c p) t -> p c t", p=128)
        wg_v = wg.rearrange("(c p) f -> p c f", p=128)
        wu_v = wu.rearrange("(c p) f -> p c f", p=128)
        wd_v = wd.rearrange("(c p) d -> p c d", p=128)
        stg_i = 0
        wi = 0
        gq = 0
        for p in range(npass):
            t0 = p * TP
            for c in range(8):
                S.dma(lambda e, c=c, t0=t0: e.dma_start(out=x[:, c, :], in_=Hin_v[:, c, t0:t0 + TP]),
                      writes=[("x", c, t) for t in range(ntile)])
                for t in range(ntile):
                    S.act(lambda e, c=c, t=t: e.activation(out=xbf[:, c, t * 512:(t + 1) * 512], in_=x[:, c, t * 512:(t + 1) * 512], func=AF.Copy),
                          reads=[("x", c, t)], writes=[("xbf", c, t)])
            for fg in range(NF // 2):
                f0 = fg * 256
                b = wi % 2
                wi += 1
                for (wv, wb, nm) in ((wg_v, wgb, "wgb"), (wu_v, wub, "wub")):
                    s = stg_i % 2
                    stg_i += 1
                    S.dma(lambda e, wv=wv, s=s, f0=f0: e.dma_start(out=stg[s][:, :, :], in_=wv[:, :, f0:f0 + 256]),
                          writes=[("stg", s)])
                    S.pool(lambda e, wb=wb, b=b, s=s: e.tensor_copy(out=wb[b][:, :, :], in_=stg[s][:, :, :]),
                           reads=[("stg", s)], writes=[(nm, b)])
                for fc in range(2):
                    f = fg * 2 + fc
                    for t in range(ntile):
                        q = gq % 2
                        gq += 1
                        sl = slice(t * 512, (t + 1) * 512)
                        for k in range(8):
                            S.pe(lambda e, q=q, b=b, k=k, fc=fc, sl=sl: e.matmul(
                                ps_g[q][:, :], lhsT=wgb[b][:, k, fc * 128:(fc + 1) * 128], rhs=xbf[:, k, sl],
                                start=(k == 0), stop=(k == 7)),
                                reads=[("wgb", b), ("xbf", k, t)], writes=[("ps_g", q)])
                        for k in range(8):
                            S.pe(lambda e, q=q, b=b, k=k, fc=fc, sl=sl: e.matmul(
                                ps_u[q][:, :], lhsT=wub[b][:, k, fc * 128:(fc + 1) * 128], rhs=xbf[:, k, sl],
                                start=(k == 0), stop=(k == 7)),
                                reads=[("wub", b), ("xbf", k, t)], writes=[("ps_u", q)])
                        S.act(lambda e, q=q: e.activation(out=sg[q][:, :], in_=ps_g[q][:, :], func=AF.Silu),
                              reads=[("ps_g", q)], writes=[("sg", q)])
                        S.dve(lambda e, q=q, f=f, sl=sl: e.tensor_tensor(out=hmid[:, f, sl], in0=sg[q][:, :], in1=ps_u[q][:, :], op=ALU.mult),
                              reads=[("sg", q), ("ps_u", q)], writes=[("hmid", f, t)])
            oq = 0
            for dg in range(4):
                d0 = dg * 256
                b = dg % 2
                for (c0, nch) in ((0, 8), (8, 8), (16, 6)):
                    s = stg_i % 2
                    stg_i += 1
                    S.dma(lambda e, s=s, c0=c0, nch=nch, d0=d0: e.dma_start(out=stg[s][:, 0:nch, :], in_=wd_v[:, c0:c0 + nch, d0:d0 + 256]),
                          writes=[("stg", s)])
                    S.pool(lambda e, b=b, s=s, c0=c0, nch=nch: e.tensor_copy(out=wdb[b][:, c0:c0 + nch, :], in_=stg[s][:, 0:nch, :]),
                           reads=[("stg", s)], writes=[("wdb", b, c0)])
                for dc in range(2):
                    d = dg * 2 + dc
                    for t in range(ntile):
                        q = oq % 2
                        oq += 1
                        sl = slice(t * 512, (t + 1) * 512)
                        for f in range(NF):
                            S.pe(lambda e, q=q, b=b, f=f, dc=dc, sl=sl: e.matmul(
                                ps_o[q][:, :], lhsT=wdb[b][:, f, dc * 128:(dc + 1) * 128], rhs=hmid[:, f, sl],
                                start=(f == 0), stop=(f == NF - 1)),
                                reads=[("wdb", b, (f // 8) * 8), ("hmid", f, t)], writes=[("ps_o", q)])
                        S.act(lambda e, q=q: e.activation(out=otmp[q][:, :], in_=ps_o[q][:, :], func=AF.Copy, scale=0.5),
                              reads=[("ps_o", q)], writes=[("otmp", q)])
                        S.dve(lambda e, q=q, d=d, sl=sl: e.scalar_tensor_tensor(out=x[:, d, sl], in0=x[:, d, sl], scalar=ALPHA, in1=otmp[q][:, :],
                                                                                op0=ALU.mult, op1=ALU.add),
                              reads=[("otmp", q), ("x", d, t)], writes=[("x", d, t)])
            ln_feature_major(S, x, "x", ntile, gam, bet, ones_bf, scr, ps_s, ps_q, "ffn")
            for c in range(8):
                S.dma(lambda e, c=c, t0=t0: e.dma_start(out=Hout_v[:, c, t0:t0 + TP], in_=x[:, c, :]),
                      reads=[("x", c, t) for t in range(ntile)], writes=[("Hout", p, c)])
        S.emit()


SEQ = 8192
NBLK = SEQ // 128
NEG = -30000.0


def load_cast_weight(S, w_dram_view, dst_bf, stg, key, ncol, nrowchunks=8):
    for c0 in range(0, ncol, 256):
        w = min(256, ncol - c0)
        S.dma(lambda e, c0=c0, w=w: e.dma_start(out=stg[:, 0:nrowchunks, 0:w], in_=w_dram_view[:, :, c0:c0 + w]),
              writes=[("stg_shared",)])
        S.pool(lambda e, c0=c0, w=w: e.tensor_copy(out=dst_bf[:, :, c0:c0 + w], in_=stg[:, 0:nrowchunks, 0:w]),
               reads=[("stg_shared",)], writes=[(key,)])


def stage_attn(S, kind, Hfull, wq, wk, wv, wf, bfr, cst, oh, OT, dbg=None):
    nc = S.nc
    S.new_stage()
    fox = kind == "fox"
    with contextlib.ExitStack() as es:
        sb = lambda name, shape, dt: es.enter_context(nc.sbuf_tensor(name + S.sfx, shape, dt))
        pst = lambda name, shape, dt=F32: es.enter_context(nc.psum_tensor(name + S.sfx, shape, dt))
        QT = sb("QT", [128, 2, SEQ], BF16)
        KT = sb("KT", [128, 2, SEQ], BF16)
        V = sb("V", [128, NBLK, 4, 66], BF16)
        xt = [sb("xt%d" % i, [128, 8, 512], F32) for i in range(2)]
        xb = [sb("xb%d" % i, [128, 8, 512], BF16) for i in range(2)]
        stg = sb("stg", [128, 8, 256], F32)
        wqb = sb("wqb", [128, 8, 256], BF16)
        wkb = sb("wkb", [128, 8, 256], BF16)
        wvb = sb("wvb", [128, 8, 256], BF16)
        cf = sb("cf", [128, 384], F32)
        identb = sb("identb", [128, 128], BF16)
        trimb = sb("trimb", [128, 128], BF16)
        onesf = sb("onesf", [128, 128], F32)
        onecol = sb("onecol", [128, 1], F32)
        NSB = 4
        QTz = [[sb("QTz%d%d" % (e_, c_), [128, 512], BF16) for c_ in range(2)] for e_ in range(2)]
        Pt = [sb("Pt%d" % i, [128, 512], BF16) for i in range(NSB)]
        rc = sb("rc", [128, 512], F32)
        rb = sb("rb", [128, 512], F32)
        ot = [sb("ot%d" % i, [128, 512], BF16) for i in range(2)]
        ps_p = [pst("ps_p%d" % i, [128, 512]) for i in range(2)]
        ps_S = [pst("ps_S%d" % i, [128, 512]) for i in range(2)]
        ps_O = [pst("ps_O%d" % i, [128, 512]) for i in range(2)]
        ps_x = pst("ps_x", [128, 512])
        ps_y = pst("ps_y", [128, 512])
        if fox:
            wfb = sb("wfb", [128, 8, 4], BF16)
            wfs = sb("wfs", [128, 8, 4], F32)
            bft = sb("bft", [128, 4], F32)
            lf = sb("lf", [128, NBLK, 4], F32)
            hsA = sb("hsA", [128, NBLK, 4], F32)
            hsB = sb("hsB", [128, NBLK, 4], F32)
            tot = sb("tot", [128, NBLK, 4], F32)
            cum = sb("cum", [128, NBLK, 4], F32)
            ncum = sb("ncum", [128, NBLK, 4], F32)
            dg = [sb("dg%d" % i, [128, 128], F32) for i in range(2)]
            cqb = [sb("cqb%d" % i, [128, 512], F32) for i in range(2)]
            Sb = [sb("Sb%d" % i, [128, 512], F32) for i in range(NSB)]
        else:
            ohb = sb("ohb", [128, 32, 128], BF16)
            id30 = sb("id30", [128, 128], BF16)
            kmf = sb("kmf", [128, 2, 32], F32)
            kmb = sb("kmb", [128, 2, 32], BF16)
            G = sb("G", [128, 32], F32)
            mx = sb("mx", [128, 8], F32)
            selm = sb("selm", [128, 4, 32], BF16)
            selbT = [sb("selbT%d" % i, [128, 512], BF16) for i in range(2)]

        ident = cf[:, 0:128]
        triU = cf[:, 128:256]
        S.dma(lambda e: e.dma_start(out=cf[:, :], in_=cst), writes=[("cf",)])
        S.pool(lambda e: e.tensor_copy(out=identb[:, :], in_=cf[:, 0:128]), reads=[("cf",)], writes=[("identb",)])
        S.pool(lambda e: e.tensor_copy(out=trimb[:, :], in_=cf[:, 256:384]), reads=[("cf",)], writes=[("trimb",)])
        S.pool(lambda e: e.memset(onesf[:, :], 1.0), writes=[("onesf",)])
        S.pool(lambda e: e.memset(onecol[:, :], 1.0), writes=[("onecol",)])
        S.pool(lambda e: e.memset(V[:, :, :, 64:66], 1.0), writes=[("Vones",)])
        for e_ in range(2):
            for c_ in range(2):
                S.pool(lambda e, e_=e_, c_=c_: e.memset(QTz[e_][c_][:, :], 0.0), writes=[("QTz", e_, c_)])
        if not fox:
            for c_ in range(2):
                S.pool(lambda e, c_=c_: e.memset(selbT[c_][:, :], 0.0), writes=[("selbT", c_)])
        Hv = Hfull.rearrange("(c p) t -> p c t", p=128)
        load_cast_weight(S, wq.rearrange("(c p) f -> p c f", p=128), wqb, stg, "wqb", 256)
        load_cast_weight(S, wk.rearrange("(c p) f -> p c f", p=128), wkb, stg, "wkb", 256)
        load_cast_weight(S, wv.rearrange("(c p) f -> p c f", p=128), wvb, stg, "wvb", 256)
        if fox:
            S.dma(lambda e: [e.dma_start(out=wfs[:, :, :], in_=wf.rearrange("(c p) f -> p c f", p=128)),
                             e.dma_start(out=bft[:, :], in_=bfr)], writes=[("wfs",), ("bft",)], n=2)
            S.pool(lambda e: e.tensor_copy(out=wfb[:, :, :], in_=wfs[:, :, :]), reads=[("wfs",)], writes=[("wfb",)])
        else:
            stgf = stg[:, :, :].rearrange("p a b -> p (a b)")
            for hf in range(2):
                S.dma(lambda e, hf=hf: e.dma_start(out=stgf[:, :], in_=oh[:, hf * 2048:(hf + 1) * 2048]), writes=[("stg_shared",)])
                S.pool(lambda e, hf=hf: e.tensor_copy(out=ohb[:, hf * 16:(hf + 1) * 16, :], in_=stgf[:, :].rearrange("p (n m) -> p n m", m=128)),
                       reads=[("stg_shared",)], writes=[("ohb",)])
            S.pool(lambda e: e.tensor_scalar(out=id30[:, :], in0=cf[:, 0:128], scalar1=-NEG, scalar2=None, op0=ALU.mult),
                   reads=[("cf",)], writes=[("id30",)])

        pq = 0
        for tt in range(SEQ // 512):
            t0 = tt * 512
            b = tt % 2
            S.dma(lambda e, b=b, t0=t0: e.dma_start(out=xt[b][:, :, :], in_=Hv[:, :, t0:t0 + 512]), writes=[("xt", b)])
            S.pool(lambda e, b=b: e.tensor_copy(out=xb[b][:, :, :], in_=xt[b][:, :, :]), reads=[("xt", b)], writes=[("xb", b)])
            for (wb, wkey, dst, dkey, scale) in ((wqb, "wqb", QT, "QT", 0.125), (wkb, "wkb", KT, "KT", 1.0)):
                for hp in range(2):
                    q = pq % 2
                    pq += 1
                    for k in range(8):
                        S.pe(lambda e, q=q, wb=wb, k=k, hp=hp, b=b: e.matmul(
                            ps_p[q][:, :], lhsT=wb[:, k, hp * 128:(hp + 1) * 128], rhs=xb[b][:, k, :], start=(k == 0), stop=(k == 7)),
                            reads=[(wkey,), ("xb", b)], writes=[("ps_p", q)])
                    S.act(lambda e, q=q, dst=dst, hp=hp, t0=t0, scale=scale: e.activation(
                        out=dst[:, hp, t0:t0 + 512], in_=ps_p[q][:, :], func=AF.Copy, scale=scale),
                        reads=[("ps_p", q)], writes=[(dkey, hp, tt)])
            for s in range(4):
                blk = tt * 4 + s
                q = pq % 2
                pq += 1
                for k in range(8):
                    S.pe(lambda e, q=q, k=k, s=s, b=b: e.matmul(
                        ps_p[q][:, 0:256], lhsT=xb[b][:, k, s * 128:(s + 1) * 128], rhs=wvb[:, k, :], start=(k == 0), stop=(k == 7)),
                        reads=[("wvb",), ("xb", b)], writes=[("ps_p", q)])
                S.dve(lambda e, q=q, blk=blk: e.tensor_copy(out=V[:, blk, :, 0:64], in_=ps_p[q][:, 0:256].rearrange("p (h d) -> p h d", h=4)),
                      reads=[("ps_p", q)], writes=[("V", blk)])
                if fox:
                    q = pq % 2
                    pq += 1
                    for k in range(8):
                        S.pe(lambda e, q=q, k=k, s=s, b=b: e.matmul(
                            ps_p[q][:, 0:4], lhsT=xb[b][:, k, s * 128:(s + 1) * 128], rhs=wfb[:, k, :], start=(k == 0), stop=(k == 7)),
                            reads=[("wfb",), ("xb", b)], writes=[("ps_p", q)])
                    S.dve(lambda e, q=q, blk=blk: e.tensor_tensor(out=lf[:, blk, :], in0=ps_p[q][:, 0:4], in1=bft[:, :], op=ALU.add),
                          reads=[("ps_p", q), ("bft",)], writes=[("lf",)])
        if fox:
            lf2 = lf[:, :, :].rearrange("p n h -> p (n h)")
            S.act(lambda e: e.activation(out=lf2, in_=lf2, func=AF.Exp, scale=-1.0), reads=[("lf",)], writes=[("lf",)])
            S.act(lambda e: e.activation(out=lf2, in_=lf2, func=AF.Ln, bias=onecol[:, 0:1]), reads=[("lf",), ("onecol",)], writes=[("lf",)])
            S.dve(lambda e: e.tensor_scalar(out=lf2, in0=lf2, scalar1=-1.0, scalar2=None, op0=ALU.mult), reads=[("lf",)], writes=[("lf",)])
            S.pe(lambda e: e.matmul(ps_x[:, 0:256], lhsT=triU, rhs=lf2, start=True, stop=True), reads=[("lf",), ("cf",)], writes=[("ps_x",)])
            S.pe(lambda e: e.matmul(ps_y[:, 0:256], lhsT=onesf[:, :], rhs=lf2, start=True, stop=True), reads=[("lf",), ("onesf",)], writes=[("ps_y",)])
            S.dve(lambda e: e.tensor_copy(out=tot[:, :, :].rearrange("p n h -> p (n h)"), in_=ps_y[:, 0:256]), reads=[("ps_y",)], writes=[("tot",)])
            S.dve(lambda e: e.tensor_copy(out=hsA[:, :, :], in_=tot[:, :, :]), reads=[("tot",)], writes=[("hsA",)])
            A, B, ka, kb_ = hsA, hsB, "hsA", "hsB"
            for sft in (1, 2, 4, 8, 16, 32):
                S.dve(lambda e, A=A, B=B, sft=sft: e.tensor_tensor(out=B[:, sft:, :], in0=A[:, sft:, :], in1=A[:, 0:NBLK - sft, :], op=ALU.add),
                      reads=[(ka,)], writes=[(kb_,)])
                S.dve(lambda e, A=A, B=B, sft=sft: e.tensor_copy(out=B[:, 0:sft, :], in_=A[:, 0:sft, :]),
                      reads=[(ka,)], writes=[(kb_,)])
                A, B, ka, kb_ = B, A, kb_, ka
            S.dve(lambda e, A=A: e.tensor_tensor(out=cum[:, :, :], in0=A[:, :, :], in1=tot[:, :, :], op=ALU.subtract),
                  reads=[(ka,), ("tot",)], writes=[("cum",)])
            S.dve(lambda e: e.tensor_tensor(out=cum[:, :, :].rearrange("p n h -> p (n h)"), in0=cum[:, :, :].rearrange("p n h -> p (n h)"),
                                            in1=ps_x[:, 0:256], op=ALU.add),
                  reads=[("cum",), ("ps_x",)], writes=[("cum",)])
            S.dve(lambda e: e.tensor_scalar(out=ncum[:, :, :], in0=cum[:, :, :], scalar1=-1.0, scalar2=None, op0=ALU.mult),
                  reads=[("cum",)], writes=[("ncum",)])
        else:
            for hp in range(2):
                S.dve(lambda e, hp=hp: e.tensor_reduce(out=kmf[:, hp, :], in_=KT[:, hp, :].rearrange("p (n l) -> p n l", l=256),
                                                       axis=AX.X, op=ALU.add),
                      reads=[("KT", hp, tt) for tt in range(SEQ // 512)], writes=[("kmf",)])
            S.dve(lambda e: e.tensor_copy(out=kmb[:, :, :], in_=kmf[:, :, :]), reads=[("kmf",)], writes=[("kmb",)])

        if dbg is not None and fox:
            S.pool(lambda e: e.tensor_copy(out=Sb[0][:, :], in_=QT[:, 0, 512:1024]), reads=[("QT", 0, 1)], writes=[("Sb", 0)])
            S.pool(lambda e: e.tensor_copy(out=Sb[1][:, 0:128], in_=KT[:, 0, 256:384]), reads=[("KT", 0, 0)], writes=[("Sb", 1)])
            S.dma(lambda e: [e.dma_start(out=dbg[:, 1536:2048], in_=Sb[0][:, :]), e.dma_start(out=dbg[:, 2048:2176], in_=Sb[1][:, 0:128])],
                  reads=[("Sb", 0), ("Sb", 1)], writes=[("dbg3",)], n=2)
            S.dma(lambda e: [e.dma_start(out=dbg[:, 0:256], in_=cum[:, :, :].rearrange("p n h -> p (n h)")),
                             e.dma_start(out=dbg[:, 256:512], in_=lf[:, :, :].rearrange("p n h -> p (n h)"))],
                  reads=[("cum",), ("lf",)], writes=[("dbg",)], n=2)
        sqi = 0
        oqi = 0
        NT = SEQ // 512
        pending = []

        def drain(n):
            for _ in range(n):
                if pending:
                    pending.pop(0)[1]()

        def flush(kind=None):
            while pending and (kind is None or any(k == kind for k, _ in pending)):
                pending.pop(0)[1]()

        def prologue_steps(h, qt):
            hp = h // 2
            rows = slice((h % 2) * 64, (h % 2) * 64 + 64)
            q0 = qt * 512
            cb = qt % 2
            steps = []
            e_ = h % 2

            def st_q():
                S.pool(lambda e: e.tensor_copy(out=QTz[e_][cb][rows, :], in_=QT[rows, hp, q0:q0 + 512]),
                       reads=[("QT", hp, qt)], writes=[("QTz", e_, cb)])
            steps.append(st_q)
            if fox:
                for s_ in range(4):
                    blk = 4 * qt + s_

                    def st(s_=s_, blk=blk):
                        S.dve(lambda e: e.tensor_scalar(out=dg[s_ % 2][:, :], in0=ident, scalar1=cum[:, blk, h:h + 1], scalar2=None, op0=ALU.mult),
                              reads=[("cum",), ("cf",)], writes=[("dg", s_ % 2)])
                        S.pe(lambda e: e.matmul(ps_x[:, s_ * 128:(s_ + 1) * 128], lhsT=onesf[:, :], rhs=dg[s_ % 2][:, :], start=True, stop=True),
                             reads=[("dg", s_ % 2), ("onesf",)], writes=[("ps_x",)])
                    steps.append(st)

                def fin():
                    S.act(lambda e: e.activation(out=cqb[cb][:, :], in_=ps_x[:, :], func=AF.Copy), reads=[("ps_x",)], writes=[("cqb", cb)])
                steps.append(fin)
            else:
                for s_ in range(4):
                    qb = 4 * qt + s_
                    own = qb // 2

                    def st_a(s_=s_, own=own):
                        if own >= 3:
                            S.pe(lambda e: e.matmul(ps_x[:, s_ * 32:(s_ + 1) * 32], lhsT=QT[rows, hp, q0 + s_ * 128:q0 + (s_ + 1) * 128], rhs=kmb[rows, hp, :],
                                                    start=True, stop=True),
                                 reads=[("QT", hp, qt), ("kmb",)], writes=[("ps_x",)])

                    def st_b(s_=s_, own=own):
                        if own >= 3:
                            S.dve(lambda e: e.tensor_copy(out=G[:, 0:own], in_=ps_x[:, s_ * 32:s_ * 32 + own]), reads=[("ps_x",)], writes=[("G",)])
                            S.dve(lambda e: e.max(out=mx[:, :], in_=G[:, :]), reads=[("G",)], writes=[("mx",)])
                            S.dve(lambda e: e.tensor_scalar(out=selm[:, s_, :], in0=G[:, :], scalar1=mx[:, 2:3], scalar2=-1.0, op0=ALU.is_ge, op1=ALU.add),
                                  reads=[("G",), ("mx",)], writes=[("selm", s_)])
                            S.dve(lambda e: e.memset(selm[:, s_, own:own + 1], 0.0), writes=[("selm", s_)])
                        else:
                            S.dve(lambda e: e.memset(selm[:, s_, :], -1.0), writes=[("selm", s_)])
                            S.dve(lambda e: e.memset(selm[:, s_, 0:own + 1], 0.0), writes=[("selm", s_)])

                    def st_c(s_=s_):
                        S.pe(lambda e: e.matmul(ps_y[0:32, s_ * 128:(s_ + 1) * 128], lhsT=selm[:, s_, :], rhs=id30[:, :], start=True, stop=True),
                             reads=[("selm", s_), ("id30",)], writes=[("ps_y",)])
                    steps += [st_a, st_b, st_c]

                def fin():
                    S.act(lambda e: e.activation(out=selbT[cb][0:32, :], in_=ps_y[0:32, :], func=AF.Copy), reads=[("ps_y",)], writes=[("selbT", cb)])
                steps.append(fin)
            return [("pro", f) for f in steps]

        def finalize_steps(h, qt, oq):
            q0 = qt * 512

            def f1():
                S.dve(lambda e: e.reciprocal(out=rc[64:65, :], in_=ps_O[oq][64:65, :]), reads=[("ps_O", oq)], writes=[("rc",)])

            def f2():
                S.pe(lambda e: e.matmul(ps_x[0:64, :], lhsT=onesf[64:65, 0:64], rhs=rc[64:65, :], start=True, stop=True),
                     reads=[("rc",), ("onesf",)], writes=[("ps_x",)])

            def f3():
                S.act(lambda e: e.activation(out=rb[0:64, :], in_=ps_x[0:64, :], func=AF.Copy), reads=[("ps_x",)], writes=[("rb",)])

            def f4():
                S.dve(lambda e: e.tensor_tensor(out=ot[oq][0:64, :], in0=ps_O[oq][0:64, :], in1=rb[0:64, :], op=ALU.mult),
                      reads=[("ps_O", oq), ("rb",)], writes=[("ot", oq)])
                S.dma(lambda e: e.dma_start(out=OT[h * 64:(h + 1) * 64, q0:q0 + 512], in_=ot[oq][0:64, :]),
                      reads=[("ot", oq)], writes=[("OT", h, qt)])
            return [("fin", f) for f in (f1, f2, f3, f4)]

        for h in range(4):
            hp = h // 2
            rows = slice((h % 2) * 64, (h % 2) * 64 + 64)
            if not fox:
                S.dve(lambda e: e.memset(G[:, :], -1e30), writes=[("G",)])
            for _, f in prologue_steps(h, 0):
                f()
            for qt in range(NT):
                q0 = qt * 512
                cb = qt % 2
                if qt + 1 < NT:
                    pending.extend(prologue_steps(h, qt + 1))
                nkb = 4 * qt + 4
                oq = oqi % 2
                oqi += 1
                unit_sq = {}

                def emit_S(kb, qt=qt, q0=q0, cb=cb, h=h, hp=hp, rows=rows):
                    nonlocal sqi
                    c0 = max(0, kb - 4 * qt) * 128
                    diag = kb >= 4 * qt
                    sq = sqi % NSB
                    sqi += 1
                    unit_sq[kb] = sq
                    psS, psSk = ((ps_S[0], ("ps_S", 0)), (ps_S[1], ("ps_S", 1)), (ps_p[0], ("ps_p", 0)), (ps_p[1], ("ps_p", 1)))[sq]
                    last_s = not diag and fox
                    S.pe(lambda e: e.matmul(psS[:, c0:512], lhsT=KT[:, hp, kb * 128:(kb + 1) * 128], rhs=QTz[h % 2][cb][:, c0:512],
                                            start=True, stop=last_s),
                         reads=[("KT", hp, kb // 4), ("QTz", h % 2, cb)], writes=[psSk])
                    if not fox:
                        S.pe(lambda e: e.matmul(psS[:, c0:512], lhsT=ohb[:, kb // 2, :], rhs=selbT[cb][:, c0:512], start=False, stop=(not diag)),
                             reads=[("ohb",), ("selbT", cb)], writes=[psSk])
                    if diag:
                        S.pe(lambda e: e.matmul(psS[:, c0:c0 + 128], lhsT=identb[:, :], rhs=trimb[:, :], start=False, stop=True),
                             reads=[("identb",), ("trimb",)], writes=[psSk])
                    if fox:
                        S.dve(lambda e: e.tensor_tensor(out=Sb[sq][:, c0:512], in0=psS[:, c0:512], in1=cqb[cb][:, c0:512], op=ALU.add),
                              reads=[psSk, ("cqb", cb)], writes=[("Sb", sq)])
                        S.act(lambda e: e.activation(out=Pt[sq][:, c0:512], in_=Sb[sq][:, c0:512], func=AF.Exp, bias=ncum[:, kb, h:h + 1]),
                              reads=[("Sb", sq), ("ncum",)], writes=[("Pt", sq)])
                    else:
                        S.act(lambda e: e.activation(out=Pt[sq][:, c0:512], in_=psS[:, c0:512], func=AF.Exp),
                              reads=[psSk], writes=[("Pt", sq)])

                def emit_PV(kb, qt=qt, h=h, oq=oq, nkb=nkb):
                    c0 = max(0, kb - 4 * qt) * 128
                    sq = unit_sq[kb]
                    S.pe(lambda e: e.matmul(ps_O[oq][0:65, c0:512], lhsT=V[:, kb, h, 0:65], rhs=Pt[sq][:, c0:512], start=(kb == 0), stop=(kb == nkb - 1)),
                         reads=[("V", kb), ("Vones",), ("Pt", sq)], writes=[("ps_O", oq)])

                LOOK = NSB - 1
                for i in range(min(LOOK, nkb)):
                    emit_S(i)
                for i in range(nkb):
                    if i + LOOK < nkb:
                        emit_S(i + LOOK)
                    emit_PV(i)
                    drain(2)
                flush("pro")
                pending.extend(finalize_steps(h, qt, oq))
            flush()
        S.emit()


def stage_ssd(S, Hfull, wz, wx, wB, wC, wdt, cw, cbias, dtb, alog, dcol, nw, cst, YT, ntiles=SEQ // 512):
    nc = S.nc
    S.new_stage()
    with contextlib.ExitStack() as es:
        sb = lambda name, shape, dt: es.enter_context(nc.sbuf_tensor(name + S.sfx, shape, dt))
        pst = lambda name, shape, dt=F32: es.enter_context(nc.psum_tensor(name + S.sfx, shape, dt))
        xt = [sb("xt%d" % i, [128, 8, 512], F32) for i in range(2)]
        xb = [sb("xb%d" % i, [128, 8, 512], BF16) for i in range(2)]
        stg = sb("stg", [128, 8, 256], F32)
        wzb = sb("wzb", [128, 8, 512], BF16)
        wxb = sb("wxb", [128, 8, 512], BF16)
        wBb = sb("wBb", [128, 8, 128], BF16)
        wCb = sb("wCb", [128, 8, 128], BF16)
        wdts = sb("wdts", [128, 8, 8], F32)
        wdtb = sb("wdtb", [128, 8, 8], BF16)
        cf = sb("cf", [128, 384], F32)
        identb = sb("identb", [128, 128], BF16)
        trimb = sb("trimb", [128, 128], BF16)
        onesf = sb("onesf", [128, 128], F32)
        onesb = sb("onesb", [128, 128], BF16)
        onecol = sb("onecol", [128, 1], F32)
        cwt = sb("cwt", [128, 6, 4], F32)
        cbt = sb("cbt", [128, 6], F32)
        dtbt = sb("dtbt", [128, 8], F32)
        arep = sb("arep", [128, 8], F32)
        dct = sb("dct", [128, 4], F32)
        nwt = sb("nwt", [128, 4], F32)
        pre = sb("pre", [128, 6, 515], F32)
        acc = [sb("acc%d" % i, [128, 512], F32) for i in range(2)]
        xcf = sb("xcf", [128, 4, 512], F32)
        xcb = sb("xcb", [128, 4, 512], BF16)
        BTb = sb("BTb", [128, 512], BF16)
        CTb = sb("CTb", [128, 512], BF16)
        CTf = sb("CTf", [128, 512], F32)
        zs = sb("zs", [128, 4, 512], F32)
        xtokf = sb("xtokf", [128, 512], F32)
        Btokb = sb("Btokb", [128, 128], BF16)
        dtp = sb("dtp", [128, 8], F32)
        dtt = sb("dtt", [128, 8], F32)
        adt = sb("adt", [128, 8], F32)
        ncs = sb("ncs", [128, 8], F32)
        wtt = sb("wtt", [128, 8], F32)
        dtw = sb("dtw", [128, 8], F32)
        cbTf = sb("cbTf", [128, 128], F32)
        xdtp = [sb("xdtp%d" % i, [128, 8, 128], BF16) for i in range(2)]
        xdtw = sb("xdtw", [128, 512], BF16)
        arpa = sb("arpa", [128, 8, 128], F32)
        Efa = [sb("Efa%d" % i, [128, 8, 128], F32) for i in range(2)]
        Asb = sb("Asb", [128, 8, 128], F32)
        MTa = sb("MTa", [128, 8, 128], BF16)
        CTsa = sb("CTsa", [128, 8, 128], BF16)
        Sf = sb("Sf", [128, 512], F32)
        stf = [sb("stf%d" % i, [128, 512], F32) for i in range(2)]
        Sbp = sb("Sbp", [128, 8, 128], BF16)
        yf = sb("yf", [128, 512], F32)
        yg = sb("yg", [128, 4, 512], F32)
        sqb = sb("sqb", [128, 4, 512], BF16)
        rstd = sb("rstd", [128, 512], F32)
        ytmp = [sb("ytmp%d" % i, [128, 512], F32) for i in range(2)]
        yo = [sb("yo%d" % i, [128, 512], BF16) for i in range(2)]
        ps_p = [pst("ps_p%d" % i, [128, 512]) for i in range(2)]
        ps_y = [pst("ps_y%d" % i, [128, 512]) for i in range(4)]
        ps_a = pst("ps_a", [128, 512])
        ps_s = pst("ps_s", [128, 512])
        triU = cf[:, 128:256]

        S.dma(lambda e: e.dma_start(out=cf[:, :], in_=cst), writes=[("cf",)])
        S.pool(lambda e: e.tensor_copy(out=identb[:, :], in_=cf[:, 0:128]), reads=[("cf",)], writes=[("identb",)])
        S.pool(lambda e: e.tensor_copy(out=trimb[:, :], in_=cf[:, 256:384]), reads=[("cf",)], writes=[("trimb",)])
        S.pool(lambda e: e.memset(onesf[:, :], 1.0), writes=[("onesf",)])
        S.pool(lambda e: e.memset(onesb[:, :], 1.0), writes=[("onesb",)])
        S.pool(lambda e: e.memset(onecol[:, :], 1.0), writes=[("onecol",)])
        S.pool(lambda e: e.memset(pre[:, :, 0:3], 0.0), writes=[("pre", cc) for cc in range(6)])
        S.pool(lambda e: e.memset(Sf[:, :], 0.0), writes=[("Sf",)])
        S.pool(lambda e: e.memset(Sbp[:, :, :], 0.0), writes=[("Sbp",)])
        S.pool(lambda e: e.memset(xdtp[0][:, :, :], 0.0), writes=[("xdtp", 0)])
        S.pool(lambda e: e.memset(xdtp[1][:, :, :], 0.0), writes=[("xdtp", 1)])
        S.dma(lambda e: [e.dma_start(out=cwt[:, :, :], in_=cw), e.dma_start(out=cbt[:, :], in_=cbias),
                         e.dma_start(out=dtbt[:, :], in_=dtb), e.dma_start(out=arep[:, :], in_=alog),
                         e.dma_start(out=dct[:, :], in_=dcol), e.dma_start(out=nwt[:, :], in_=nw),
                         e.dma_start(out=wdts[:, :, :], in_=wdt.rearrange("(c p) f -> p c f", p=128))],
              writes=[("prm",)], n=7)
        S.pool(lambda e: e.tensor_copy(out=wdtb[:, :, :], in_=wdts[:, :, :]), reads=[("prm",)], writes=[("wdtb",)])
        S.act(lambda e: e.activation(out=arep[:, :], in_=arep[:, :], func=AF.Exp), reads=[("prm",)], writes=[("arep",)])
        S.dve(lambda e: e.tensor_scalar(out=arep[:, :], in0=arep[:, :], scalar1=-1.0, scalar2=None, op0=ALU.mult), reads=[("arep",)], writes=[("arep",)])
        Hv = Hfull.rearrange("(c p) t -> p c t", p=128)
        load_cast_weight(S, wz.rearrange("(c p) f -> p c f", p=128), wzb, stg, "wzb", 512)
        load_cast_weight(S, wx.rearrange("(c p) f -> p c f", p=128), wxb, stg, "wxb", 512)
        load_cast_weight(S, wB.rearrange("(c p) f -> p c f", p=128), wBb, stg, "wBb", 128)
        load_cast_weight(S, wC.rearrange("(c p) f -> p c f", p=128), wCb, stg, "wCb", 128)

        pq = 0
        hq = 0
        for tt in range(ntiles):
            t0 = tt * 512
            b = tt % 2
            S.dma(lambda e, b=b, t0=t0: e.dma_start(out=xt[b][:, :, :], in_=Hv[:, :, t0:t0 + 512]), writes=[("xt", b)])
            S.pool(lambda e, b=b: e.tensor_copy(out=xb[b][:, :, :], in_=xt[b][:, :, :]), reads=[("xt", b)], writes=[("xb", b)])
            for hc in range(4):
                q = pq % 2
                pq += 1
                for k in range(8):
                    S.pe(lambda e, q=q, k=k, hc=hc, b=b: e.matmul(ps_p[q][:, :], lhsT=wzb[:, k, hc * 128:(hc + 1) * 128], rhs=xb[b][:, k, :],
                                                                 start=(k == 0), stop=(k == 7)),
                         reads=[("wzb",), ("xb", b)], writes=[("ps_p", q)])
                S.act(lambda e, q=q, hc=hc: e.activation(out=zs[:, hc, :], in_=ps_p[q][:, :], func=AF.Silu),
                      reads=[("ps_p", q)], writes=[("zs", hc)])
            for cc in range(6):
                q = pq % 2
                pq += 1
                if cc < 4:
                    wsel, wkey, csl = wxb, "wxb", slice(cc * 128, (cc + 1) * 128)
                elif cc == 4:
                    wsel, wkey, csl = wBb, "wBb", slice(0, 128)
                else:
                    wsel, wkey, csl = wCb, "wCb", slice(0, 128)
                for k in range(8):
                    S.pe(lambda e, q=q, k=k, wsel=wsel, csl=csl, b=b: e.matmul(ps_p[q][:, :], lhsT=wsel[:, k, csl], rhs=xb[b][:, k, :],
                                                                            start=(k == 0), stop=(k == 7)),
                         reads=[(wkey,), ("xb", b)], writes=[("ps_p", q)])
                S.dve(lambda e, q=q, cc=cc: e.tensor_copy(out=pre[:, cc, 3:515], in_=ps_p[q][:, :]), reads=[("ps_p", q)], writes=[("pre", cc)])
                a = acc[cc % 2]
                ak = ("acc", cc % 2)
                S.dve(lambda e, a=a, cc=cc: e.tensor_scalar(out=a[:, :], in0=pre[:, cc, 0:512], scalar1=cwt[:, cc, 0:1], scalar2=None, op0=ALU.mult),
                      reads=[("pre", cc), ("prm",)], writes=[ak])
                for kk in range(1, 4):
                    S.dve(lambda e, a=a, cc=cc, kk=kk: e.scalar_tensor_tensor(out=a[:, :], in0=pre[:, cc, kk:kk + 512], scalar=cwt[:, cc, kk:kk + 1],
                                                                             in1=a[:, :], op0=ALU.mult, op1=ALU.add),
                          reads=[("pre", cc), ak], writes=[ak])
                S.pool(lambda e, cc=cc: e.tensor_copy(out=pre[:, cc, 0:3], in_=pre[:, cc, 512:515]), reads=[("pre", cc)], writes=[("pre", cc)])
                if cc < 4:
                    S.act(lambda e, a=a, cc=cc: e.activation(out=xcf[:, cc, :], in_=a[:, :], func=AF.Silu, bias=cbt[:, cc:cc + 1]),
                          reads=[ak, ("prm",)], writes=[("xcf", cc)])
                    S.pool(lambda e, cc=cc: e.tensor_copy(out=xcb[:, cc, :], in_=xcf[:, cc, :]), reads=[("xcf", cc)], writes=[("xcb", cc)])
                elif cc == 4:
                    S.act(lambda e, a=a, cc=cc: e.activation(out=BTb[:, :], in_=a[:, :], func=AF.Silu, bias=cbt[:, cc:cc + 1]),
                          reads=[ak, ("prm",)], writes=[("BTb",)])
                else:
                    S.act(lambda e, a=a, cc=cc: e.activation(out=CTf[:, :], in_=a[:, :], func=AF.Silu, bias=cbt[:, cc:cc + 1]),
                          reads=[ak, ("prm",)], writes=[("CTf",)])
                    S.pool(lambda e: e.tensor_copy(out=CTb[:, :], in_=CTf[:, :]), reads=[("CTf",)], writes=[("CTb",)])
            def front(c, tt=tt, b=b):
                cb2 = c % 2
                cs = slice(c * 128, (c + 1) * 128)
                for cc in range(4):
                    S.pe(lambda e, cc=cc, cs=cs: e.matmul(ps_p[0][:, cc * 128:(cc + 1) * 128], lhsT=xcb[:, cc, cs], rhs=identb[:, :], start=True, stop=True),
                         reads=[("xcb", cc), ("identb",)], writes=[("ps_p", 0)])
                S.pe(lambda e, cs=cs: e.matmul(ps_p[1][:, 0:128], lhsT=BTb[:, cs], rhs=identb[:, :], start=True, stop=True),
                     reads=[("BTb",), ("identb",)], writes=[("ps_p", 1)])
                S.act(lambda e: e.activation(out=xtokf[:, :], in_=ps_p[0][:, :], func=AF.Copy), reads=[("ps_p", 0)], writes=[("xtokf",)])
                S.dve(lambda e: e.tensor_copy(out=Btokb[:, :], in_=ps_p[1][:, 0:128]), reads=[("ps_p", 1)], writes=[("Btokb",)])
                for k in range(8):
                    S.pe(lambda e, k=k, cs=cs, b=b: e.matmul(ps_a[:, 384:392], lhsT=xb[b][:, k, cs], rhs=wdtb[:, k, :], start=(k == 0), stop=(k == 7)),
                         reads=[("xb", b), ("wdtb",)], writes=[("ps_a", "dt")])
                S.dve(lambda e: e.tensor_tensor(out=dtp[:, :], in0=ps_a[:, 384:392], in1=dtbt[:, :], op=ALU.add),
                      reads=[("ps_a", "dt"), ("prm",)], writes=[("dtp",)])
                S.act(lambda e: e.activation(out=dtp[:, :], in_=dtp[:, :], func=AF.Exp), reads=[("dtp",)], writes=[("dtp",)])
                S.act(lambda e: e.activation(out=dtt[:, :], in_=dtp[:, :], func=AF.Ln, bias=onecol[:, 0:1]), reads=[("dtp",), ("onecol",)], writes=[("dtt",)])
                S.dve(lambda e: e.tensor_tensor(out=adt[:, :], in0=dtt[:, :], in1=arep[:, :], op=ALU.mult), reads=[("dtt",), ("arep",)], writes=[("adt",)])
                S.pe(lambda e: e.matmul(ps_a[:, 392:400], lhsT=triU, rhs=adt[:, :], start=True, stop=True), reads=[("adt",), ("cf",)], writes=[("ps_a", "cs")])
                S.pe(lambda e: e.matmul(ps_a[:, 400:408], lhsT=onesf[:, :], rhs=adt[:, :], start=True, stop=True), reads=[("adt",), ("onesf",)], writes=[("ps_a", "tot")])
                S.dve(lambda e: e.tensor_scalar(out=ncs[:, :], in0=ps_a[:, 392:400], scalar1=-1.0, scalar2=None, op0=ALU.mult),
                      reads=[("ps_a", "cs")], writes=[("ncs",)])
                S.dve(lambda e: e.tensor_tensor(out=wtt[:, :], in0=ps_a[:, 400:408], in1=ncs[:, :], op=ALU.add),
                      reads=[("ps_a", "tot"), ("ncs",)], writes=[("wtt",)])
                S.act(lambda e: e.activation(out=wtt[:, :], in_=wtt[:, :], func=AF.Exp), reads=[("wtt",)], writes=[("wtt",)])
                S.dve(lambda e: e.tensor_tensor(out=dtw[:, :], in0=dtt[:, :], in1=wtt[:, :], op=ALU.mult), reads=[("dtt",), ("wtt",)], writes=[("dtw",)])
                S.pe(lambda e, cs=cs: e.matmul(ps_a[:, 256:384], lhsT=BTb[:, cs], rhs=CTb[:, cs], start=True, stop=True),
                     reads=[("BTb",), ("CTb",)], writes=[("ps_a", "cb")])
                S.dve(lambda e: e.tensor_tensor(out=cbTf[:, :], in0=ps_a[:, 256:384], in1=triU, op=ALU.mult), reads=[("ps_a", "cb"), ("cf",)], writes=[("cbTf",)])
                xv = xtokf[:, :].rearrange("p (r d) -> p r d", d=64)
                S.dve(lambda e, xv=xv: e.tensor_tensor(out=xdtp[cb2][:, 0:8:2, 0:64], in0=xv[:, 0:8:2, :],
                                                       in1=dtt[:, 0:8:2].unsqueeze(2).to_broadcast([128, 4, 64]), op=ALU.mult),
                      reads=[("xtokf",), ("dtt",)], writes=[("xdtp", cb2)])
                S.pool(lambda e, xv=xv: e.tensor_tensor(out=xdtp[cb2][:, 1:8:2, 64:128], in0=xv[:, 1:8:2, :],
                                                        in1=dtt[:, 1:8:2].unsqueeze(2).to_broadcast([128, 4, 64]), op=ALU.mult),
                       reads=[("xtokf",), ("dtt",)], writes=[("xdtp", cb2)])
                S.pool(lambda e, xv=xv: e.tensor_tensor(out=xdtw[:, :].rearrange("p (r d) -> p r d", d=64), in0=xv,
                                                        in1=dtw[:, :].unsqueeze(2).to_broadcast([128, 8, 64]), op=ALU.mult),
                       reads=[("xtokf",), ("dtw",)], writes=[("xdtw",)])
                S.pe(lambda e: e.matmul(ps_s[:, :], lhsT=Btokb[:, :], rhs=xdtw[:, :], start=True, stop=True),
                     reads=[("Btokb",), ("xdtw",)], writes=[("ps_s",)])
                S.act(lambda e: e.activation(out=stf[cb2][:, :], in_=ps_s[:, :], func=AF.Copy), reads=[("ps_s",)], writes=[("stf", cb2)])
                S.dve(lambda e: e.tensor_copy(out=arpa[:, :, :], in_=adt[:, :].unsqueeze(2).to_broadcast([128, 8, 128])),
                      reads=[("adt",)], writes=[("arpa",)])
                for hh, (bank, bkey) in enumerate(((ps_p[1], ("ps_p", 1)), (ps_s, ("ps_s",)))):
                    for r4 in range(4):
                        r = hh * 4 + r4
                        S.pe(lambda e, bank=bank, r=r, r4=r4: e.matmul(bank[:, r4 * 128:(r4 + 1) * 128], lhsT=arpa[:, r, :], rhs=triU, start=True, stop=True),
                             reads=[("arpa",), ("cf",)], writes=[bkey])
                    hs4 = slice(hh * 4, hh * 4 + 4)
                    S.act(lambda e, bank=bank, hs4=hs4: e.activation(out=Efa[cb2][:, hs4, :].rearrange("p r l -> p (r l)"), in_=bank[:, :], func=AF.Exp),
                          reads=[bkey], writes=[("Efa", cb2, hh)])
                    S.act(lambda e, bank=bank, hs4=hs4: e.activation(out=Asb[:, hs4, :].rearrange("p r l -> p (r l)"), in_=bank[:, :], func=AF.Copy),
                          reads=[bkey], writes=[("Asb", hh)])

            def mid(c, tt=tt):
                cb2 = c % 2
                cs = slice(c * 128, (c + 1) * 128)
                S.dve(lambda e: e.tensor_tensor(out=Asb[:, :, :], in0=Asb[:, :, :], in1=ncs[:, :].unsqueeze(2).to_broadcast([128, 8, 128]), op=ALU.add),
                      reads=[("Asb", 0), ("Asb", 1), ("ncs",)], writes=[("Asb", 0), ("Asb", 1)])
                S.dve(lambda e: e.tensor_tensor(out=Asb[:, :, :], in0=Asb[:, :, :], in1=triU.unsqueeze(1).to_broadcast([128, 8, 128]), op=ALU.mult),
                      reads=[("Asb", 0), ("Asb", 1), ("cf",)], writes=[("Asb", 0), ("Asb", 1)])
                S.act(lambda e: e.activation(out=Asb[:, :, :].rearrange("p r l -> p (r l)"), in_=Asb[:, :, :].rearrange("p r l -> p (r l)"), func=AF.Exp),
                      reads=[("Asb", 0), ("Asb", 1)], writes=[("Asb", 0), ("Asb", 1)])
                S.dve(lambda e: e.tensor_tensor(out=MTa[:, :, :], in0=Asb[:, :, :], in1=cbTf[:, :].unsqueeze(1).to_broadcast([128, 8, 128]), op=ALU.mult),
                      reads=[("Asb", 0), ("Asb", 1), ("cbTf",)], writes=[("MTa",)])
                S.pool(lambda e, cs=cs: e.tensor_tensor(out=CTsa[:, :, :], in0=Efa[cb2][:, :, :], in1=CTf[:, cs].unsqueeze(1).to_broadcast([128, 8, 128]), op=ALU.mult),
                       reads=[("Efa", cb2, 0), ("Efa", cb2, 1), ("CTf",)], writes=[("CTsa",)])

            def back(c, tt=tt):
                cb2 = c % 2
                cs = slice(c * 128, (c + 1) * 128)
                for r in range(8):
                    hc = r // 2
                    S.pe(lambda e, r=r, hc=hc, cs=cs: e.matmul(ps_y[hc][:, cs], lhsT=xdtp[cb2][:, r, :], rhs=MTa[:, r, :], start=(r % 2 == 0), stop=False),
                         reads=[("xdtp", cb2), ("MTa",)], writes=[("ps_y", hc)])
                    S.pe(lambda e, r=r, hc=hc, cs=cs: e.matmul(ps_y[hc][:, cs], lhsT=Sbp[:, r, :], rhs=CTsa[:, r, :], start=False, stop=(r % 2 == 1)),
                         reads=[("Sbp",), ("CTsa",)], writes=[("ps_y", hc)])
                Sv = Sf[:, :].rearrange("p (r d) -> p r d", d=64)
                S.dve(lambda e, Sv=Sv: e.tensor_tensor(out=Sv, in0=Sv, in1=Efa[cb2][:, :, 127:128].to_broadcast([128, 8, 64]), op=ALU.mult),
                      reads=[("Sf",), ("Efa", cb2, 0), ("Efa", cb2, 1)], writes=[("Sf",)])
                S.dve(lambda e: e.tensor_tensor(out=Sf[:, :], in0=Sf[:, :], in1=stf[cb2][:, :], op=ALU.add),
                      reads=[("Sf",), ("stf", cb2)], writes=[("Sf",)])
                S.act(lambda e, Sv=Sv: e.activation(out=Sbp[:, 0:8:2, 0:64], in_=Sv[:, 0:8:2, :], func=AF.Copy), reads=[("Sf",)], writes=[("Sbp",)])
                S.act(lambda e, Sv=Sv: e.activation(out=Sbp[:, 1:8:2, 64:128], in_=Sv[:, 1:8:2, :], func=AF.Copy), reads=[("Sf",)], writes=[("Sbp",)])

            front(0)
            mid(0)
            for c in range(1, 4):
                front(c)
                back(c - 1)
                mid(c)
            back(3)
            for hc in range(4):
                S.dve(lambda e, hc=hc: e.tensor_scalar(out=yf[:, :], in0=xcf[:, hc, :], scalar1=dct[:, hc:hc + 1], scalar2=None, op0=ALU.mult),
                      reads=[("xcf", hc), ("prm",)], writes=[("yf",)])
                S.dve(lambda e, hc=hc: e.tensor_tensor(out=yf[:, :], in0=ps_y[hc][:, :], in1=yf[:, :], op=ALU.add),
                      reads=[("yf",), ("ps_y", hc)], writes=[("yf",)])
                S.dve(lambda e, hc=hc: e.tensor_tensor(out=yg[:, hc, :], in0=yf[:, :], in1=zs[:, hc, :], op=ALU.mult),
                      reads=[("yf",), ("zs", hc)], writes=[("yg", hc)])
                S.act(lambda e, hc=hc: e.activation(out=sqb[:, hc, :], in_=yg[:, hc, :], func=AF.Square), reads=[("yg", hc)], writes=[("sqb", hc)])
            for hc in range(4):
                S.pe(lambda e, hc=hc: e.matmul(ps_p[0][:, :], lhsT=onesb[:, :], rhs=sqb[:, hc, :], start=(hc == 0), stop=(hc == 3)),
                     reads=[("sqb", hc), ("onesb",)], writes=[("ps_p", 0)])
            S.dve(lambda e: e.tensor_scalar(out=rstd[:, :], in0=ps_p[0][:, :], scalar1=1.0 / 512, scalar2=LN_EPS, op0=ALU.mult, op1=ALU.add),
                  reads=[("ps_p", 0)], writes=[("rstd",)])
            S.act(lambda e: e.activation(out=rstd[:, :], in_=rstd[:, :], func=AF.Ln), reads=[("rstd",)], writes=[("rstd",)])
            S.act(lambda e: e.activation(out=rstd[:, :], in_=rstd[:, :], func=AF.Exp, scale=-0.5), reads=[("rstd",)], writes=[("rstd",)])
            for hc in range(4):
                yb = hc % 2
                S.dve(lambda e, hc=hc, yb=yb: e.tensor_tensor(out=ytmp[yb][:, :], in0=yg[:, hc, :], in1=rstd[:, :], op=ALU.mult),
                      reads=[("yg", hc), ("rstd",)], writes=[("ytmp", yb)])
                S.act(lambda e, hc=hc, yb=yb: e.activation(out=yo[yb][:, :], in_=ytmp[yb][:, :], func=AF.Copy, scale=nwt[:, hc:hc + 1]),
                      reads=[("ytmp", yb), ("prm",)], writes=[("yo", yb)])
                S.dma(lambda e, hc=hc, yb=yb, t0=t0: e.dma_start(out=YT[hc * 128:(hc + 1) * 128, t0:t0 + 512], in_=yo[yb][:, :]),
                      reads=[("yo", yb)], writes=[("YT", hc, tt)])
        S.emit()


def stage_outproj(S, OTin, wout, Hin, Hout, lng, lnb, T, Kdim, TP=1024):
    nc = S.nc
    S.new_stage()
    Kc = Kdim // 128
    npass = T // TP
    ntile = TP // 512
    with contextlib.ExitStack() as es:
        sb = lambda name, shape, dt: es.enter_context(nc.sbuf_tensor(name + S.sfx, shape, dt))
        x = sb("x", [128, 8, TP], F32)
        ob = sb("ob", [128, Kc, TP], BF16)
        wob = sb("wob", [128, Kc, 1024], BF16)
        stg = sb("stg", [128, 8, 256], F32)
        scr = dict(ybf=sb("ybf", [128, 8, 512], BF16), sqb=sb("sqb", [128, 8, 512], BF16),
                   mean=sb("mean", [128, 512], F32), rstd=sb("rstd", [128, 512], F32),
                   nmr=sb("nmr", [128, 512], F32), tmp=sb("lntmp", [128, 2, 512], F32))
        gam = sb("gam", [128, 8], F32)
        bet = sb("bet", [128, 8], F32)
        ones_bf = sb("ones_bf", [128, 128], BF16)
        ps_o = [es.enter_context(nc.psum_tensor("ps_o%d" % i + S.sfx, [128, 512], F32)) for i in range(2)]
        ps_s = es.enter_context(nc.psum_tensor("ps_s" + S.sfx, [128, 512], F32))
        ps_q = es.enter_context(nc.psum_tensor("ps_q" + S.sfx, [128, 512], F32))
        S.dma(lambda e: [e.dma_start(out=gam[:, :], in_=lng), e.dma_start(out=bet[:, :], in_=lnb)], writes=[("lnp",)], n=2)
        S.pool(lambda e: e.memset(ones_bf[:, :], 1.0), writes=[("ones",)])
        wv = wout.rearrange("(c p) d -> p c d", p=128)
        for k0 in range(0, Kc, 8):
            for c0 in range(0, 1024, 256):
                S.dma(lambda e, k0=k0, c0=c0: e.dma_start(out=stg[:, :, :], in_=wv[:, k0:k0 + 8, c0:c0 + 256]), writes=[("stg_shared",)])
                S.pool(lambda e, k0=k0, c0=c0: e.tensor_copy(out=wob[:, k0:k0 + 8, c0:c0 + 256], in_=stg[:, :, :]),
                       reads=[("stg_shared",)], writes=[("wob",)])
        Hin_v = Hin.rearrange("(c p) t -> p c t", p=128)
        Hout_v = Hout.rearrange("(c p) t -> p c t", p=128)
        O_v = OTin.rearrange("(c p) t -> p c t", p=128)
        oq = 0
        for p in range(npass):
            t0 = p * TP
            for c in range(8):
                S.dma(lambda e, c=c, t0=t0: e.dma_start(out=x[:, c, :], in_=Hin_v[:, c, t0:t0 + TP]),
                      writes=[("x", c, t) for t in range(ntile)])
            for c in range(Kc):
                S.dma(lambda e, c=c, t0=t0: e.dma_start(out=ob[:, c, :], in_=O_v[:, c, t0:t0 + TP]), writes=[("ob", c)])
            for d in range(8):
                for t in range(ntile):
                    q = oq % 2
                    oq += 1
                    sl = slice(t * 512, (t + 1) * 512)
                    for k in range(Kc):
                        S.pe(lambda e, q=q, k=k, d=d, sl=sl: e.matmul(ps_o[q][:, :], lhsT=wob[:, k, d * 128:(d + 1) * 128], rhs=ob[:, k, sl],
                                                                     start=(k == 0), stop=(k == Kc - 1)),
                             reads=[("wob",), ("ob", k)], writes=[("ps_o", q)])
                    S.dve(lambda e, q=q, d=d, sl=sl: e.scalar_tensor_tensor(out=x[:, d, sl], in0=x[:, d, sl], scalar=ALPHA, in1=ps_o[q][:, :],
                                                                           op0=ALU.mult, op1=ALU.add),
                          reads=[("ps_o", q), ("x", d, t)], writes=[("x", d, t)])
            ln_feature_major(S, x, "x", ntile, gam, bet, ones_bf, scr, ps_s, ps_q, "op")
            for c in range(8):
                S.dma(lambda e, c=c, t0=t0: e.dma_start(out=Hout_v[:, c, t0:t0 + TP], in_=x[:, c, :]),
                      reads=[("x", c, t) for t in range(ntile)], writes=[("Hout", p, c)])
        S.emit()


T_CORE = 2048
NCORES = 8


def _new_nc():
    return bass.Bass("TRN2", target_bir_lowering=False)


def _din(nc, name, shape, dt=F32):
    return nc.dram_tensor(name, list(shape), dt, kind="ExternalInput").ap()


def _dout(nc, name, shape, dt=F32):
    return nc.dram_tensor(name, list(shape), dt, kind="ExternalOutput").ap()


def _ffn_inputs(nc, tag):
    return dict(wg=_din(nc, "wg" + tag, [D, DFF]), wu=_din(nc, "wu" + tag, [D, DFF]), wd=_din(nc, "wd" + tag, [DFF, D]),
                lng=_din(nc, "lng" + tag, [128, 8]), lnb=_din(nc, "lnb" + tag, [128, 8]))


def build_ffn_prog():
    nc = _new_nc()
    Hin = _din(nc, "Hin", [D, T_CORE])
    f = _ffn_inputs(nc, "0")
    Hout = _dout(nc, "Hout", [D, T_CORE])
    with contextlib.ExitStack() as es:
        S = Sched(nc)
        S.setup(es)
        stage_ffn(S, Hin, Hout, f["wg"], f["wu"], f["wd"], f["lng"], f["lnb"], T_CORE)
    return nc


def build_post_prog(Kdim, n_ffn):
    nc = _new_nc()
    Hin = _din(nc, "Hin", [D, T_CORE])
    OTin = _din(nc, "OTin", [Kdim, T_CORE], BF16)
    wout = _din(nc, "wout", [Kdim, D])
    lng = _din(nc, "lngm", [128, 8])
    lnb = _din(nc, "lnbm", [128, 8])
    fs = [_ffn_inputs(nc, str(i)) for i in range(n_ffn)]
    Hout = _dout(nc, "Hout", [D, T_CORE])
    scratch = [nc.dram_tensor("hscr%d" % i, [D, T_CORE], F32).ap() for i in range(n_ffn)]
    with contextlib.ExitStack() as es:
        S = Sched(nc)
        S.setup(es)
        cur = scratch[0] if n_ffn > 0 else Hout
        stage_outproj(S, OTin, wout, Hin, cur, lng, lnb, T_CORE, Kdim)
        for i in range(n_ffn):
            nxt = Hout if i == n_ffn - 1 else scratch[i + 1]
            f = fs[i]
            stage_ffn(S, cur, nxt, f["wg"], f["wu"], f["wd"], f["lng"], f["lnb"], T_CORE)
            cur = nxt
    return nc


def build_attn_prog(kind):
    nc = _new_nc()
    Hfull = _din(nc, "Hfull", [D, SEQ])
    wq = _din(nc, "wq", [D, 256])
    wk = _din(nc, "wk", [D, 256])
    wv = _din(nc, "wv", [D, 256])
    wf = _din(nc, "wf", [D, 4]) if kind == "fox" else None
    bfr = _din(nc, "bfr", [128, 4]) if kind == "fox" else None
    cst = _din(nc, "cst", [128, 384])
    oh = _din(nc, "oh", [128, 32 * 128]) if kind == "moba" else None
    OT = _dout(nc, "OT", [256, SEQ], BF16)
    with contextlib.ExitStack() as es:
        S = Sched(nc)
        S.setup(es)
        stage_attn(S, kind, Hfull, wq, wk, wv, wf, bfr, cst, oh, OT)
    return nc


def build_ssd_prog():
    nc = _new_nc()
    Hfull = _din(nc, "Hfull", [D, SEQ])
    wz = _din(nc, "wz", [D, 512])
    wx = _din(nc, "wx", [D, 512])
    wB = _din(nc, "wB", [D, 128])
    wC = _din(nc, "wC", [D, 128])
    wdt = _din(nc, "wdt", [D, 8])
    cw = _din(nc, "cw", [128, 6, 4])
    cbias = _din(nc, "cbias", [128, 6])
    dtb = _din(nc, "dtb", [128, 8])
    alog = _din(nc, "alog", [128, 8])
    dcol = _din(nc, "dcol", [128, 4])
    nw = _din(nc, "nw", [128, 4])
    cst = _din(nc, "cst", [128, 384])
    YT = _dout(nc, "OT", [512, SEQ], BF16)
    with contextlib.ExitStack() as es:
        S = Sched(nc)
        S.setup(es)
        stage_ssd(S, Hfull, wz, wx, wB, wC, wdt, cw, cbias, dtb, alog, dcol, nw, cst, YT)
    return nc


def _consts():
    ident = np.eye(128, dtype=np.float32)
    triU = np.triu(np.ones((128, 128), np.float32))
    trim = np.where(np.arange(128)[:, None] > np.arange(128)[None, :], NEG, 0.0).astype(np.float32)
    cst = np.ascontiguousarray(np.concatenate([ident, triU, trim], axis=1))
    oh = np.zeros((128, 32, 128), np.float32)
    for n in range(32):
        oh[n, n, :] = 1.0
    return cst, oh.reshape(128, 32 * 128)


def _c(a):
    return np.ascontiguousarray(a, dtype=np.float32)


def _pc(v):
    return _c(np.asarray(v).reshape(8, 128).T)


def _ffn_map(inp, layer, half, tag):
    return {"wg" + tag: _c(inp["ffn_w_gate"][layer, half]), "wu" + tag: _c(inp["ffn_w_up"][layer, half]),
            "wd" + tag: _c(inp["ffn_w_down"][layer, half]),
            "lng" + tag: _pc(inp["ln_g"][layer, 2 * half]), "lnb" + tag: _pc(inp["ln_b"][layer, 2 * half])}


def _ssd_maps(inp, j, g, cst):
    w_in = inp["ssm_w_in"][j]
    DI = 2048
    cwf = inp["ssm_conv_w"][j]
    cbf = inp["ssm_conv_b"][j]
    chans = np.concatenate([np.arange(g * 512, (g + 1) * 512), np.arange(DI + g * 128, DI + (g + 1) * 128),
                            np.arange(DI + 512 + g * 128, DI + 512 + (g + 1) * 128)])
    cw = cwf[:, chans].reshape(4, 6, 128).transpose(2, 1, 0)
    cb = cbf[chans].reshape(6, 128).T
    hs = slice(g * 8, (g + 1) * 8)
    rep = lambda v: np.broadcast_to(np.asarray(v)[None, :], (128, len(v)))
    dcol = np.repeat(inp["ssm_d"][j][hs], 64).reshape(4, 128).T
    nw = inp["ssm_norm_w"][j][g * 512:(g + 1) * 512].reshape(4, 128).T
    m = dict(wz=w_in[:, g * 512:(g + 1) * 512], wx=w_in[:, DI + g * 512:DI + (g + 1) * 512],
             wB=w_in[:, 2 * DI + g * 128:2 * DI + (g + 1) * 128], wC=w_in[:, 2 * DI + 512 + g * 128:2 * DI + 512 + (g + 1) * 128],
             wdt=w_in[:, 2 * DI + 1024 + g * 8:2 * DI + 1024 + (g + 1) * 8],
             cw=cw, cbias=cb, dtb=rep(inp["ssm_dt_bias"][j][hs]), alog=rep(inp["ssm_a_log"][j][hs]), dcol=dcol, nw=nw, cst=cst)
    return {k: _c(v) for k, v in m.items()}


def _attn_maps(inp, kind, g, cst, oh):
    w_in = inp["fox_w_in"][0] if kind == "fox" else inp["moba_w_in"][0]
    m = dict(wq=_c(w_in[:, g * 256:(g + 1) * 256]), wk=_c(w_in[:, 1024 + g * 256:1024 + (g + 1) * 256]),
             wv=_c(w_in[:, 2048 + g * 256:2048 + (g + 1) * 256]), cst=cst)
    if kind == "fox":
        m["wf"] = _c(w_in[:, 3072 + g * 4:3072 + (g + 1) * 4])
        m["bfr"] = _c(np.broadcast_to(inp["fox_b_f"][0][g * 4:(g + 1) * 4][None, :], (128, 4)))
    else:
        m["oh"] = oh
    return m


def _run(nc, in_maps):
    res = run_bass_kernel_spmd(nc, in_maps, core_ids=list(range(NCORES)))
    return res.results


_DBG = None


def kernel(**inputs):
    inp = {k: np.asarray(v) for k, v in inputs.items()}
    x = inp["x"]
    cst, oh = _consts()
    progs = {}

    def prog(key, builder):
        if key not in progs:
            progs[key] = builder()
        return progs[key]

    H = [_c(x[c // 4, (c % 4) * T_CORE:(c % 4 + 1) * T_CORE].T) for c in range(NCORES)]
    maps = []
    for c in range(NCORES):
        m = {"Hin": H[c]}
        m.update(_ffn_map(inp, 0, 0, "0"))
        maps.append(m)
    r = _run(prog("ffn", build_ffn_prog), maps)
    H = [r[c]["Hout"] for c in range(NCORES)]
    if _DBG is not None:
        _DBG("A", H)
    for layer in range(4):
        kindi, j = layer % 3, layer // 3
        kind = ("ssd", "fox", "moba")[kindi]
        Hfull = [_c(np.concatenate([H[b * 4 + q] for q in range(4)], axis=1)) for b in range(2)]
        maps = []
        for c in range(NCORES):
            b, g = c // 4, c % 4
            m = _ssd_maps(inp, j, g, cst) if kind == "ssd" else _attn_maps(inp, kind, g, cst, oh)
            m["Hfull"] = Hfull[b]
            maps.append(m)
        r = _run(prog(kind, build_ssd_prog if kind == "ssd" else (lambda kind=kind: build_attn_prog(kind))), maps)
        Kdim = 2048 if kind == "ssd" else 1024
        Ofull = [np.concatenate([r[b * 4 + g]["OT"] for g in range(4)], axis=0) for b in range(2)]
        w_out = {"ssd": inp["ssm_w_out"], "fox": inp["fox_w_out"], "moba": inp["moba_w_out"]}[kind][j]
        if _DBG is not None:
            _DBG(("mix", layer), (Ofull, w_out))
        n_ffn = 2 if layer < 3 else 1
        maps = []
        for c in range(NCORES):
            b, q = c // 4, c % 4
            m = {"Hin": H[c], "OTin": np.ascontiguousarray(Ofull[b][:, q * T_CORE:(q + 1) * T_CORE]), "wout": _c(w_out),
                 "lngm": _pc(inp["ln_g"][layer, 1]), "lnbm": _pc(inp["ln_b"][layer, 1])}
            m.update(_ffn_map(inp, layer, 1, "0"))
            if n_ffn == 2:
                m.update(_ffn_map(inp, layer + 1, 0, "1"))
            maps.append(m)
        r = _run(prog(("post", Kdim, n_ffn), lambda Kdim=Kdim, n_ffn=n_ffn: build_post_prog(Kdim, n_ffn)), maps)
        H = [r[c]["Hout"] for c in range(NCORES)]
        if _DBG is not None:
            _DBG(("post", layer), H)
    out = np.empty((2, SEQ, D), np.float32)
    for c in range(NCORES):
        out[c // 4, (c % 4) * T_CORE:(c % 4 + 1) * T_CORE] = H[c].T
    return out
```
